# Optimizing a Trainium2 kernel written in Bass

```python
import jax, jax.numpy as jnp
from jax import lax
import numpy as np

D_MODEL = 1024
BATCH = 4
SEQ = 4096
DEPTH = 2

HEAD_DIM = 64
MOBA_HEADS = 4
MOBA_BLOCK = 256
MOBA_TOPK = 3
MOBA_QCHUNK = 128
LRU_WIDTH = 256
LRU_BLOCKS = 4
LRU_BLOCK_DIM = LRU_WIDTH // LRU_BLOCKS
CONV_WIDTH = 4
LRU_C = 8.0
DIL_HEADS = 4
DIL_CONFIGS = ((128, 1), (512, 4), (2048, 16))
DIL_QBLOCK = 128
GLA_HEADS = 4
GLA_DK = 32
GLA_DV = 64
GLA_LOWRANK = 16
GLA_TAU = 16.0
GLA_CHUNK = 32

ROPE_THETA = 10000.0
LN_EPS = 1e-5
NEG = -1e30
MOD_SCALE = 0.1

A_W = MOBA_HEADS * HEAD_DIM
B_W = LRU_WIDTH
C_W = DIL_HEADS * HEAD_DIM
D_KW = GLA_HEADS * GLA_DK
D_W = GLA_HEADS * GLA_DV
D_MIX = A_W + B_W + C_W + D_W
COLS = (('a_q', A_W), ('a_k', A_W), ('a_v', A_W), ('a_g', A_W),
        ('b_x', B_W), ('b_g', B_W),
        ('c_q', C_W), ('c_k', C_W), ('c_v', C_W), ('c_g', C_W),
        ('d_q', D_KW), ('d_k', D_KW), ('d_v', D_W), ('d_g', D_W), ('d_r', GLA_LOWRANK))
D_IN = sum(w for _, w in COLS)
DEEPNORM_ALPHA = (2 * DEPTH) ** 0.25
DEEPNORM_BETA = (8 * DEPTH) ** -0.25

kernel_name = 'hybrid_moba_rglru_dilated_gla_deepnorm'

F32 = jnp.float32


def _layer_norm(x, g=None, b=None):
    xf = x.astype(F32)
    mu = jnp.mean(xf, -1, keepdims=True)
    var = jnp.mean(jnp.square(xf - mu), -1, keepdims=True)
    y = (xf - mu) * lax.rsqrt(var + LN_EPS)
    if g is not None:
        y = y * g + b
    return y.astype(x.dtype)


def _softmax_lse(s):
    m = jnp.max(s, -1, keepdims=True)
    e = jnp.exp(s - m)
    zsum = jnp.sum(e, -1, keepdims=True)
    return e / zsum, (m + jnp.log(zsum))[..., 0]


def _heads(z, n, dh):
    B, S, _ = z.shape
    return z.reshape(B, S, n, dh)


def _to_bhsd(t):
    return t.transpose(0, 2, 1, 3)


def _rope(x, pos):
    half = x.shape[-1] // 2
    inv = ROPE_THETA ** (-jnp.arange(half, dtype=F32) / half)
    ang = pos.astype(F32)[..., None] * inv
    cos = jnp.cos(ang)[:, :, None, :]
    sin = jnp.sin(ang)[:, :, None, :]
    xf = x.astype(F32)
    x1, x2 = xf[..., :half], xf[..., half:]
    return jnp.concatenate([x1 * cos - x2 * sin, x1 * sin + x2 * cos], -1).astype(x.dtype)


def _moba(q, k, v):
    B, H, S, hd = q.shape
    nb = -(-S // MOBA_BLOCK)
    sp = nb * MOBA_BLOCK
    padw = ((0, 0), (0, 0), (0, sp - S), (0, 0))
    qp = jnp.pad(q, padw).astype(F32) * hd ** -0.5
    kp = jnp.pad(k, padw).astype(F32)
    vp = jnp.pad(v, padw).astype(F32)
    qb = qp.reshape(B, H, nb, MOBA_BLOCK, hd)
    kb = kp.reshape(B, H, nb, MOBA_BLOCK, hd)
    vb = vp.reshape(B, H, nb, MOBA_BLOCK, hd)
    causal = jnp.tril(jnp.ones((MOBA_BLOCK, MOBA_BLOCK), bool))
    s_own = jnp.where(causal, jnp.einsum('bhnqd,bhnkd->bhnqk', qb, kb), NEG)
    p_own, lse_own = _softmax_lse(s_own)
    o_own = jnp.einsum('bhnqk,bhnkd->bhnqd', p_own, vb).reshape(B, H, sp, hd)
    lse_own = lse_own.reshape(B, H, sp)
    topk = min(MOBA_TOPK, nb - 1)
    if topk == 0:
        return o_own[:, :, :S].astype(q.dtype)
    qblk = jnp.arange(sp) // MOBA_BLOCK
    k_mean = jnp.mean(kb, axis=3)
    gate = jnp.einsum('bhsd,bhnd->bhsn', qp, k_mean)
    gate = jnp.where(jnp.arange(nb)[None, :] < qblk[:, None], gate, NEG)
    _, sel = lax.top_k(gate, topk)
    valid = sel < qblk[:, None]
    nc = sp // MOBA_QCHUNK

    def to_chunks(t):
        return jnp.moveaxis(t.reshape(B, H, nc, MOBA_QCHUNK, t.shape[-1]), 2, 0)

    bi = jnp.arange(B)[:, None, None, None]
    hi = jnp.arange(H)[None, :, None, None]

    def chunk(args):
        qc, selc, validc = args
        kg = kb[bi, hi, selc]
        vg = vb[bi, hi, selc]
        s = jnp.einsum('bhqd,bhqnkd->bhqnk', qc, kg)
        s = jnp.where(validc[..., None], s, NEG).reshape(B, H, MOBA_QCHUNK, topk * MOBA_BLOCK)
        p, lse = _softmax_lse(s)
        o = jnp.einsum('bhqnk,bhqnkd->bhqd', p.reshape(B, H, MOBA_QCHUNK, topk, MOBA_BLOCK), vg)
        return o, lse

    o_sel, lse_sel = lax.map(chunk, (to_chunks(qp), to_chunks(sel), to_chunks(valid)))
    o_sel = jnp.moveaxis(o_sel, 0, 2).reshape(B, H, sp, hd)
    lse_sel = jnp.moveaxis(lse_sel, 0, 2).reshape(B, H, sp)
    m = jnp.maximum(lse_own, lse_sel)
    w_own = jnp.exp(lse_own - m)
    w_sel = jnp.exp(lse_sel - m)
    o = (w_own[..., None] * o_own + w_sel[..., None] * o_sel) / (w_own + w_sel)[..., None]
    return o[:, :, :S].astype(q.dtype)


def _dilated_branch(q, k, v, window, dil):
    B, H, S, hd = q.shape
    n_steps = window // dil
    QB = DIL_QBLOCK
    L = -(-S // (dil * QB)) * QB
    sp = L * dil
    nb = L // QB

    def regroup(t):
        t = jnp.pad(t, ((0, 0), (0, 0), (0, sp - S), (0, 0)))
        return t.reshape(B, H, L, dil, hd).transpose(0, 1, 3, 2, 4).reshape(B, H, dil, nb, QB, hd)

    def prev_block(t):
        return jnp.pad(t[:, :, :, :-1], ((0, 0), (0, 0), (0, 0), (1, 0), (0, 0), (0, 0)))

    qb, kb, vb = regroup(q), regroup(k), regroup(v)
    kk = jnp.concatenate([prev_block(kb), kb], axis=4)
    vv = jnp.concatenate([prev_block(vb), vb], axis=4)
    qi = jnp.arange(QB)[:, None]
    kj = jnp.arange(2 * QB)[None, :]
    dist = qi + QB - kj
    band = (dist >= 0) & (dist <= n_steps)
    first = (jnp.arange(nb) == 0)[:, None, None] & (kj < QB)[None]
    valid = band[None] & ~first
    s = jnp.where(valid, jnp.einsum('bhrnqd,bhrnkd->bhrnqk', qb, kk), NEG)
    p, lse = _softmax_lse(s)
    o = jnp.einsum('bhrnqk,bhrnkd->bhrnqd', p, vv)
    o = o.reshape(B, H, dil, L, hd).transpose(0, 1, 3, 2, 4).reshape(B, H, sp, hd)[:, :, :S]
    lse = lse.reshape(B, H, dil, L).transpose(0, 1, 3, 2).reshape(B, H, sp)[:, :, :S]
    return o, lse


def _dilated(q, k, v):
    hd = q.shape[-1]
    qf = q.astype(F32) * hd ** -0.5
    kf = k.astype(F32)
    vf = v.astype(F32)
    outs, lses = [], []
    for window, dil in DIL_CONFIGS:
        o, l = _dilated_branch(qf, kf, vf, window, dil)
        outs.append(o)
        lses.append(l)
    wts = jax.nn.softmax(jnp.stack(lses, 0), axis=0)
    o = jnp.einsum('cbhs,cbhsd->bhsd', wts, jnp.stack(outs, 0))
    return o.astype(q.dtype)


def _causal_conv(x, w, b):
    C = x.shape[-1]
    y = lax.conv_general_dilated(x, w[:, None, :], window_strides=(1,),
                                 padding=[(CONV_WIDTH - 1, 0)],
                                 dimension_numbers=('NWC', 'WIO', 'NWC'),
                                 feature_group_count=C)
    return y + b


def _rg_lru(x, w_a, b_a, w_x, b_x, lam):
    B, S, W = x.shape
    xb = x.reshape(B, S, LRU_BLOCKS, LRU_BLOCK_DIM)
    r = jax.nn.sigmoid(jnp.einsum('bsgi,gio->bsgo', xb, w_a).reshape(B, S, W) + b_a).astype(F32)
    i = jax.nn.sigmoid(jnp.einsum('bsgi,gio->bsgo', xb, w_x).reshape(B, S, W) + b_x).astype(F32)
    log_a = -LRU_C * r * jax.nn.softplus(-lam.astype(F32))
    a = jnp.exp(log_a)
    u = jnp.sqrt(-jnp.expm1(2.0 * log_a)) * (i * x.astype(F32))

    def combine(left, right):
        a1, b1 = left
        a2, b2 = right
        return a1 * a2, a2 * b1 + b2

    _, h = lax.associative_scan(combine, (a, u), axis=1)
    return h.astype(x.dtype)


def _gla(q, k, v, log_alpha):
    B, H, S, dk = q.shape
    dv = v.shape[-1]
    C = GLA_CHUNK
    nc = S // C

    def rs(t):
        return t.astype(F32).reshape(B, H, nc, C, t.shape[-1])

    qc = rs(q) * dk ** -0.5
    kc, vc, gc = rs(k), rs(v), rs(log_alpha)
    bcum = jnp.cumsum(gc, axis=3)
    causal = jnp.tril(jnp.ones((C, C), bool))
    diff = bcum[..., :, None, :] - bcum[..., None, :, :]
    decay = jnp.exp(jnp.where(causal[..., None], diff, NEG))
    A = jnp.einsum('bhnid,bhnjd,bhnijd->bhnij', qc, kc, decay)
    o_intra = jnp.einsum('bhnij,bhnjv->bhniv', A, vc)
    q_in = qc * jnp.exp(bcum)
    b_last = bcum[..., -1:, :]
    kv = jnp.einsum('bhnjd,bhnjv->bhndv', kc * jnp.exp(b_last - bcum), vc)
    chunk_decay = jnp.exp(b_last[..., 0, :])

    def step(state, inp):
        qi, kvi, dec = inp
        o = jnp.einsum('bhid,bhdv->bhiv', qi, state)
        return dec[..., None] * state + kvi, o

    xs = (jnp.moveaxis(q_in, 2, 0), jnp.moveaxis(kv, 2, 0), jnp.moveaxis(chunk_decay, 2, 0))
    _, o_inter = lax.scan(step, jnp.zeros((B, H, dk, dv), F32), xs)
    o = o_intra + jnp.moveaxis(o_inter, 0, 2)
    return o.reshape(B, H, S, dv)


def _layer(x, c, pos, w_mod, b_mod, w_in, conv_w, conv_b, lru_wa, lru_ba, lru_wx, lru_bx,
           lru_lam, gla_wr, gla_br, gla_gn, w_out, ln_g, ln_b):
    B, S, _ = x.shape
    shift, scale, gate = jnp.split(c @ w_mod + b_mod, 3, axis=-1)
    u = _layer_norm(x) * (1 + scale[:, None]) + shift[:, None]
    z = u @ w_in
    p = {}
    off = 0
    for name, w in COLS:
        p[name] = z[..., off:off + w]
        off += w
    qa = _rope(_heads(p['a_q'], MOBA_HEADS, HEAD_DIM), pos)
    ka = _rope(_heads(p['a_k'], MOBA_HEADS, HEAD_DIM), pos)
    va = _heads(p['a_v'], MOBA_HEADS, HEAD_DIM)
    ya = _moba(_to_bhsd(qa), _to_bhsd(ka), _to_bhsd(va)).transpose(0, 2, 1, 3).reshape(B, S, A_W)
    yb = _rg_lru(_causal_conv(p['b_x'], conv_w, conv_b), lru_wa, lru_ba, lru_wx, lru_bx, lru_lam)
    qc = _rope(_heads(p['c_q'], DIL_HEADS, HEAD_DIM), pos)
    kc = _rope(_heads(p['c_k'], DIL_HEADS, HEAD_DIM), pos)
    vc = _heads(p['c_v'], DIL_HEADS, HEAD_DIM)
    yc = _dilated(_to_bhsd(qc), _to_bhsd(kc), _to_bhsd(vc)).transpose(0, 2, 1, 3).reshape(B, S, C_W)
    log_alpha = jax.nn.log_sigmoid((p['d_r'] @ gla_wr + gla_br).astype(F32)) / GLA_TAU
    yd = _gla(_to_bhsd(_heads(p['d_q'], GLA_HEADS, GLA_DK)),
              _to_bhsd(_heads(p['d_k'], GLA_HEADS, GLA_DK)),
              _to_bhsd(_heads(p['d_v'], GLA_HEADS, GLA_DV)),
              _to_bhsd(_heads(log_alpha, GLA_HEADS, GLA_DK)))
    yd = yd * lax.rsqrt(jnp.mean(yd * yd, -1, keepdims=True) + LN_EPS) * gla_gn.astype(F32)
    yd = yd.transpose(0, 2, 1, 3).reshape(B, S, D_W).astype(x.dtype)
    mix = jnp.concatenate([ya * jax.nn.silu(p['a_g']), yb * jax.nn.silu(p['b_g']),
                           yc * jax.nn.silu(p['c_g']), yd * jax.nn.silu(p['d_g'])], -1)
    y = mix @ w_out
    return _layer_norm(DEEPNORM_ALPHA * x + (1 + gate[:, None]) * y, ln_g, ln_b)


def setup_inputs(seed: int = 0) -> dict:
    key = jax.random.key(seed)
    ks = jax.random.split(key, 20)

    def nrm(k, shape, s):
        return jax.random.normal(k, shape, F32) * s

    x = nrm(ks[0], (BATCH, SEQ, D_MODEL), 1.0)
    c = nrm(ks[1], (BATCH, D_MODEL), 1.0)
    positions = (jnp.arange(SEQ, dtype=jnp.int32)[None, :]
                 + jax.random.randint(ks[2], (BATCH, 1), 0, 1024, dtype=jnp.int32))
    w_mod = nrm(ks[3], (DEPTH, D_MODEL, 3 * D_MODEL), MOD_SCALE * D_MODEL ** -0.5)
    b_mod = nrm(ks[4], (DEPTH, 3 * D_MODEL), 0.01)
    w_in = nrm(ks[5], (DEPTH, D_MODEL, D_IN), D_MODEL ** -0.5)
    conv_w = nrm(ks[6], (DEPTH, CONV_WIDTH, LRU_WIDTH), CONV_WIDTH ** -0.5)
    conv_b = nrm(ks[7], (DEPTH, LRU_WIDTH), 0.01)
    lru_wa = nrm(ks[8], (DEPTH, LRU_BLOCKS, LRU_BLOCK_DIM, LRU_BLOCK_DIM), LRU_BLOCK_DIM ** -0.5)
    lru_ba = nrm(ks[9], (DEPTH, LRU_WIDTH), 0.01)
    lru_wx = nrm(ks[10], (DEPTH, LRU_BLOCKS, LRU_BLOCK_DIM, LRU_BLOCK_DIM), LRU_BLOCK_DIM ** -0.5)
    lru_bx = nrm(ks[11], (DEPTH, LRU_WIDTH), 0.01)
    a_c = jax.random.uniform(ks[12], (DEPTH, LRU_WIDTH), F32, 0.9, 0.999)
    s = a_c ** (1.0 / LRU_C)
    lru_lam = jnp.log(s) - jnp.log1p(-s)
    gla_wr = nrm(ks[13], (DEPTH, GLA_LOWRANK, D_KW), GLA_LOWRANK ** -0.5)
    gla_br = nrm(ks[14], (DEPTH, D_KW), 0.1)
    gla_gn = 1.0 + nrm(ks[15], (DEPTH, GLA_DV), 0.02)
    w_out = nrm(ks[16], (DEPTH, D_MIX, D_MODEL), DEEPNORM_BETA * D_MIX ** -0.5)
    ln_g = 1.0 + nrm(ks[17], (DEPTH, D_MODEL), 0.02)
    ln_b = nrm(ks[18], (DEPTH, D_MODEL), 0.02)
    return {'x': x, 'c': c, 'positions': positions, 'w_mod': w_mod, 'b_mod': b_mod,
            'w_in': w_in, 'conv_w': conv_w, 'conv_b': conv_b, 'lru_wa': lru_wa,
            'lru_ba': lru_ba, 'lru_wx': lru_wx, 'lru_bx': lru_bx, 'lru_lam': lru_lam,
            'gla_wr': gla_wr, 'gla_br': gla_br, 'gla_gn': gla_gn, 'w_out': w_out,
            'ln_g': ln_g, 'ln_b': ln_b}


def reference(x, c, positions, w_mod, b_mod, w_in, conv_w, conv_b, lru_wa, lru_ba, lru_wx,
              lru_bx, lru_lam, gla_wr, gla_br, gla_gn, w_out, ln_g, ln_b):
    for l in range(DEPTH):
        x = _layer(x, c, positions, w_mod[l], b_mod[l], w_in[l], conv_w[l], conv_b[l],
                   lru_wa[l], lru_ba[l], lru_wx[l], lru_bx[l], lru_lam[l], gla_wr[l],
                   gla_br[l], gla_gn[l], w_out[l], ln_g[l], ln_b[l])
    return x
```

```python
import math
from contextlib import ExitStack

import numpy as np
import concourse.bass as bass
import concourse.mybir as mybir
from concourse.bass_utils import run_bass_kernel_spmd

F32 = mybir.dt.float32
BF16 = mybir.dt.bfloat16
I32 = mybir.dt.int32
AF = mybir.ActivationFunctionType
ALU = mybir.AluOpType
AX = mybir.AxisListType

D = 1024
NEGM = -240000.0
LN_EPS = 1e-5
ALPHA = 4.0 ** 0.25
SEM_LIMIT = 30000
import os
_KSTOP = os.environ.get("KSTOP", "")
_KSKIP = set(os.environ.get("KSKIP", "").split(","))


_STOP = [False]


def ckpt(name):
    if _KSTOP and _KSTOP == name:
        _STOP[0] = True


class Buf:
    __slots__ = ("name", "w", "r", "sem", "cnt")

    def __init__(self, name):
        self.name = name
        self.w = None
        self.r = {}
        self.sem = None
        self.cnt = 0


class KB:
    def __init__(self, nc, es):
        self.nc = nc
        self.es = es
        self.eng = {"pe": nc.tensor, "dve": nc.vector, "act": nc.scalar, "pool": nc.gpsimd, "sp": nc.sync}
        self.sem = {}
        self.cnt = {}
        self.nsem = 0
        for k in self.eng:
            self._newsem(k)
        self.waited = {k: {} for k in self.eng}
        self.nwait = 0
        self.nins = 0
        self.bufs = {}
        self.streams = {}
        self.semtot = {}
        self.store_names = set()

    def B(self, name):
        b = self.bufs.get(name)
        if b is None:
            b = self.bufs[name] = Buf(name)
        return b

    def _newsem(self, k):
        self.sem[k] = self.es.enter_context(self.nc.semaphore("s_%s_%d" % (k, self.nsem)))
        self.cnt[k] = 0
        self.nsem += 1

    def _need(self, e, ev, need):
        if ev is None:
            return
        sem, val, src = ev
        if src == "pe" and e == "pe":
            return
        if self.waited[e].get(id(sem), 0) >= val:
            return
        cur = need.get(id(sem))
        if cur is None or cur[1] < val:
            need[id(sem)] = (sem, val)

    def _deps(self, e, reads, writes):
        need = {}
        for b in reads:
            self._need(e, b.w, need)
        for b in writes:
            self._need(e, b.w, need)
            for ev in b.r.values():
                self._need(e, ev, need)
        return list(need.values())

    def _emit_waits(self, e, waits):
        for sem, val in waits:
            self.eng[e].wait_ge(sem, val)
            self.waited[e][id(sem)] = val
            self.nwait += 1

    def _commit(self, ev, reads, writes):
        for b in reads:
            b.r[id(ev[0])] = ev
        for b in writes:
            b.w = ev
            b.r = {}

    def op(self, e, fn, reads=(), writes=(), attach=None, lhs_reads=None, inc=True):
        if _STOP[0]:
            return None
        if attach is None:
            attach = (e != "pe")
        if e == "pe" and lhs_reads is not None:
            pre = self._deps(e, lhs_reads, ())
            self._emit_waits(e, pre)
            attach = True
        ex = [b for b in reads if b.name.startswith("ps")]
        if ex:
            writes = list(writes) + [b for b in ex if b not in writes]
        waits = self._deps(e, reads, writes)
        last = None
        if attach and waits:
            last = waits.pop()
        self._emit_waits(e, waits)
        ins = fn(self.eng[e])
        if last is not None:
            ins._wait_ge(last[0], last[1])
            self.waited[e][id(last[0])] = last[1]
        self.nins += 1
        if e == "pe" and not inc:
            ev = (self.sem[e], self.cnt[e] + 1, e)
            self._commit(ev, reads, writes)
            return ev
        if e != "pe" and self.cnt[e] >= SEM_LIMIT:
            self._newsem(e)
        self.cnt[e] += 1
        ev = (self.sem[e], self.cnt[e], e)
        ins.then_inc(self.sem[e], 1)
        self._commit(ev, reads, writes)
        return ev

    def dma(self, out, in_, reads=(), writes=(), e="sp", sembuf=None):
        if _STOP[0]:
            return None
        waits = self._deps(e, reads, writes)
        self._emit_waits(e, waits)
        b = sembuf if sembuf is not None else writes[0]
        if b.sem is None or b.cnt >= SEM_LIMIT:
            b.sem = self.es.enter_context(self.nc.semaphore("d_%d" % self.nsem))
            b.cnt = 0
            self.nsem += 1
        ins = self.eng[e].dma_start(out=out, in_=in_)
        b.cnt += 16
        ins.then_inc(b.sem, 16)
        self.nins += 1
        ev = (b.sem, b.cnt, "dma")
        self._commit(ev, reads, writes)
        return ev

    def pe_fence(self):
        if _STOP[0] or self.cnt["pe"] == 0:
            return
        self.eng["pe"].wait_ge(self.sem["pe"], self.cnt["pe"])
        self.nwait += 1

    def wait_all(self, e, bufs):
        if _STOP[0]:
            return
        need = {}
        for b in bufs:
            self._need(e, b.w, need)
            for ev in b.r.values():
                self._need(e, ev, need)
        self._emit_waits(e, list(need.values()))


def build(SEQ=4096, DEPTH=2):
    _STOP[0] = False
    NT = SEQ // 256
    NS = SEQ // 128
    nc = bass.Bass("TRN2", target_bir_lowering=False)

    def din(name, shape, dtype=F32):
        return nc.dram_tensor(name, list(shape), dtype, kind="ExternalInput").ap()

    x_in = din("x", [SEQ, D])
    cT_in = din("cT", [128, 8])
    pos_in = din("pos", [1, SEQ], I32)
    inv_in = din("rope_inv", [128, 1])
    w_mod = din("w_mod", [DEPTH, D, 3 * D])
    b_mod = din("b_mod", [DEPTH, 3 * D])
    w_in = din("w_in", [DEPTH, D, 3344])
    convw_in = din("conv_w", [DEPTH, 128, 8])
    convb_in = din("conv_b", [DEPTH, 128, 2])
    lruwa_in = din("lru_wa", [DEPTH, 4, 64, 64])
    lruba_in = din("lru_ba", [DEPTH, 128, 2])
    lruwx_in = din("lru_wx", [DEPTH, 4, 64, 64])
    lrubx_in = din("lru_bx", [DEPTH, 128, 2])
    lrulam_in = din("lru_lam", [DEPTH, 128, 2])
    glawr_in = din("gla_wr", [DEPTH, 16, 128])
    glabr_in = din("gla_br", [DEPTH, 128])
    glagn_in = din("gla_gn", [DEPTH, 64])
    w_out = din("w_out", [DEPTH, D, D])
    lng_in = din("ln_g", [DEPTH, D])
    lnb_in = din("ln_b", [DEPTH, D])
    out_d = nc.dram_tensor("out", [SEQ, D], F32, kind="ExternalOutput").ap()
    x1_d = nc.dram_tensor("x1_scr", [SEQ, D], F32).ap()
    cs_d = nc.dram_tensor("cs_scr", [NT, 128, 512], F32).ap()
    mod_d = nc.dram_tensor("mod_scr", [DEPTH, 1024], F32).ap()
    ut_d = nc.dram_tensor("ut_scr", [NT, 128, 2048], BF16).ap()

    es = ExitStack()
    with es:
        kb = KB(nc, es)
        kb.store_names = {"out", "x1_scr", "cs_scr", "mod_scr", "ut_scr"}
        B = kb.B

        def sbt(st, name, shape, dtype):
            return st.enter_context(nc.sbuf_tensor(name, list(shape), dtype))

        PS = [es.enter_context(nc.psum_tensor("ps%d" % i, [128, 512], F32)) for i in range(7)]
        PSB = es.enter_context(nc.psum_tensor("psb", [128, 1024], BF16))
        gp_state = [0]
        NGEN = 3

        POOLS = {"front": [0], "lru": [1], "gla0": [3, 4], "gla1": [5, 2], "out": [6], "p1f": [0, 1], "p1z": [2]}
        pool_state = {k: 0 for k in POOLS}

        def gp(pool=None):
            if pool is None:
                i = gp_state[0] % NGEN
                gp_state[0] += 1
            else:
                lst = POOLS[pool]
                i = lst[pool_state[pool] % len(lst)]
                pool_state[pool] += 1
            return PS[i], B("ps%d" % i)

        ST = [(PS[3], B("ps3")), (PS[4], B("ps4")), (PS[2], B("ps2"))]
        NZb = [(PS[5], B("ps5")), (PS[6], B("ps6"))]
        PB = (PSB, B("psb"))

        block = es.enter_context(nc.Block())

        def TT(e, out, in0, in1, op, R, W):
            return kb.op(e, lambda g: g.tensor_tensor(out=out, in0=in0, in1=in1, op=op), R, W)

        def TS(e, out, in0, s1, s2, op0, op1, R, W):
            if op1 is None:
                return kb.op(e, lambda g: g.tensor_scalar(out=out, in0=in0, scalar1=s1, scalar2=None, op0=op0), R, W)
            return kb.op(e, lambda g: g.tensor_scalar(out=out, in0=in0, scalar1=s1, scalar2=s2, op0=op0, op1=op1), R, W)

        def STT(out, in0, scalar, in1, op0, op1, R, W):
            return kb.op("dve", lambda g: g.scalar_tensor_tensor(out=out, in0=in0, scalar=scalar, in1=in1, op0=op0, op1=op1), R, W)

        def ACTF(out, in_, func, R, W, bias=None, scale=None):
            kw = {}
            if bias is not None:
                kw["bias"] = bias
            if scale is not None:
                kw["scale"] = scale
            return kb.op("act", lambda g: g.activation(out=out, in_=in_, func=func, **kw), R, W)

        def SIGM(dst, src, R, Wb, nbias=None):
            ACTF(dst, src, AF.Exp, R, [Wb], scale=-1.0, bias=nbias)
            ACTF(dst, dst, AF.Ln, [Wb], [Wb], bias=1.0)
            ACTF(dst, dst, AF.Exp, [Wb], [Wb], scale=-1.0)

        def CP(e, out, in_, R, W):
            if e == "act":
                return kb.op("act", lambda g: g.activation(out=out, in_=in_, func=AF.Copy), R, W)
            return kb.op(e, lambda g: g.tensor_copy(out=out, in_=in_), R, W)

        def MM(out, lhsT, rhs, start, stop, R, W, LR=None, inc=None):
            if inc is None:
                inc = bool(stop)
            return kb.op("pe", lambda g: g.matmul(out, lhsT=lhsT, rhs=rhs, start=start, stop=stop), R, W, lhs_reads=LR, inc=inc)

        def TR(out, in_, ident, R, W):
            return kb.op("pe", lambda g: g.transpose(out=out, in_=in_, identity=ident), R, W)

        def MEMSET(e, ap, val, W):
            return kb.op(e, lambda g: g.memset(ap, val), (), W)

        def ASEL(out, in_, pattern, cmp, fill, base, cm, R, W):
            return kb.op("pool", lambda g: g.affine_select(out=out, in_=in_, pattern=pattern, compare_op=cmp,
                                                            fill=fill, base=base, channel_multiplier=cm), R, W)

        cst = es
        ident_f = sbt(cst, "ident_f", [128, 128], F32)
        ident_b = sbt(cst, "ident_b", [128, 128], BF16)
        rperm = sbt(cst, "rperm", [128, 128], BF16)
        tri = sbt(cst, "tri", [128, 128], F32)
        tri16 = sbt(cst, "tri16", [128, 128], F32)
        ones16 = sbt(cst, "ones16", [128, 128], F32)
        onesrow = sbt(cst, "onesrow", [1, 128], F32)
        cm = sbt(cst, "cm", [128, 2, 256], BF16)
        tb = sbt(cst, "tb", [128, 2432], BF16)
        ind = sbt(cst, "ind", [128, 16, 128], BF16)
        selA = sbt(cst, "selA", [128, 128], F32)
        selB = sbt(cst, "selB", [128, 128], F32)
        hm = sbt(cst, "hm", [128, 4, 128], BF16)
        bmall = sbt(cst, "bmall", [128, 16, 16], F32)
        inv = sbt(cst, "inv", [128, 1], F32)
        sgn = sbt(cst, "sgn", [128, 1], F32)
        cB = sbt(cst, "cB", [128, 8, 128], F32)
        Bc = B("consts")

        MEMSET("pool", ident_f[:], 1.0, [Bc])
        ASEL(ident_f[:], ident_f[:], [[-1, 128]], ALU.is_equal, 0.0, 0, 1, [Bc], [Bc])
        CP("dve", ident_b[:], ident_f[:], [Bc], [Bc])
        for blk, src in ((0, 1), (1, 0), (2, 3), (3, 2)):
            CP("dve", rperm[:, blk * 32:(blk + 1) * 32], ident_b[:, src * 32:(src + 1) * 32], [Bc], [Bc])
        MEMSET("pool", tri[:], 1.0, [Bc])
        ASEL(tri[:], tri[:], [[1, 128]], ALU.is_ge, 0.0, 0, -1, [Bc], [Bc])
        TS("dve", tri16[:], tri[:], -1.0 / 16.0, None, ALU.mult, None, [Bc], [Bc])
        MEMSET("pool", ones16[:], -1.0 / 16.0, [Bc])
        MEMSET("pool", onesrow[:], 1.0, [Bc])
        MEMSET("pool", selA[:], 0.0, [Bc])
        MEMSET("pool", selA[64:65, 0:64], 1.0, [Bc])
        MEMSET("pool", selB[:], 1.0, [Bc])
        ASEL(selB[:], selB[:], [[0, 128]], ALU.is_equal, 0.0, -63, 1, [Bc], [Bc])
        MEMSET("pool", selB[:, 0:64], 0.0, [Bc])
        MEMSET("pool", ind[:], 1.0, [Bc])
        ASEL(ind[:], ind[:], [[-1, 16], [0, 128]], ALU.is_equal, 0.0, 0, 1, [Bc], [Bc])
        MEMSET("pool", hm[:], 1.0, [Bc])
        ASEL(hm[:], hm[:], [[-32, 4], [0, 128]], ALU.is_ge, 0.0, 0, 1, [Bc], [Bc])
        ASEL(hm[:], hm[:], [[32, 4], [0, 128]], ALU.is_ge, 0.0, 31, -1, [Bc], [Bc])
        MEMSET("pool", bmall[:], 0.0, [Bc])
        ASEL(bmall[:], bmall[:], [[1, 16], [-1, 16]], ALU.is_ge, NEGM, -1, 0, [Bc], [Bc])
        with ExitStack() as tmp:
            zer = sbt(tmp, "zer", [128, 256], F32)
            pidx = sbt(tmp, "pidx", [128, 1], I32)
            pf = sbt(tmp, "pf", [128, 1], F32)
            cT = sbt(tmp, "cT_sb", [128, 8], F32)
            Bt = B("ctmp")
            MEMSET("pool", zer[:], 0.0, [Bt])
            for c in range(2):
                ASEL(cm[:, c, :], zer[:], [[1, 256]], ALU.is_ge, NEGM, -128 * c, -1, [Bt], [Bc])
            TBW = 2432
            with ExitStack() as t2s:
                di = sbt(t2s, "di2", [128, TBW], I32)
                dfl = sbt(t2s, "dfl2", [128, TBW], F32)
                ge0 = sbt(t2s, "ge02", [128, TBW], F32)
                ca = sbt(t2s, "ca2", [128, TBW], F32)
                cb_ = sbt(t2s, "cb2", [128, TBW], F32)
                mi = sbt(t2s, "mi2", [128, TBW], I32)
                msum = sbt(t2s, "msum2", [128, TBW], F32)
                kb.op("pool", lambda g: g.iota(di[:], [[1, TBW]], base=-128, channel_multiplier=-1), (), [Bt])
                CP("dve", dfl[:], di[:], [Bt], [Bt])
                TS("dve", ge0[:], dfl[:], 0.0, None, ALU.is_ge, None, [Bt], [Bt])
                TS("dve", ca[:], dfl[:], 128.0, None, ALU.is_le, None, [Bt], [Bt])
                TT("dve", msum[:], ca[:], ge0[:], ALU.mult, [Bt], [Bt])
                for msk, lim in ((3, 512.0), (15, 2048.0)):
                    TS("dve", mi[:], di[:], msk, None, ALU.bitwise_and, None, [Bt], [Bt])
                    CP("dve", ca[:], mi[:], [Bt], [Bt])
                    TS("dve", ca[:], ca[:], 0.0, None, ALU.is_equal, None, [Bt], [Bt])
                    TS("dve", cb_[:], dfl[:], lim, None, ALU.is_le, None, [Bt], [Bt])
                    TT("dve", cb_[:], cb_[:], ge0[:], ALU.mult, [Bt], [Bt])
                    TT("dve", ca[:], ca[:], cb_[:], ALU.mult, [Bt], [Bt])
                    TT("dve", msum[:], msum[:], ca[:], ALU.add, [Bt], [Bt])
                TS("dve", ca[:], msum[:], 0.0, NEGM, ALU.is_equal, ALU.mult, [Bt], [Bt])
                TS("dve", cb_[:], msum[:], 1.0, None, ALU.max, None, [Bt], [Bt])
                ACTF(cb_[:], cb_[:], AF.Ln, [Bt], [Bt])
                STT(tb[:], cb_[:], 8.0, ca[:], ALU.mult, ALU.add, [Bt], [Bc])
                for e in ("dve", "act", "pool"):
                    kb.wait_all(e, [Bt, Bc])
            kb.dma(inv[:], inv_in[:, :], (), [Bc])
            kb.op("pool", lambda g: g.iota(pidx[:], [[0, 1]], base=0, channel_multiplier=1), (), [Bt])
            TS("dve", pidx[:], pidx[:], 63, None, ALU.bitwise_and, None, [Bt], [Bt])
            CP("dve", pf[:], pidx[:], [Bt], [Bt])
            TS("dve", pf[:], pf[:], 32.0, None, ALU.is_lt, None, [Bt], [Bt])
            TS("dve", sgn[:], pf[:], -2.0, 1.0, ALU.mult, ALU.add, [Bt], [Bc])
            kb.dma(cT[:], cT_in[:, :], (), [Bt])
            for kc in range(8):
                CP("dve", cB[:, kc, :], cT[:, kc:kc + 1].broadcast_to([128, 128]), [Bt], [Bc])
            posi = sbt(tmp, "posi", [128, SEQ], I32)
            ang = sbt(tmp, "ang", [128, SEQ], F32)
            a2 = sbt(tmp, "a2", [128, SEQ], F32)
            tqc = sbt(tmp, "tqc", [128, SEQ], F32)
            tqs = sbt(tmp, "tqs", [128, SEQ], F32)
            rc = sbt(tmp, "rc", [128, SEQ], F32)
            rs_ = sbt(tmp, "rs_", [128, SEQ], F32)
            MAGIC = 12582912.0
            C1 = 6.28125
            C2 = 2.0 * math.pi - C1
            Bp, Bw, Bw2 = B("posi"), B("ropew"), B("ropew2")
            kb.dma(posi[:], pos_in[0:1, :].partition_broadcast(128), (), [Bp])
            CP("dve", ang[:], posi[:], [Bp], [Bw])
            TS("dve", ang[:], ang[:], inv[:, 0:1], None, ALU.mult, None, [Bw, Bc], [Bw])
            TS("dve", a2[:], ang[:], math.pi / 2.0, None, ALU.add, None, [Bw], [B("ra2")])
            TS("dve", tqc[:], a2[:], 1.0 / (2.0 * math.pi), MAGIC, ALU.mult, ALU.add, [B("ra2")], [B("rtqc")])
            TS("dve", tqc[:], tqc[:], -MAGIC, None, ALU.add, None, [B("rtqc")], [B("rtqc")])
            STT(rc[:], tqc[:], -C1, a2[:], ALU.mult, ALU.add, [B("rtqc"), B("ra2")], [B("rrc")])
            STT(rc[:], tqc[:], -C2, rc[:], ALU.mult, ALU.add, [B("rtqc"), B("rrc")], [B("rrc")])
            TS("dve", rc[:], rc[:], -3.1415925, 3.1415925, ALU.max, ALU.min, [B("rrc")], [B("rrc")])
            ACTF(rc[:], rc[:], AF.Sin, [B("rrc")], [B("rrc")])
            TS("dve", tqs[:], ang[:], 1.0 / (2.0 * math.pi), MAGIC, ALU.mult, ALU.add, [Bw], [B("rtqs")])
            TS("dve", tqs[:], tqs[:], -MAGIC, None, ALU.add, None, [B("rtqs")], [B("rtqs")])
            STT(rs_[:], tqs[:], -C1, ang[:], ALU.mult, ALU.add, [B("rtqs"), Bw], [B("rrs")])
            STT(rs_[:], tqs[:], -C2, rs_[:], ALU.mult, ALU.add, [B("rtqs"), B("rrs")], [B("rrs")])
            TS("dve", rs_[:], rs_[:], -3.1415925, 3.1415925, ALU.max, ALU.min, [B("rrs")], [B("rrs")])
            ACTF(rs_[:], rs_[:], AF.Sin, [B("rrs")], [B("rrs")])
            TS("dve", rs_[:], rs_[:], sgn[:, 0:1], None, ALU.mult, None, [B("rrs"), Bc], [B("rrs")])
            csv = cs_d.rearrange("i p c -> p i c")
            kb.dma(csv[:, :, 0:256], rc[:].rearrange("p (i q) -> p i q", q=256), [B("rrc")], [B("csd0")], sembuf=B("rrc"))
            kb.dma(csv[:, :, 256:512], rs_[:].rearrange("p (i q) -> p i q", q=256), [B("rrs")], [B("csd1")], sembuf=B("rrs"))
            for i in range(2, NT):
                B("csd%d" % i)
            kb.wait_all("sp", [B("csd%d" % i) for i in range(NT)])
            for e in ("pe", "dve", "act", "pool", "sp"):
                kb.wait_all(e, [Bt, Bw, Bp, Bc, B("ra2"), B("rtqc"), B("rrc"), B("rtqs"), B("rrs"), B("csd0"), B("csd1")])

        scale1 = sbt(cst, "scale1", [128, 8], F32)
        shiftc = sbt(cst, "shiftc", [128, 8], F32)

        COLS = dict(a_q=0, a_k=256, a_v=512, a_g=768, b_x=1024, b_g=1280, c_q=1536, c_k=1792, c_v=2048,
                    c_g=2304, d_q=2560, d_k=2688, d_v=2816, d_g=3072, d_r=3328)

        def ln_stats(xrow, tagR, st, mv, sd, rstd, nmr, Bst):
            kb.op("dve", lambda g: g.bn_stats(out=st[:, 0:6], in_=xrow[:, 0:512]), tagR, [Bst])
            kb.op("dve", lambda g: g.bn_stats(out=st[:, 6:12], in_=xrow[:, 512:1024]), tagR, [Bst])
            kb.op("dve", lambda g: g.bn_aggr(out=mv[:, 0:2], in_=st[:, 0:12]), [Bst], [Bst])
            TS("dve", sd[:, 0:1], mv[:, 1:2], LN_EPS, None, ALU.add, None, [Bst], [Bst])
            ACTF(sd[:, 0:1], sd[:, 0:1], AF.Ln, [Bst], [Bst])
            ACTF(rstd[:, 0:1], sd[:, 0:1], AF.Exp, [Bst], [Bst], scale=-0.5)
            TS("dve", nmr[:, 0:1], mv[:, 0:1], rstd[:, 0:1], -1.0, ALU.mult, ALU.mult, [Bst], [Bst])

        ckpt("c")
        for l in range(DEPTH):
            xsrc = x_in if l == 0 else x1_d
            xdst = out_d if l == DEPTH - 1 else x1_d
            src_is_scr = l > 0
            L = "L%d_" % l
            with ExitStack() as lay:
                mixAC = sbt(lay, L + "mixAC", [128, 4, SEQ], BF16)

                with ExitStack() as p1:
                    W1 = sbt(p1, L + "W1", [128, 8, 2048], BF16)
                    BW1 = B(L + "W1")
                    with ExitStack() as ms:
                        wm = [sbt(ms, L + "wm%d" % k, [128, 8, 512], F32) for k in range(2)]
                        modB = sbt(ms, L + "modB", [128, 3072], F32)
                        bmodB = sbt(ms, L + "bmodB", [128, 3072], F32)
                        dtmp = sbt(ms, L + "dtmp", [128, 8, 128], F32)
                        stage = [sbt(ms, L + "stg%d" % k, [128, 2048], F32) for k in range(4)]
                        Bm = B("modB")

                        def mod_chain():
                            kb.dma(bmodB[:], b_mod[l:l + 1, :].partition_broadcast(128), (), [B("bmodB")])
                            wmv = w_mod[l].rearrange("(kc p) n -> p kc n", p=128)
                            for g in range(6):
                                slot = g % 2
                                Bw_ = B("wm%d" % slot)
                                kb.dma(wm[slot][:], wmv[:, :, g * 512:(g + 1) * 512], (), [Bw_])
                                yield
                                bank, Bb = gp("p1f")
                                for kc in range(8):
                                    MM(bank[:, 0:512], cB[:, kc, :], wm[slot][:, kc, :], kc == 0, kc == 7, [Bc, Bw_], [Bb])
                                    yield
                                TT("dve", modB[:, g * 512:(g + 1) * 512], bank[:, 0:512], bmodB[:, g * 512:(g + 1) * 512],
                                   ALU.add, [Bb, B("bmodB")], [Bm])
                                yield
                            for (dst, off, add1) in ((shiftc, 0, 0.0), (scale1, 1024, 1.0)):
                                TT("dve", dtmp[:], modB[:, off:off + 1024].rearrange("p (k n) -> p k n", k=8),
                                   ident_f[:].unsqueeze(1).broadcast_to([128, 8, 128]), ALU.mult, [Bm, Bc], [B(L + "dtmp")])
                                kb.op("dve", lambda g: g.tensor_reduce(out=dst[:, 0:8], in_=dtmp[:], axis=AX.X, op=ALU.add),
                                      [B(L + "dtmp")], [B("modcols")])
                                if add1:
                                    TS("dve", dst[:, 0:8], dst[:, 0:8], 1.0, None, ALU.add, None, [B("modcols")], [B("modcols")])
                                yield
                            kb.dma(mod_d[l:l + 1, :], modB[0:1, 2048:3072], [Bm], [B("modd")], sembuf=Bm)
                            yield

                        def w1_chain():
                            srcblk = [0, 1, 4, 5, 3, 7, 2, 6]
                            for kc in range(8):
                                slot = kc % 4
                                Bs = B("stg%d" % slot)
                                kb.dma(stage[slot][:, 0:1024], w_in[l, kc * 128:(kc + 1) * 128, 0:1024], (), [Bs])
                                kb.dma(stage[slot][:, 1024:2048], w_in[l, kc * 128:(kc + 1) * 128, 1536:2560], (), [Bs])
                                yield
                                for j in range(8):
                                    eng = ("pool", "act", "dve")[j % 3]
                                    sb_ = srcblk[j]
                                    CP(eng, W1[:, kc, j * 256:(j + 1) * 256], stage[slot][:, sb_ * 256:(sb_ + 1) * 256], [Bs], [BW1])
                                    yield

                        chains_ = [w1_chain(), mod_chain()]
                        while chains_:
                            for g_ in list(chains_):
                                try:
                                    next(g_)
                                except StopIteration:
                                    chains_.remove(g_)
                        for e in ("pe", "dve", "pool", "act", "sp"):
                            kb.wait_all(e, [Bm, B("bmodB"), B("wm0"), B("wm1"), B(L + "dtmp"), B("modd"), B("stg0"), B("stg1"), B("stg2"), B("stg3")])

                    ckpt("w")
                    KT = [sbt(p1, L + "KT%d" % m, [128, 2, SEQ], BF16) for m in range(2)]
                    VC = [sbt(p1, L + "VC%d" % m, [128, NS, 2, 129], BF16) for m in range(2)]
                    for m in range(2):
                        MEMSET("pool", VC[m][:, :, :, 64:65], 1.0, [B(L + "Vones%d" % m)])
                    xs = [sbt(p1, L + "xs%d" % k, [128, 1024], F32) for k in range(2)]
                    st_ = sbt(p1, L + "st", [128, 12], F32)
                    mv_ = sbt(p1, L + "mv", [128, 2], F32)
                    sd_ = sbt(p1, L + "sd", [128, 1], F32)
                    rstd_ = sbt(p1, L + "rstd", [128, 1], F32)
                    nmr_ = sbt(p1, L + "nmr", [128, 1], F32)
                    uT = [sbt(p1, L + "uT%d" % k, [128, 8, 256], BF16) for k in range(2)]
                    cs = [sbt(p1, L + "cs%d" % k, [128, 512], F32) for k in range(2)]
                    qb = [sbt(p1, L + "qb%d" % k, [128, 256], BF16) for k in range(2)]
                    t1 = [sbt(p1, L + "t1%d" % k, [128, 256], F32) for k in range(2)]
                    t2 = [sbt(p1, L + "t2%d" % k, [128, 256], F32) for k in range(2)]
                    QT = [[sbt(p1, L + "QT%d_%d" % (m, k), [128, 4, 256], BF16) for k in range(2)] for m in range(2)]
                    for m in range(2):
                        for k in range(2):
                            MEMSET("pool", QT[m][k][:], 0.0, [B(L + "QT%d_%d" % (m, k))])
                    sg = [sbt(p1, L + "sg%d" % k, [128, 4, 256], F32) for k in range(2)]
                    ksb = sbt(p1, L + "ksb", [128, 2, 16], BF16)
                    ksf = sbt(p1, L + "ksf", [128, 2, 16], F32)
                    gm = sbt(p1, L + "gm", [128, 8, 16], F32)
                    t8 = sbt(p1, L + "t8", [128, 8, 8], F32)
                    mbf = sbt(p1, L + "mbf", [128, 8, 16], F32)
                    mbias = sbt(p1, L + "mbias", [128, 8, 16], BF16)
                    MBT = [sbt(p1, L + "MBT%d" % k, [128, 4, 256], BF16) for k in range(2)]
                    for k in range(2):
                        MEMSET("pool", MBT[k][:], 0.0, [B(L + "MBT%d" % k)])
                    PT = [sbt(p1, L + "PT%d" % k, [128, 512], BF16) for k in range(4)]
                    nzs = [sbt(p1, L + "nzs%d" % k, [128, 256], F32) for k in range(2)]
                    for k in range(2):
                        MEMSET("pool", nzs[k][:], 0.0, [B(L + "nzs%d" % k)])
                    rz = [sbt(p1, L + "rz%d" % k, [128, 256], F32) for k in range(2)]
                    MEMSET("pool", ksb[:], 0.0, [B(L + "ksb")])
                    pt_state = [0]
                    st_state = [0]
                    hp_state = [0]
                    rope_state = [0]
                    Bst = B(L + "lnst")

                    def load_ln_transpose(i, s, uTt, BuT):
                        gsub = 2 * i + s
                        xt = xs[gsub % 2]
                        Bx = B("xs%d" % (gsub % 2))
                        r0 = gsub * 128
                        rd = [B("x1t%d" % (gsub // 2))] if src_is_scr else []
                        kb.dma(xt[:], xsrc[r0:r0 + 128, :], rd, [Bx])
                        ln_stats(xt, [Bx], st_, mv_, sd_, rstd_, nmr_, Bst)
                        ACTF(xt[:], xt[:], AF.Identity, [Bx, Bst], [Bx], bias=nmr_[:, 0:1], scale=rstd_[:, 0:1])
                        for g in range(2):
                            bank, Bb = gp("p1f")
                            for k4 in range(4):
                                kc = 4 * g + k4
                                TR(bank[:, k4 * 128:(k4 + 1) * 128], xt[:, kc * 128:(kc + 1) * 128], ident_f[:], [Bx, Bc], [Bb])
                            for k4 in range(4):
                                kc = 4 * g + k4
                                if k4 % 2 == 0:
                                    TS("dve", uTt[:, kc, s * 128:(s + 1) * 128], bank[:, k4 * 128:(k4 + 1) * 128],
                                       scale1[:, kc:kc + 1], shiftc[:, kc:kc + 1], ALU.mult, ALU.add,
                                       [Bb, B("modcols")], [BuT])
                                else:
                                    ACTF(uTt[:, kc, s * 128:(s + 1) * 128], bank[:, k4 * 128:(k4 + 1) * 128], AF.Identity,
                                         [Bb, B("modcols")], [BuT], bias=shiftc[:, kc:kc + 1], scale=scale1[:, kc:kc + 1])

                    def p1_front(i):
                        t0 = 256 * i
                        sl = i % 2
                        uTt, BuT = uT[sl], B(L + "uT%d" % sl)
                        cst_, Bcs = cs[sl], B("cs%d" % sl)
                        kb.dma(cst_[:], cs_d[i], [B("csd0"), B("csd1")], [Bcs])
                        yield
                        for s in range(2):
                            load_ln_transpose(i, s, uTt, BuT)
                            yield
                        kb.dma(ut_d[i], uTt[:].rearrange("p k t -> p (k t)"), [BuT], [B("utd%d" % i)], sembuf=BuT)
                        yield
                        ckpt("ln")
                        sgt, Bsg = sg[sl], B(L + "sg%d" % sl)
                        for gi in range(12):
                            bank, Bb = gp("p1f")
                            for kc in range(8):
                                MM(bank[:, 0:256], W1[:, kc, gi * 128:(gi + 1) * 128], uTt[:, kc, :], kc == 0, kc == 7,
                                   [BW1, BuT], [Bb])
                                yield
                            if gi >= 8:
                                if "silu" not in _KSKIP:
                                    SIGM(sgt[:, gi - 8, :], bank[:, 0:256], [Bb], Bsg)
                                    TT("dve", sgt[:, gi - 8, :], bank[:, 0:256], sgt[:, gi - 8, :], ALU.mult, [Bb, Bsg], [Bsg])
                                    yield
                                continue
                            if "rope" in _KSKIP:
                                continue
                            m = gi // 4
                            isk = (gi // 2) % 2
                            ct = gi % 2
                            if isk:
                                dest = KT[m][:, ct, t0:t0 + 256]
                                Bd = B(L + "KT%d_%d_%d" % (m, ct, i))
                            else:
                                dest = None
                                Bd = B(L + "QT%d_%d" % (m, sl))
                            r_ = rope_state[0] % 2
                            rope_state[0] += 1
                            Bq, Bt1, Bt2 = B(L + "qb%d" % r_), B(L + "t1%d" % r_), B(L + "t2%d" % r_)
                            CP("act", qb[r_][:], bank[:, 0:256], [Bb], [Bq])
                            yield
                            bank2, Bb2 = gp("p1f")
                            MM(bank2[:, 0:256], rperm[:], qb[r_][:], True, True, [Bc, Bq], [Bb2])
                            yield
                            if "ropett" in _KSKIP:
                                continue
                            if "nocs" in _KSKIP:
                                TT("dve", t1[r_][:], bank[:, 0:256], sgt[:, 0, :], ALU.mult, [Bb], [Bt1])
                                yield
                                TT("dve", t2[r_][:], bank2[:, 0:256], sgt[:, 1, :], ALU.mult, [Bb2], [Bt2])
                                yield
                            else:
                                TT("dve", t1[r_][:], bank[:, 0:256], cst_[:, 0:256], ALU.mult, [Bb, Bcs], [Bt1])
                                yield
                                TT("dve", t2[r_][:], bank2[:, 0:256], cst_[:, 256:512], ALU.mult, [Bb2, Bcs], [Bt2])
                                yield
                            if "ropepool" in _KSKIP:
                                continue
                            if dest is not None:
                                TT("pool", dest, t1[r_][:], t2[r_][:], ALU.add, [Bt1, Bt2], [Bd])
                                yield
                            else:
                                for hb in range(2):
                                    rs = slice(64 * hb, 64 * hb + 64)
                                    TT("pool", QT[m][sl][rs, 2 * ct + hb, :], t1[r_][rs, :], t2[r_][rs, :], ALU.add,
                                       [Bt1, Bt2], [Bd])
                                    yield
                        ckpt("qk")
                        for s in range(2):
                            gsub = 2 * i + s
                            bank, Bb = gp("p1f")
                            for kc in range(8):
                                MM(bank[:, 0:512], uTt[:, kc, s * 128:(s + 1) * 128], W1[:, kc, 1536:2048], kc == 0, kc == 7,
                                   [BW1, BuT], [Bb])
                                yield
                            for m in range(2):
                                Bv = B(L + "V%d_%d" % (m, i))
                                src = bank[:, m * 256:(m + 1) * 256].rearrange("p (a h d) -> p a h d", a=2, h=2)
                                CP("act" if m == 0 else "dve", VC[m][:, gsub, :, 0:64], src[:, :, 0, :],
                                   [Bb, B(L + "Vones%d" % m)], [Bv])
                                yield
                                CP("dve" if m == 0 else "act", VC[m][:, gsub, :, 65:129], src[:, :, 1, :],
                                   [Bb, B(L + "Vones%d" % m)], [Bv])
                                yield
                        ckpt("v")
                        for ct in range(2):
                            kb.op("dve", lambda g: g.tensor_reduce(out=ksf[:, ct, i:i + 1], in_=KT[0][:, ct, t0:t0 + 256],
                                                                    axis=AX.X, op=ALU.add),
                                  [B(L + "KT0_%d_%d" % (ct, i))], [B(L + "ksf")])
                            yield
                        CP("dve", ksb[:, :, i:i + 1], ksf[:, :, i:i + 1], [B(L + "ksf")], [B(L + "ksb")])
                        yield
                        ckpt("ks%d" % i)
                        MBTt, BMB = MBT[sl], B(L + "MBT%d" % sl)
                        BQA = B(L + "QT0_%d" % sl)
                        if i >= 1:
                            bank, Bb = gp("p1f")
                            for h in range(4):
                                ct = h // 2
                                for s in range(2):
                                    g8 = h * 2 + s
                                    MM(bank[:, g8 * 16:(g8 + 1) * 16], QT[0][sl][:, h, s * 128:(s + 1) * 128],
                                       ksb[:, ct, 0:16], True, True, [BQA, B(L + "ksb")], [Bb])
                                    yield
                            ckpt("gmm%d" % i)
                            Bg = B(L + "gate")
                            TT("dve", gm[:], bank[:, 0:128].rearrange("p (g n) -> p g n", g=8),
                               bmall[:, i, :].unsqueeze(1).broadcast_to([128, 8, 16]), ALU.add, [Bb, Bc], [Bg])
                            yield
                            for g8 in range(8):
                                kb.op("dve", lambda g: g.max(out=t8[:, g8, :], in_=gm[:, g8, :]), [Bg], [Bg])
                                yield
                            TT("dve", mbf[:], gm[:], t8[:, :, 2:3].broadcast_to([128, 8, 16]), ALU.is_ge, [Bg], [Bg])
                            yield
                            TS("dve", mbf[:], mbf[:], -1.0, -NEGM, ALU.add, ALU.mult, [Bg], [Bg])
                            yield
                            TT("dve", mbias[:], mbf[:], bmall[:, i, :].unsqueeze(1).broadcast_to([128, 8, 16]), ALU.add,
                               [Bg, Bc], [Bg])
                            yield
                            ckpt("gtop%d" % i)
                            pb, Bpb = PB
                            for g8 in range(8):
                                TR(pb[0:16, g8 * 128:(g8 + 1) * 128], mbias[:, g8, :], ident_b[:], [Bg, Bc], [Bpb])
                                yield
                            CP("act", MBTt[0:16].rearrange("p h q -> p (h q)"), pb[0:16, 0:1024], [Bpb], [BMB])
                            yield

                        if i == 1:
                            ckpt("g")

                    def p1_att(i):
                        t0 = 256 * i
                        sl = i % 2
                        sgt, Bsg = sg[sl], B(L + "sg%d" % sl)
                        MBTt, BMB = MBT[sl], B(L + "MBT%d" % sl)
                        items = [(m, h) for m in range(2) for h in range(4)]
                        work = []
                        for k_, (m, h) in enumerate(items):
                            if m == 0:
                                kts = list(range(0, 2 * i + 2))
                            else:
                                kts = list(range(max(0, 2 * i - 16), 2 * i + 2))
                            ng = (len(kts) + 1) // 2
                            for gi_ in range(ng):
                                work.append(dict(k=k_, m=m, h=h, grp=kts[2 * gi_:2 * gi_ + 2], base=2 * gi_, nk=len(kts),
                                                 last=(gi_ == ng - 1), idx=len(work)))

                        def emitS(w):
                            m, h = w["m"], w["h"]
                            hr = 64 * (h % 2)
                            ct = h // 2
                            rows = slice(hr, hr + 64)
                            BQ = B(L + "QT%d_%d" % (m, sl))
                            st, Bst_ = ST[w["idx"] % 3]
                            kt0 = w["grp"][0]
                            ti = kt0 // 2
                            if m == 0:
                                w["ord"] = list(w["grp"])
                                if ti < i:
                                    MM(st[:, 0:512], ind[:, ti, :], MBTt[:, h, :].unsqueeze(1).broadcast_to([128, 2, 256]), True, False,
                                       [Bc, BMB], [Bst_], LR=[Bc])
                                else:
                                    MM(st[:, 0:512], ident_b[:], cm[:].rearrange("p c q -> p (c q)"), True, False, [Bc], [Bst_], LR=[Bc])
                            else:
                                w["ord"] = [kt0 + 1, kt0]
                                d1_ = 2 * i - (kt0 + 1)
                                off = 128 * (d1_ + 1)
                                rhs_ap = bass.AP(tb, off, [[2432, 128], [128, 2], [1, 256]])
                                MM(st[:, 0:512], ident_b[:], rhs_ap, True, False, [Bc], [Bst_], LR=[Bc])
                            for c, kt in enumerate(w["ord"]):
                                ti = kt // 2
                                MM(st[:, c * 256:(c + 1) * 256], KT[m][:, ct, kt * 128:(kt + 1) * 128],
                                   QT[m][sl][:, h, :], False, c == 1, [B(L + "KT%d_%d_%d" % (m, ct, ti)), BQ], [Bst_],
                                   LR=[B(L + "KT%d_%d_%d" % (m, ct, ti))])

                        def emitE(w):
                            st, Bst_ = ST[w["idx"] % 3]
                            pslot = w["idx"] % 4
                            wdt = 256 * len(w["grp"])
                            ACTF(PT[pslot][:, 0:wdt], st[:, 0:wdt], AF.Exp, [Bst_], [B(L + "PT%d" % pslot)], scale=0.125)

                        def emitPV(w):
                            m, h = w["m"], w["h"]
                            odd = h % 2
                            M = 128 if odd else 65
                            win = slice(1, 129) if odd else slice(0, 65)
                            pslot = w["idx"] % 4
                            nz, Bnz = NZb[w["k"] % 2]
                            for c, kt in enumerate(w["ord"]):
                                idx_ = w["base"] + c
                                MM(nz[0:M, 0:256], VC[m][:, kt, h // 2, win], PT[pslot][:, c * 256:(c + 1) * 256],
                                   idx_ == 0, idx_ == w["nk"] - 1, [B(L + "V%d_%d" % (m, kt // 2)), B(L + "PT%d" % pslot)], [Bnz],
                                   LR=[B(L + "V%d_%d" % (m, kt // 2))], inc=(c == len(w["ord"]) - 1))

                        def fin1(w):
                            odd = w["h"] % 2
                            M = 128 if odd else 65
                            nz, Bnz = NZb[w["k"] % 2]
                            hp = w["k"] % 2
                            CP("act", nzs[hp][0:M, :], nz[0:M, 0:256], [Bnz], [B(L + "nzs%d" % hp)])

                        def fin2(w):
                            m, h = w["m"], w["h"]
                            odd = h % 2
                            hr = 64 * odd
                            ct = h // 2
                            rows = slice(hr, hr + 64)
                            hp = w["k"] % 2
                            Bn, Br = B(L + "nzs%d" % hp), B(L + "rz%d" % hp)
                            zb, Bzb = NZb[w["k"] % 2]
                            if odd:
                                MM(zb[:, 256:512], selB[:, :], nzs[hp][:, :], True, True, [Bc, Bn], [Bzb])
                            else:
                                MM(zb[0:64, 256:512], selA[:, 0:64], nzs[hp][:, :], True, True, [Bc, Bn], [Bzb])
                            ACTF(rz[hp][rows, :], zb[rows, 256:512], AF.Ln, [Bzb], [Br])
                            ACTF(rz[hp][rows, :], rz[hp][rows, :], AF.Exp, [Br], [Br], scale=-1.0)
                            gidx = 2 * m + ct
                            TT("pool", nzs[hp][rows, :], nzs[hp][rows, :], sgt[rows, gidx, :], ALU.mult, [Bn, Bsg], [Bn])
                            TT("dve", mixAC[rows, gidx, t0:t0 + 256], nzs[hp][rows, :], rz[hp][rows, :], ALU.mult,
                               [Bn, Br], [B(L + "mix%d_%d" % (gidx, i))])

                        deferred = []
                        emitS(work[0])
                        if len(work) > 1:
                            emitS(work[1])
                        for idx, w in enumerate(work):
                            emitE(w)
                            if idx + 2 < len(work):
                                emitS(work[idx + 2])
                            emitPV(w)
                            if w["last"]:
                                fin1(w)
                                deferred.append((idx + 1, w))
                            while deferred and deferred[0][0] <= idx:
                                fin2(deferred.pop(0)[1])
                            yield
                        while deferred:
                            fin2(deferred.pop(0)[1])

                    P1LEN = {}

                    def run_chains1(gens, names=None):
                        gens = list(gens)
                        names = list(names) if names else [None] * len(gens)
                        cnt = [0] * len(gens)
                        alive = [True] * len(gens)
                        tot = [float(P1LEN.get(n, 100)) for n in names]
                        while any(alive):
                            j = min((k for k in range(len(gens)) if alive[k]), key=lambda k: cnt[k] / tot[k])
                            try:
                                next(gens[j])
                                cnt[j] += 1
                            except StopIteration:
                                alive[j] = False
                                if names[j] is not None:
                                    P1LEN[names[j]] = max(cnt[j], 1)

                    run_chains1([p1_front(0)], ["front"])
                    for i in range(NT):
                        gens = [p1_att(i)]
                        nms = [None]
                        P1LEN["att"] = 4 * (i + 1) + 4 * min(i + 1, 9) + 1
                        nms = ["att"]
                        if i + 1 < NT:
                            gens.append(p1_front(i + 1))
                            nms.append("front")
                        run_chains1(gens, nms)
                    for e in ("pe", "dve", "act", "pool", "sp"):
                        kb.wait_all(e, list(kb.bufs.values()))

                ckpt("p1")
                with ExitStack() as p2:
                    W2 = sbt(p2, L + "W2", [128, 8, 1296], BF16)
                    Wo = sbt(p2, L + "Wo", [128, 8, 1024], BF16)
                    lnG = sbt(p2, L + "lnG", [128, 1024], F32)
                    lnBt = sbt(p2, L + "lnB", [128, 1024], F32)
                    cw = sbt(p2, L + "cw", [128, 8], F32)
                    cbv = sbt(p2, L + "cbv", [128, 2], F32)
                    bav = sbt(p2, L + "bav", [128, 2], F32)
                    bxv = sbt(p2, L + "bxv", [128, 2], F32)
                    lamv = sbt(p2, L + "lamv", [128, 2], F32)
                    n8sp = sbt(p2, L + "n8sp", [128, 2], F32)
                    nbav = sbt(p2, L + "nbav", [128, 2], F32)
                    nbxv = sbt(p2, L + "nbxv", [128, 2], F32)
                    WaBD = sbt(p2, L + "WaBD", [128, 2, 128], BF16)
                    WxBD = sbt(p2, L + "WxBD", [128, 2, 128], BF16)
                    wr_f = sbt(p2, L + "wr_f", [16, 128], F32)
                    br_f = sbt(p2, L + "br_f", [1, 128], F32)
                    gnB = sbt(p2, L + "gnB", [128, 64], F32)
                    BW2, BWo, Bpr = B(L + "W2"), B(L + "Wo"), B(L + "p2par")
                    w2map = [(COLS["b_x"], 0, 512), (COLS["d_q"], 512, 768), (COLS["d_r"], 1280, 16)]
                    with ExitStack() as stg:
                        stage = [sbt(stg, L + "stgb%d" % k, [128, 1296], F32) for k in range(4)]
                        g1B = sbt(stg, L + "g1B", [128, 1024], F32)
                        bd = sbt(stg, L + "bd", [128, 2, 2, 128], F32)
                        for kc in range(8):
                            slot = kc % 4
                            Bs = B("stgb%d" % slot)
                            kb.dma(stage[slot][:, 0:512], w_in[l, kc * 128:(kc + 1) * 128, 1024:1536], (), [Bs])
                            kb.dma(stage[slot][:, 512:1296], w_in[l, kc * 128:(kc + 1) * 128, 2560:3344], (), [Bs])
                            CP("dve", W2[:, kc, 0:512], stage[slot][:, 0:512], [Bs], [BW2])
                            CP("pool", W2[:, kc, 512:896], stage[slot][:, 512:896], [Bs], [BW2])
                            CP("act", W2[:, kc, 896:1296], stage[slot][:, 896:1296], [Bs], [BW2])
                        kb.dma(g1B[:], mod_d[l:l + 1, :].partition_broadcast(128), [B("modd")], [B("g1B")])
                        TS("dve", g1B[:], g1B[:], 1.0, None, ALU.add, None, [B("g1B")], [B("g1B")])
                        for kc in range(8):
                            slot = kc % 4
                            Bs = B("stgb%d" % slot)
                            kb.dma(stage[slot][:, 0:1024], w_out[l, kc * 128:(kc + 1) * 128, :], (), [Bs])
                            TT("dve" if kc % 2 == 0 else "pool", Wo[:, kc, :], stage[slot][:, 0:1024], g1B[:], ALU.mult,
                               [Bs, B("g1B")], [BWo])
                        kb.dma(lnG[:], lng_in[l:l + 1, :].partition_broadcast(128), (), [B("lnG")])
                        kb.dma(lnBt[:], lnb_in[l:l + 1, :].partition_broadcast(128), (), [B("lnB")])
                        kb.dma(cw[:], convw_in[l], (), [B("cw")])
                        kb.dma(cbv[:], convb_in[l], (), [B("cbv")])
                        kb.dma(bav[:], lruba_in[l], (), [B("bav")])
                        kb.dma(bxv[:], lrubx_in[l], (), [B("bxv")])
                        kb.dma(lamv[:], lrulam_in[l], (), [B("lamv")])
                        kb.dma(wr_f[:], glawr_in[l], (), [B("wr_f")])
                        kb.dma(br_f[:], glabr_in[l:l + 1, :], (), [B("br_f")])
                        kb.dma(gnB[:], glagn_in[l:l + 1, :].partition_broadcast(128), (), [B("gnB")])
                        ACTF(n8sp[:], lamv[:], AF.Exp, [B("lamv")], [Bpr], scale=-1.0)
                        ACTF(n8sp[:], n8sp[:], AF.Ln, [Bpr], [Bpr], bias=1.0)
                        TS("dve", n8sp[:], n8sp[:], -8.0, None, ALU.mult, None, [Bpr], [Bpr])
                        TS("dve", nbav[:], bav[:], -1.0, None, ALU.mult, None, [B("bav")], [Bpr])
                        TS("dve", nbxv[:], bxv[:], -1.0, None, ALU.mult, None, [B("bxv")], [Bpr])
                        MEMSET("pool", bd[:], 0.0, [B("bd")])
                        for wi, src in enumerate((lruwa_in, lruwx_in)):
                            for ct in range(2):
                                for hb in range(2):
                                    kb.dma(bd[64 * hb:64 * hb + 64, wi, ct, 64 * hb:64 * hb + 64], src[l, 2 * ct + hb],
                                           (), [B("bd")])
                        CP("dve", WaBD[:], bd[:, 0, :, :], [B("bd")], [Bpr])
                        CP("dve", WxBD[:], bd[:, 1, :, :], [B("bd")], [Bpr])
                        for e in ("dve", "pool", "act", "sp"):
                            kb.wait_all(e, [B("stgb0"), B("stgb1"), B("stgb2"), B("stgb3"), B("g1B"), B("bd")])
                    Bpar = [Bpr, B("cw"), B("cbv"), B("bav"), B("bxv"), B("wr_f"), B("br_f"),
                            B("gnB")]

                    ckpt("s2")
                    xt2 = [sbt(p2, L + "xt%d" % k, [128, 2, 1024], F32) for k in range(3)]
                    LNS = []
                    for k_ in range(2):
                        LNS.append((sbt(p2, L + "lst%d" % k_, [128, 12], F32), sbt(p2, L + "lmv%d" % k_, [128, 2], F32),
                                    sbt(p2, L + "lsd%d" % k_, [128, 1], F32), sbt(p2, L + "lrs%d" % k_, [128, 1], F32),
                                    sbt(p2, L + "lnm%d" % k_, [128, 1], F32), B(L + "lnst2_%d" % k_)))
                    xn = [sbt(p2, L + "xn%d" % k, [128, 1024], F32) for k in range(2)]
                    st_ = sbt(p2, L + "st2", [128, 12], F32)
                    mv_ = sbt(p2, L + "mv2", [128, 2], F32)
                    sd_ = sbt(p2, L + "sd2", [128, 1], F32)
                    rstd_ = sbt(p2, L + "rstd2", [128, 1], F32)
                    nmr_ = sbt(p2, L + "nmr2", [128, 1], F32)
                    uT = [sbt(p2, L + "uTb%d" % k, [128, 8, 256], BF16) for k in range(2)]
                    mix2 = [sbt(p2, L + "mix2_%d" % k, [128, 4, 256], BF16) for k in range(2)]
                    XH = sbt(p2, L + "XH", [128, 2, 259], F32)
                    xc = sbt(p2, L + "xc", [128, 2, 256], F32)
                    xcb = sbt(p2, L + "xcb", [128, 2, 256], BF16)
                    rg = sbt(p2, L + "rg", [128, 2, 256], F32)
                    ig = sbt(p2, L + "ig", [128, 2, 256], F32)
                    av = sbt(p2, L + "av", [128, 2, 256], F32)
                    uv = sbt(p2, L + "uv", [128, 2, 256], F32)
                    hv = sbt(p2, L + "hv", [128, 2, 256], F32)
                    hcar = sbt(p2, L + "hcar", [128, 2], F32)
                    sgb = sbt(p2, L + "sgb", [128, 2, 256], F32)
                    drf = sbt(p2, L + "drf", [16, 256], F32)
                    e1 = [sbt(p2, L + "e1%d" % k_, [128, 128], F32) for k_ in range(2)]
                    lsp = [sbt(p2, L + "lsp%d" % k_, [128, 128], F32) for k_ in range(2)]
                    eq = [sbt(p2, L + "eq%d" % k_, [128, 128], F32) for k_ in range(2)]
                    ek = [sbt(p2, L + "ek%d" % k_, [128, 128], F32) for k_ in range(2)]
                    eb = [sbt(p2, L + "eb%d" % k_, [128, 128], F32) for k_ in range(2)]
                    ekl = [sbt(p2, L + "ekl%d" % k_, [128, 128], F32) for k_ in range(2)]
                    dcol = [sbt(p2, L + "dcol%d" % k_, [128, 1], F32) for k_ in range(2)]
                    qs = [sbt(p2, L + "qs%d" % k_, [128, 128], BF16) for k_ in range(2)]
                    ks = [sbt(p2, L + "ks%d" % k_, [128, 128], BF16) for k_ in range(2)]
                    kh = [sbt(p2, L + "kh%d" % k_, [128, 128], BF16) for k_ in range(2)]
                    vb = [sbt(p2, L + "vb%d" % k_, [128, 256], BF16) for k_ in range(2)]
                    QB = [sbt(p2, L + "QB%d" % k_, [128, 4, 128], BF16) for k_ in range(2)]
                    kst = [sbt(p2, L + "kst%d" % k_, [128, 128], BF16) for k_ in range(2)]
                    ATm = [sbt(p2, L + "ATm%d" % k_, [128, 4, 128], BF16) for k_ in range(2)]
                    Sw = sbt(p2, L + "Sw", [128, 256], F32)
                    Swb = sbt(p2, L + "Swb", [128, 256], BF16)
                    osb = [sbt(p2, L + "osb%d" % k_, [128, 4, 64], F32) for k_ in range(2)]
                    osq = [sbt(p2, L + "osq%d" % k_, [128, 4, 64], F32) for k_ in range(2)]
                    ss = [sbt(p2, L + "ss%d" % k_, [128, 4], F32) for k_ in range(2)]
                    rr = [sbt(p2, L + "rr%d" % k_, [128, 4], F32) for k_ in range(2)]
                    sgd = [sbt(p2, L + "sgd%d" % k_, [128, 4, 64], F32) for k_ in range(2)]
                    mixD = [sbt(p2, L + "mixD%d" % k_, [128, 256], BF16) for k_ in range(2)]
                    Bst = B(L + "lnst2")
                    MEMSET("pool", XH[:], 0.0, [B(L + "XH")])
                    MEMSET("pool", Sw[:], 0.0, [B(L + "Sw")])
                    MEMSET("pool", Swb[:], 0.0, [B(L + "Swb")])
                    MEMSET("pool", hcar[:], 0.0, [B(L + "hcar")])

                    def ch_front(i):
                        t0 = 256 * i
                        sl = i % 2
                        xtt, Bx = xt2[i % 3], B("xt%d" % (i % 3))
                        uTt, BuT = uT[sl], B(L + "uTb%d" % sl)
                        m2, Bm2 = mix2[sl], B(L + "mix2_%d" % sl)
                        st_, mv_, sd_, rstd_, nmr_, Bst = LNS[0]
                        rd = [B("x1t%d" % i)] if src_is_scr else []
                        for s in range(2):
                            r0 = t0 + 128 * s
                            kb.dma(xtt[:, s, :], xsrc[r0:r0 + 128, :], rd, [Bx])
                            yield
                        kb.dma(uTt[:].rearrange("p k t -> p (k t)"), ut_d[i], [B("utd%d" % i)], [BuT])
                        yield

                    def ch_lru(i):
                        t0 = 256 * i
                        sl = i % 2
                        xtt, Bx = xt2[i % 3], B("xt%d" % (i % 3))
                        uTt, BuT = uT[sl], B(L + "uTb%d" % sl)
                        m2, Bm2 = mix2[sl], B(L + "mix2_%d" % sl)
                        BXH, Bxc, Bg_ = B(L + "XH"), B(L + "xc"), B(L + "lrug")
                        for gi in range(4):
                            bank, Bb = gp("lru")
                            for kc in range(8):
                                MM(bank[:, 0:256], W2[:, kc, gi * 128:(gi + 1) * 128], uTt[:, kc, :], kc == 0, kc == 7,
                                   [BW2, BuT], [Bb])
                                yield
                            if gi < 2:
                                CP("act", XH[:, gi, 3:259], bank[:, 0:256], [Bb], [BXH])
                                yield
                            else:
                                SIGM(sgb[:, gi - 2, :], bank[:, 0:256], [Bb], B(L + "sgb"))
                                TT("dve", sgb[:, gi - 2, :], bank[:, 0:256], sgb[:, gi - 2, :], ALU.mult, [Bb, B(L + "sgb")], [B(L + "sgb")])
                                yield
                        for ct in range(2):
                            TS("dve", xc[:, ct, :], XH[:, ct, 0:256], cw[:, 4 * ct:4 * ct + 1], cbv[:, ct:ct + 1],
                               ALU.mult, ALU.add, [BXH] + Bpar, [Bxc])
                            yield
                            for k in range(1, 4):
                                STT(xc[:, ct, :], XH[:, ct, k:k + 256], cw[:, 4 * ct + k:4 * ct + k + 1], xc[:, ct, :],
                                    ALU.mult, ALU.add, [BXH, Bxc] + Bpar, [Bxc])
                                yield
                        CP("dve", XH[:, :, 0:3], XH[:, :, 256:259], [BXH, Bxc], [BXH])
                        yield
                        CP("act", xcb[:], xc[:], [Bxc], [B(L + "xcb")])
                        yield
                        for ct in range(2):
                            for wi, (Wt, bvec, dst) in enumerate(((WaBD, nbav, rg), (WxBD, nbxv, ig))):
                                bank, Bb = gp("lru")
                                MM(bank[:, 0:256], Wt[:, ct, :], xcb[:, ct, :], True, True, [Bpr, B(L + "xcb")], [Bb])
                                yield
                                SIGM(dst[:, ct, :], bank[:, 0:256], [Bb] + Bpar, Bg_, nbias=bvec[:, ct:ct + 1])
                                yield
                        for ct in range(2):
                            ACTF(av[:, ct, :], rg[:, ct, :], AF.Exp, [Bg_, Bpr], [B(L + "av")], scale=n8sp[:, ct:ct + 1])
                            yield
                        TT("dve", uv[:], av[:], av[:], ALU.mult, [B(L + "av")], [B(L + "uv")])
                        yield
                        TS("dve", uv[:], uv[:], -1.0, 1.0, ALU.mult, ALU.add, [B(L + "uv")], [B(L + "uv")])
                        yield
                        TS("dve", uv[:], uv[:], 1e-30, None, ALU.max, None, [B(L + "uv")], [B(L + "uv")])
                        yield
                        ACTF(uv[:], uv[:], AF.Ln, [B(L + "uv")], [B(L + "uv")])
                        ACTF(uv[:], uv[:], AF.Exp, [B(L + "uv")], [B(L + "uv")], scale=0.5)
                        yield
                        TT("dve", ig[:], ig[:], xc[:], ALU.mult, [Bg_, Bxc], [Bg_])
                        yield
                        TT("dve", uv[:], uv[:], ig[:], ALU.mult, [B(L + "uv"), Bg_], [B(L + "uv")])
                        yield
                        for ct in range(2):
                            kb.op("dve", lambda g: g.tensor_tensor_scan(out=hv[:, ct, :], data0=av[:, ct, :], data1=uv[:, ct, :],
                                                                        initial=hcar[:, ct:ct + 1], op0=ALU.mult, op1=ALU.add),
                                  [B(L + "av"), B(L + "uv"), B(L + "hcar")], [B(L + "hv")])
                            yield
                        CP("dve", hcar[:], hv[:, :, 255], [B(L + "hv")], [B(L + "hcar")])
                        yield
                        TT("dve", m2[:, 0:2, :], hv[:], sgb[:], ALU.mult, [B(L + "hv"), B(L + "sgb")], [Bm2])
                        yield

                    def ch_gla(i, s):
                        t0 = 256 * i
                        sl = i % 2
                        xtt, Bx = xt2[i % 3], B("xt%d" % (i % 3))
                        uTt, BuT = uT[sl], B(L + "uTb%d" % sl)
                        m2, Bm2 = mix2[sl], B(L + "mix2_%d" % sl)
                        if s == 0:
                            bank, Bb = gp("gla%d" % s)
                            for kc in range(8):
                                MM(bank[0:16, 0:256], W2[:, kc, 1280:1296], uTt[:, kc, :], kc == 0, kc == 7, [BW2, BuT], [Bb])
                                yield
                            CP("act", drf[:], bank[0:16, 0:256], [Bb], [B(L + "drf")])
                            yield
                            gflags[("drf", i)] = True
                        else:
                            while not gflags.get(("drf", i)):
                                yield
                        Bgl = B(L + "gla")
                        lin, Bl = gp("gla%d" % s)
                        MM(lin[:, 0:128], drf[0:16, s * 128:(s + 1) * 128], wr_f[0:16, :], True, False,
                           [B(L + "drf")] + Bpar, [Bl])
                        yield
                        MM(lin[:, 0:128], onesrow[0:1, :], br_f[0:1, :], False, True, [Bc] + Bpar, [Bl])
                        yield
                        ACTF(e1[s][:], lin[:, 0:128], AF.Exp, [Bl], [B(L + "e1%d" % s)], scale=-1.0)
                        yield
                        ACTF(lsp[s][:], e1[s][:], AF.Ln, [B(L + "e1%d" % s)], [B(L + "lsp%d" % s)], bias=1.0)
                        yield
                        bc_, Bbc = gp("gla%d" % s)
                        MM(bc_[:, 0:128], tri16[:], lsp[s][:], True, True, [Bc, B(L + "lsp%d" % s)], [Bbc])
                        yield
                        MM(bc_[:, 128:256], ones16[:], lsp[s][:], True, True, [Bc, B(L + "lsp%d" % s)], [Bbc])
                        yield
                        MM(bc_[:, 256:257], lsp[s][:], ones16[:, 0:1], True, True, [Bc, B(L + "lsp%d" % s)], [Bbc])
                        yield
                        ACTF(eq[s][:], bc_[:, 0:128], AF.Exp, [Bbc], [B(L + "eq%d" % s)])
                        yield
                        ACTF(ek[s][:], bc_[:, 0:128], AF.Exp, [Bbc], [B(L + "ek%d" % s)], scale=-1.0)
                        yield
                        ACTF(eb[s][:], bc_[:, 128:256], AF.Exp, [Bbc], [B(L + "eb%d" % s)])
                        yield
                        ACTF(dcol[s][:], bc_[:, 256:257], AF.Exp, [Bbc], [B(L + "dcol%d" % s)])
                        yield
                        TT("dve", ekl[s][:], ek[s][:], eb[s][:], ALU.mult, [B(L + "ek%d" % s), B(L + "eb%d" % s)], [B(L + "ekl%d" % s)])
                        yield
                        d1, Bd1 = gp("gla%d" % s)
                        for kc in range(8):
                            MM(d1[:, 0:512], uTt[:, kc, s * 128:(s + 1) * 128], W2[:, kc, 512:1024], kc == 0, kc == 7,
                               [BW2, BuT], [Bd1])
                            yield
                        STT(qs[s][:], d1[:, 0:128], 32.0 ** -0.5, eq[s][:], ALU.mult, ALU.mult, [Bd1, B(L + "eq%d" % s)], [B(L + "qs%d" % s)])
                        yield
                        TT("dve", ks[s][:], d1[:, 128:256], ek[s][:], ALU.mult, [Bd1, B(L + "ek%d" % s)], [B(L + "ks%d" % s)])
                        yield
                        TT("dve", kh[s][:], d1[:, 128:256], ekl[s][:], ALU.mult, [Bd1, B(L + "ekl%d" % s)], [B(L + "kh%d" % s)])
                        yield
                        CP("act", vb[s][:], d1[:, 256:512], [Bd1], [B(L + "vb%d" % s)])
                        yield
                        pb, Bpb = PB
                        po = 512 * s
                        TR(pb[:, po:po + 128], qs[s][:], ident_b[:], [B(L + "qs%d" % s), Bc], [Bpb])
                        yield
                        TR(pb[:, po + 128:po + 256], ks[s][:], ident_b[:], [B(L + "ks%d" % s), Bc], [Bpb])
                        yield
                        TT("dve", QB[s][:], pb[:, po:po + 128].unsqueeze(1).broadcast_to([128, 4, 128]), hm[:], ALU.mult,
                           [Bpb, Bc], [B(L + "QB%d" % s)])
                        yield
                        CP("act", kst[s][:], pb[:, po + 128:po + 256], [Bpb], [B(L + "kst%d" % s)])
                        yield
                        at, Bat = gp("gla%d" % s)
                        MM(at[:, 0:512], kst[s][:], QB[s][:].rearrange("p h i -> p (h i)"), True, True,
                           [B(L + "kst%d" % s), B(L + "QB%d" % s)], [Bat])
                        yield
                        TT("dve", ATm[s][:], at[:, 0:512].rearrange("p (h i) -> p h i", h=4),
                           tri[:].unsqueeze(1).broadcast_to([128, 4, 128]), ALU.mult, [Bat, Bc], [B(L + "ATm%d" % s)])
                        yield
                        if s == 1:
                            while not gflags.get(("swb", i, 0)):
                                yield
                        ob, Bob = gp("gla%d" % s)
                        for h in range(4):
                            MM(ob[:, 64 * h:64 * h + 64], ATm[s][:, h, :], vb[s][:, 64 * h:64 * h + 64], True, False,
                               [B(L + "ATm%d" % s), B(L + "vb%d" % s)], [Bob])
                            yield
                            MM(ob[:, 64 * h:64 * h + 64], QB[s][:, h, :], Swb[:, 64 * h:64 * h + 64], False, True,
                               [B(L + "QB%d" % s), B(L + "Swb")], [Bob])
                            yield
                        CP("act", osb[s][:].rearrange("p h d -> p (h d)"), ob[:, 0:256], [Bob], [B(L + "osb%d" % s)])
                        yield
                        spb, Bsp = gp("gla%d" % s)
                        MM(spb[:, 0:256], kh[s][:], vb[s][:], True, True, [B(L + "kh%d" % s), B(L + "vb%d" % s)], [Bsp])
                        yield
                        STT(Sw[:], Sw[:], dcol[s][:, 0:1], spb[:, 0:256], ALU.mult, ALU.add, [B(L + "Sw"), B(L + "dcol%d" % s), Bsp],
                            [B(L + "Sw")])
                        yield
                        CP("act", Swb[:], Sw[:], [B(L + "Sw")], [B(L + "Swb")])
                        gflags[("swb", i, s)] = True
                        yield
                        d2, Bd2 = gp("gla%d" % s)
                        for kc in range(8):
                            MM(d2[:, 0:256], uTt[:, kc, s * 128:(s + 1) * 128], W2[:, kc, 1024:1280], kc == 0, kc == 7,
                               [BW2, BuT], [Bd2])
                            yield
                        TT("dve", osq[s][:], osb[s][:], osb[s][:], ALU.mult, [B(L + "osb%d" % s)], [B(L + "osq%d" % s)])
                        yield
                        kb.op("dve", lambda g: g.tensor_reduce(out=ss[s][:, 0:4], in_=osq[s][:], axis=AX.X, op=ALU.add),
                              [B(L + "osq%d" % s)], [B(L + "ss%d" % s)])
                        yield
                        TS("dve", ss[s][:], ss[s][:], 1.0 / 64.0, LN_EPS, ALU.mult, ALU.add, [B(L + "ss%d" % s)], [B(L + "ss%d" % s)])
                        yield
                        ACTF(ss[s][:], ss[s][:], AF.Ln, [B(L + "ss%d" % s)], [B(L + "ss%d" % s)])
                        yield
                        ACTF(rr[s][:], ss[s][:], AF.Exp, [B(L + "ss%d" % s)], [B(L + "rr%d" % s)], scale=-0.5)
                        yield
                        SIGM(sgd[s][:].rearrange("p h d -> p (h d)"), d2[:, 0:256], [Bd2], B(L + "sgd%d" % s))
                        TT("dve", sgd[s][:].rearrange("p h d -> p (h d)"), d2[:, 0:256], sgd[s][:].rearrange("p h d -> p (h d)"), ALU.mult,
                           [Bd2, B(L + "sgd%d" % s)], [B(L + "sgd%d" % s)])
                        yield
                        TT("dve", sgd[s][:], sgd[s][:], gnB[:].unsqueeze(1).broadcast_to([128, 4, 64]), ALU.mult,
                           [B(L + "sgd%d" % s)] + Bpar, [B(L + "sgd%d" % s)])
                        yield
                        TT("dve", osb[s][:], osb[s][:], rr[s][:].unsqueeze(2).broadcast_to([128, 4, 64]), ALU.mult,
                           [B(L + "osb%d" % s), B(L + "rr%d" % s)], [B(L + "osb%d" % s)])
                        yield
                        TT("dve", mixD[s][:].rearrange("p (h d) -> p h d", h=4), osb[s][:], sgd[s][:], ALU.mult,
                           [B(L + "osb%d" % s), B(L + "sgd%d" % s)], [B(L + "mixD%d" % s)])
                        yield
                        TR(pb[:, po + 256:po + 384], mixD[s][:, 0:128], ident_b[:], [B(L + "mixD%d" % s), Bc], [Bpb])
                        yield
                        TR(pb[:, po + 384:po + 512], mixD[s][:, 128:256], ident_b[:], [B(L + "mixD%d" % s), Bc], [Bpb])
                        yield
                        CP("act", m2[:, 2:4, s * 128:(s + 1) * 128], pb[:, po + 256:po + 512].rearrange("p (k t) -> p k t", k=2),
                           [Bpb], [Bm2])
                        yield

                    def ch_out(i):
                        t0 = 256 * i
                        sl = i % 2
                        xtt, Bx = xt2[i % 3], B("xt%d" % (i % 3))
                        uTt, BuT = uT[sl], B(L + "uTb%d" % sl)
                        m2, Bm2 = mix2[sl], B(L + "mix2_%d" % sl)
                        st_, mv_, sd_, rstd_, nmr_, Bst = LNS[1]
                        for s in range(2):
                            for half in range(2):
                                yb, Byb = gp("out")
                                for f in range(8):
                                    grp = f // 2
                                    if grp in (0, 2):
                                        slot4 = (0 if grp == 0 else 2) + (f % 2)
                                        lhs = mixAC[:, slot4, t0 + s * 128:t0 + (s + 1) * 128]
                                        Rd = [B(L + "mix%d_%d" % (slot4, i))]
                                    else:
                                        slot4 = (0 if grp == 1 else 2) + (f % 2)
                                        lhs = m2[:, slot4, s * 128:(s + 1) * 128]
                                        Rd = [Bm2]
                                    MM(yb[:, 0:512], lhs, Wo[:, f, half * 512:(half + 1) * 512], f == 0, f == 7, Rd + [BWo], [Byb])
                                    yield
                                STT(xtt[:, s, half * 512:(half + 1) * 512], xtt[:, s, half * 512:(half + 1) * 512], ALPHA,
                                    yb[:, 0:512], ALU.mult, ALU.add, [Bx, Byb], [Bx])
                                yield
                            ln_stats(xtt[:, s, :], [Bx], st_, mv_, sd_, rstd_, nmr_, Bst)
                            yield
                            ACTF(xtt[:, s, :], xtt[:, s, :], AF.Identity, [Bx, Bst], [Bx], bias=nmr_[:, 0:1], scale=rstd_[:, 0:1])
                            yield
                            TT("dve", xtt[:, s, :], xtt[:, s, :], lnG[:], ALU.mult, [Bx, B("lnG")], [Bx])
                            yield
                            TT("dve", xtt[:, s, :], xtt[:, s, :], lnBt[:], ALU.add, [Bx, B("lnB")], [Bx])
                            yield
                        Bdst = B("x1t%d" % i) if l < DEPTH - 1 else B("outt%d" % i)
                        for s in range(2):
                            r0 = t0 + 128 * s
                            kb.dma(xdst[r0:r0 + 128, :], xtt[:, s, :], [Bx], [Bdst], sembuf=Bx)
                            yield


                    def run_chains(gens):
                        gens = list(gens)
                        while gens:
                            for g_ in list(gens):
                                try:
                                    next(g_)
                                except StopIteration:
                                    gens.remove(g_)

                    gflags = {}
                    run_chains([ch_front(0)])
                    for i in range(NT):
                        gens = [ch_gla(i, 0), ch_gla(i, 1), ch_lru(i)]
                        if i + 1 < NT:
                            gens.append(ch_front(i + 1))
                        if i >= 1:
                            gens.append(ch_out(i - 1))
                        run_chains(gens)
                    run_chains([ch_out(NT - 1)])
                    for e in ("pe", "dve", "act", "pool", "sp"):
                        kb.wait_all(e, list(kb.bufs.values()))
        for e in ("sp", "act"):
            kb.wait_all(e, [b for n_, b in kb.bufs.items() if n_.startswith("outt")])
        build.stats = dict(nins=kb.nins, nwait=kb.nwait, nsem=kb.nsem)
    return nc


_NC_CACHE = {}


def _layout_inputs(inp, b, SEQ):
    f = lambda a: np.ascontiguousarray(a, dtype=np.float32)
    dep = inp["w_mod"].shape[0]
    half = 32
    inv = (np.float32(10000.0) ** (-(np.arange(128) % half).astype(np.float32) / np.float32(half))).astype(np.float32)
    d = {
        "x": f(inp["x"][b]),
        "cT": f(inp["c"][b].reshape(8, 128).T),
        "pos": np.ascontiguousarray(inp["positions"][b][None, :].astype(np.int32)),
        "rope_inv": f(inv[:, None]),
        "w_mod": f(inp["w_mod"]), "b_mod": f(inp["b_mod"]), "w_in": f(inp["w_in"]),
        "conv_w": f(inp["conv_w"].reshape(dep, 4, 2, 128).transpose(0, 3, 2, 1).reshape(dep, 128, 8)),
        "conv_b": f(inp["conv_b"].reshape(dep, 2, 128).transpose(0, 2, 1)),
        "lru_wa": f(inp["lru_wa"]), "lru_ba": f(inp["lru_ba"].reshape(dep, 2, 128).transpose(0, 2, 1)),
        "lru_wx": f(inp["lru_wx"]), "lru_bx": f(inp["lru_bx"].reshape(dep, 2, 128).transpose(0, 2, 1)),
        "lru_lam": f(inp["lru_lam"].reshape(dep, 2, 128).transpose(0, 2, 1)),
        "gla_wr": f(inp["gla_wr"]), "gla_br": f(inp["gla_br"]), "gla_gn": f(inp["gla_gn"]),
        "w_out": f(inp["w_out"]), "ln_g": f(inp["ln_g"]), "ln_b": f(inp["ln_b"]),
    }
    return d


def kernel(**inputs):
    inp = {k: np.asarray(v) for k, v in inputs.items()}
    Bn, SEQ, _ = inp["x"].shape
    DEPTH = inp["w_mod"].shape[0]
    key = (SEQ, DEPTH)
    if key not in _NC_CACHE:
        _NC_CACHE[key] = build(SEQ, DEPTH)
    nc = _NC_CACHE[key]
    in_maps = [_layout_inputs(inp, b, SEQ) for b in range(Bn)]
    res = run_bass_kernel_spmd(nc, in_maps, core_ids=list(range(Bn)))
    return np.stack([np.asarray(r["out"], dtype=np.float32) for r in res.results], axis=0)
```

```python
import math
from contextlib import ExitStack

import numpy as np
import concourse.bass as bass
import concourse.mybir as mybir
from concourse.bass_utils import run_bass_kernel_spmd

F32 = mybir.dt.float32
BF16 = mybir.dt.bfloat16
I32 = mybir.dt.int32
AF = mybir.ActivationFunctionType
ALU = mybir.AluOpType
AX = mybir.AxisListType

D = 1024
NEGM = -240000.0
LN_EPS = 1e-5
ALPHA = 4.0 ** 0.25
SEM_LIMIT = 30000
import os
_KSTOP = os.environ.get("KSTOP", "")
_KSKIP = set(os.environ.get("KSKIP", "").split(","))


_STOP = [False]


def ckpt(name):
    if _KSTOP and _KSTOP == name:
        _STOP[0] = True


class Buf:
    __slots__ = ("name", "w", "r", "sem", "cnt")

    def __init__(self, name):
        self.name = name
        self.w = None
        self.r = {}
        self.sem = None
        self.cnt = 0


class KB:
    def __init__(self, nc, es):
        self.nc = nc
        self.es = es
        self.eng = {"pe": nc.tensor, "dve": nc.vector, "act": nc.scalar, "pool": nc.gpsimd, "sp": nc.sync}
        self.sem = {}
        self.cnt = {}
        self.nsem = 0
        for k in self.eng:
            self._newsem(k)
        self.waited = {k: {} for k in self.eng}
        self.nwait = 0
        self.nins = 0
        self.bufs = {}
        self.streams = {}
        self.semtot = {}
        self.store_names = set()

    def B(self, name):
        b = self.bufs.get(name)
        if b is None:
            b = self.bufs[name] = Buf(name)
        return b

    def _newsem(self, k):
        self.sem[k] = self.es.enter_context(self.nc.semaphore("s_%s_%d" % (k, self.nsem)))
        self.cnt[k] = 0
        self.nsem += 1

    def _need(self, e, ev, need):
        if ev is None:
            return
        sem, val, src = ev
        if src == "pe" and e == "pe":
            return
        if self.waited[e].get(id(sem), 0) >= val:
            return
        cur = need.get(id(sem))
        if cur is None or cur[1] < val:
            need[id(sem)] = (sem, val)

    def _deps(self, e, reads, writes):
        need = {}
        for b in reads:
            self._need(e, b.w, need)
        for b in writes:
            self._need(e, b.w, need)
            for ev in b.r.values():
                self._need(e, ev, need)
        return list(need.values())

    def _emit_waits(self, e, waits):
        for sem, val in waits:
            self.eng[e].wait_ge(sem, val)
            self.waited[e][id(sem)] = val
            self.nwait += 1

    def _commit(self, ev, reads, writes):
        for b in reads:
            b.r[id(ev[0])] = ev
        for b in writes:
            b.w = ev
            b.r = {}

    def op(self, e, fn, reads=(), writes=(), attach=None, lhs_reads=None, inc=True):
        if _STOP[0]:
            return None
        if attach is None:
            attach = (e != "pe")
        if e == "pe" and lhs_reads is not None:
            pre = self._deps(e, lhs_reads, ())
            self._emit_waits(e, pre)
            attach = True
        ex = [b for b in reads if b.name.startswith("ps")]
        if ex:
            writes = list(writes) + [b for b in ex if b not in writes]
        waits = self._deps(e, reads, writes)
        last = None
        if attach and waits:
            last = waits.pop()
        self._emit_waits(e, waits)
        ins = fn(self.eng[e])
        if last is not None:
            ins._wait_ge(last[0], last[1])
            self.waited[e][id(last[0])] = last[1]
        self.nins += 1
        if e == "pe" and not inc:
            ev = (self.sem[e], self.cnt[e] + 1, e)
            self._commit(ev, reads, writes)
            return ev
        if e != "pe" and self.cnt[e] >= SEM_LIMIT:
            self._newsem(e)
        self.cnt[e] += 1
        ev = (self.sem[e], self.cnt[e], e)
        ins.then_inc(self.sem[e], 1)
        self._commit(ev, reads, writes)
        return ev

    def dma(self, out, in_, reads=(), writes=(), e="sp", sembuf=None):
        if _STOP[0]:
            return None
        waits = self._deps(e, reads, writes)
        self._emit_waits(e, waits)
        b = sembuf if sembuf is not None else writes[0]
        if b.sem is None or b.cnt >= SEM_LIMIT:
            b.sem = self.es.enter_context(self.nc.semaphore("d_%d" % self.nsem))
            b.cnt = 0
            self.nsem += 1
        ins = self.eng[e].dma_start(out=out, in_=in_)
        b.cnt += 16
        ins.then_inc(b.sem, 16)
        self.nins += 1
        ev = (b.sem, b.cnt, "dma")
        self._commit(ev, reads, writes)
        return ev

    def pe_fence(self):
        if _STOP[0] or self.cnt["pe"] == 0:
            return
        self.eng["pe"].wait_ge(self.sem["pe"], self.cnt["pe"])
        self.nwait += 1

    def wait_all(self, e, bufs):
        if _STOP[0]:
            return
        need = {}
        for b in bufs:
            self._need(e, b.w, need)
            for ev in b.r.values():
                self._need(e, ev, need)
        self._emit_waits(e, list(need.values()))


def build(SEQ=4096, DEPTH=2):
    _STOP[0] = False
    NT = SEQ // 256
    NS = SEQ // 128
    nc = bass.Bass("TRN2", target_bir_lowering=False)

    def din(name, shape, dtype=F32):
        return nc.dram_tensor(name, list(shape), dtype, kind="ExternalInput").ap()

    x_in = din("x", [SEQ, D])
    cT_in = din("cT", [128, 8])
    pos_in = din("pos", [1, SEQ], I32)
    inv_in = din("rope_inv", [128, 1])
    w_mod = din("w_mod", [DEPTH, D, 3 * D])
    b_mod = din("b_mod", [DEPTH, 3 * D])
    w_in = din("w_in", [DEPTH, D, 3344])
    convw_in = din("conv_w", [DEPTH, 128, 8])
    convb_in = din("conv_b", [DEPTH, 128, 2])
    lruwa_in = din("lru_wa", [DEPTH, 4, 64, 64])
    lruba_in = din("lru_ba", [DEPTH, 128, 2])
    lruwx_in = din("lru_wx", [DEPTH, 4, 64, 64])
    lrubx_in = din("lru_bx", [DEPTH, 128, 2])
    lrulam_in = din("lru_lam", [DEPTH, 128, 2])
    glawr_in = din("gla_wr", [DEPTH, 16, 128])
    glabr_in = din("gla_br", [DEPTH, 128])
    glagn_in = din("gla_gn", [DEPTH, 64])
    w_out = din("w_out", [DEPTH, D, D])
    lng_in = din("ln_g", [DEPTH, D])
    lnb_in = din("ln_b", [DEPTH, D])
    out_d = nc.dram_tensor("out", [SEQ, D], F32, kind="ExternalOutput").ap()
    x1_d = nc.dram_tensor("x1_scr", [SEQ, D], F32).ap()
    cs_d = nc.dram_tensor("cs_scr", [NT, 128, 512], F32).ap()
    mod_d = nc.dram_tensor("mod_scr", [DEPTH, 1024], F32).ap()
    ut_d = nc.dram_tensor("ut_scr", [NT, 128, 2048], BF16).ap()

    es = ExitStack()
    with es:
        kb = KB(nc, es)
        kb.store_names = {"out", "x1_scr", "cs_scr", "mod_scr", "ut_scr"}
        B = kb.B

        def sbt(st, name, shape, dtype):
            return st.enter_context(nc.sbuf_tensor(name, list(shape), dtype))

        PS = [es.enter_context(nc.psum_tensor("ps%d" % i, [128, 512], F32)) for i in range(7)]
        PSB = es.enter_context(nc.psum_tensor("psb", [128, 1024], BF16))
        gp_state = [0]
        NGEN = 3

        POOLS = {"front": [0], "lru": [1], "gla0": [3, 4], "gla1": [5, 2], "out": [6], "p1f": [0, 1], "p1z": [2]}
        pool_state = {k: 0 for k in POOLS}

        def gp(pool=None):
            if pool is None:
                i = gp_state[0] % NGEN
                gp_state[0] += 1
            else:
                lst = POOLS[pool]
                i = lst[pool_state[pool] % len(lst)]
                pool_state[pool] += 1
            return PS[i], B("ps%d" % i)

        ST = [(PS[3], B("ps3")), (PS[4], B("ps4")), (PS[2], B("ps2"))]
        NZb = [(PS[5], B("ps5")), (PS[6], B("ps6"))]
        PB = (PSB, B("psb"))

        block = es.enter_context(nc.Block())

        def TT(e, out, in0, in1, op, R, W):
            return kb.op(e, lambda g: g.tensor_tensor(out=out, in0=in0, in1=in1, op=op), R, W)

        def TS(e, out, in0, s1, s2, op0, op1, R, W):
            if op1 is None:
                return kb.op(e, lambda g: g.tensor_scalar(out=out, in0=in0, scalar1=s1, scalar2=None, op0=op0), R, W)
            return kb.op(e, lambda g: g.tensor_scalar(out=out, in0=in0, scalar1=s1, scalar2=s2, op0=op0, op1=op1), R, W)

        def STT(out, in0, scalar, in1, op0, op1, R, W):
            return kb.op("dve", lambda g: g.scalar_tensor_tensor(out=out, in0=in0, scalar=scalar, in1=in1, op0=op0, op1=op1), R, W)

        def ACTF(out, in_, func, R, W, bias=None, scale=None):
            kw = {}
            if bias is not None:
                kw["bias"] = bias
            if scale is not None:
                kw["scale"] = scale
            return kb.op("act", lambda g: g.activation(out=out, in_=in_, func=func, **kw), R, W)

        def SIGM(dst, src, R, Wb, nbias=None):
            ACTF(dst, src, AF.Exp, R, [Wb], scale=-1.0, bias=nbias)
            ACTF(dst, dst, AF.Ln, [Wb], [Wb], bias=1.0)
            ACTF(dst, dst, AF.Exp, [Wb], [Wb], scale=-1.0)

        def CP(e, out, in_, R, W):
            if e == "act":
                return kb.op("act", lambda g: g.activation(out=out, in_=in_, func=AF.Copy), R, W)
            return kb.op(e, lambda g: g.tensor_copy(out=out, in_=in_), R, W)

        def MM(out, lhsT, rhs, start, stop, R, W, LR=None, inc=None):
            if inc is None:
                inc = bool(stop)
            return kb.op("pe", lambda g: g.matmul(out, lhsT=lhsT, rhs=rhs, start=start, stop=stop), R, W, lhs_reads=LR, inc=inc)

        def TR(out, in_, ident, R, W):
            return kb.op("pe", lambda g: g.transpose(out=out, in_=in_, identity=ident), R, W)

        def MEMSET(e, ap, val, W):
            return kb.op(e, lambda g: g.memset(ap, val), (), W)

        def ASEL(out, in_, pattern, cmp, fill, base, cm, R, W):
            return kb.op("pool", lambda g: g.affine_select(out=out, in_=in_, pattern=pattern, compare_op=cmp,
                                                            fill=fill, base=base, channel_multiplier=cm), R, W)

        cst = es
        ident_f = sbt(cst, "ident_f", [128, 128], F32)
        ident_b = sbt(cst, "ident_b", [128, 128], BF16)
        rperm = sbt(cst, "rperm", [128, 128], BF16)
        tri = sbt(cst, "tri", [128, 128], F32)
        tri16 = sbt(cst, "tri16", [128, 128], F32)
        ones16 = sbt(cst, "ones16", [128, 128], F32)
        onesrow = sbt(cst, "onesrow", [1, 128], F32)
        cm = sbt(cst, "cm", [128, 2, 256], BF16)
        tb = sbt(cst, "tb", [128, 2432], BF16)
        ind = sbt(cst, "ind", [128, 16, 128], BF16)
        selA = sbt(cst, "selA", [128, 128], F32)
        selB = sbt(cst, "selB", [128, 128], F32)
        hm = sbt(cst, "hm", [128, 4, 128], BF16)
        bmall = sbt(cst, "bmall", [128, 16, 16], F32)
        inv = sbt(cst, "inv", [128, 1], F32)
        sgn = sbt(cst, "sgn", [128, 1], F32)
        cB = sbt(cst, "cB", [128, 8, 128], F32)
        Bc = B("consts")

        MEMSET("pool", ident_f[:], 1.0, [Bc])
        ASEL(ident_f[:], ident_f[:], [[-1, 128]], ALU.is_equal, 0.0, 0, 1, [Bc], [Bc])
        CP("dve", ident_b[:], ident_f[:], [Bc], [Bc])
        for blk, src in ((0, 1), (1, 0), (2, 3), (3, 2)):
            CP("dve", rperm[:, blk * 32:(blk + 1) * 32], ident_b[:, src * 32:(src + 1) * 32], [Bc], [Bc])
        MEMSET("pool", tri[:], 1.0, [Bc])
        ASEL(tri[:], tri[:], [[1, 128]], ALU.is_ge, 0.0, 0, -1, [Bc], [Bc])
        TS("dve", tri16[:], tri[:], -1.0 / 16.0, None, ALU.mult, None, [Bc], [Bc])
        MEMSET("pool", ones16[:], -1.0 / 16.0, [Bc])
        MEMSET("pool", onesrow[:], 1.0, [Bc])
        MEMSET("pool", selA[:], 0.0, [Bc])
        MEMSET("pool", selA[64:65, 0:64], 1.0, [Bc])
        MEMSET("pool", selB[:], 1.0, [Bc])
        ASEL(selB[:], selB[:], [[0, 128]], ALU.is_equal, 0.0, -63, 1, [Bc], [Bc])
        MEMSET("pool", selB[:, 0:64], 0.0, [Bc])
        MEMSET("pool", ind[:], 1.0, [Bc])
        ASEL(ind[:], ind[:], [[-1, 16], [0, 128]], ALU.is_equal, 0.0, 0, 1, [Bc], [Bc])
        MEMSET("pool", hm[:], 1.0, [Bc])
        ASEL(hm[:], hm[:], [[-32, 4], [0, 128]], ALU.is_ge, 0.0, 0, 1, [Bc], [Bc])
        ASEL(hm[:], hm[:], [[32, 4], [0, 128]], ALU.is_ge, 0.0, 31, -1, [Bc], [Bc])
        MEMSET("pool", bmall[:], 0.0, [Bc])
        ASEL(bmall[:], bmall[:], [[1, 16], [-1, 16]], ALU.is_ge, NEGM, -1, 0, [Bc], [Bc])
        with ExitStack() as tmp:
            zer = sbt(tmp, "zer", [128, 256], F32)
            pidx = sbt(tmp, "pidx", [128, 1], I32)
            pf = sbt(tmp, "pf", [128, 1], F32)
            cT = sbt(tmp, "cT_sb", [128, 8], F32)
            Bt = B("ctmp")
            MEMSET("pool", zer[:], 0.0, [Bt])
            for c in range(2):
                ASEL(cm[:, c, :], zer[:], [[1, 256]], ALU.is_ge, NEGM, -128 * c, -1, [Bt], [Bc])
            TBW = 2432
            with ExitStack() as t2s:
                di = sbt(t2s, "di2", [128, TBW], I32)
                dfl = sbt(t2s, "dfl2", [128, TBW], F32)
                ge0 = sbt(t2s, "ge02", [128, TBW], F32)
                ca = sbt(t2s, "ca2", [128, TBW], F32)
                cb_ = sbt(t2s, "cb2", [128, TBW], F32)
                mi = sbt(t2s, "mi2", [128, TBW], I32)
                msum = sbt(t2s, "msum2", [128, TBW], F32)
                kb.op("pool", lambda g: g.iota(di[:], [[1, TBW]], base=-128, channel_multiplier=-1), (), [Bt])
                CP("dve", dfl[:], di[:], [Bt], [Bt])
                TS("dve", ge0[:], dfl[:], 0.0, None, ALU.is_ge, None, [Bt], [Bt])
                TS("dve", ca[:], dfl[:], 128.0, None, ALU.is_le, None, [Bt], [Bt])
                TT("dve", msum[:], ca[:], ge0[:], ALU.mult, [Bt], [Bt])
                for msk, lim in ((3, 512.0), (15, 2048.0)):
                    TS("dve", mi[:], di[:], msk, None, ALU.bitwise_and, None, [Bt], [Bt])
                    CP("dve", ca[:], mi[:], [Bt], [Bt])
                    TS("dve", ca[:], ca[:], 0.0, None, ALU.is_equal, None, [Bt], [Bt])
                    TS("dve", cb_[:], dfl[:], lim, None, ALU.is_le, None, [Bt], [Bt])
                    TT("dve", cb_[:], cb_[:], ge0[:], ALU.mult, [Bt], [Bt])
                    TT("dve", ca[:], ca[:], cb_[:], ALU.mult, [Bt], [Bt])
                    TT("dve", msum[:], msum[:], ca[:], ALU.add, [Bt], [Bt])
                TS("dve", ca[:], msum[:], 0.0, NEGM, ALU.is_equal, ALU.mult, [Bt], [Bt])
                TS("dve", cb_[:], msum[:], 1.0, None, ALU.max, None, [Bt], [Bt])
                ACTF(cb_[:], cb_[:], AF.Ln, [Bt], [Bt])
                STT(tb[:], cb_[:], 8.0, ca[:], ALU.mult, ALU.add, [Bt], [Bc])
                for e in ("dve", "act", "pool"):
                    kb.wait_all(e, [Bt, Bc])
            kb.dma(inv[:], inv_in[:, :], (), [Bc])
            kb.op("pool", lambda g: g.iota(pidx[:], [[0, 1]], base=0, channel_multiplier=1), (), [Bt])
            TS("dve", pidx[:], pidx[:], 63, None, ALU.bitwise_and, None, [Bt], [Bt])
            CP("dve", pf[:], pidx[:], [Bt], [Bt])
            TS("dve", pf[:], pf[:], 32.0, None, ALU.is_lt, None, [Bt], [Bt])
            TS("dve", sgn[:], pf[:], -2.0, 1.0, ALU.mult, ALU.add, [Bt], [Bc])
            kb.dma(cT[:], cT_in[:, :], (), [Bt])
            for kc in range(8):
                CP("dve", cB[:, kc, :], cT[:, kc:kc + 1].broadcast_to([128, 128]), [Bt], [Bc])
            posi = sbt(tmp, "posi", [128, SEQ], I32)
            ang = sbt(tmp, "ang", [128, SEQ], F32)
            a2 = sbt(tmp, "a2", [128, SEQ], F32)
            tqc = sbt(tmp, "tqc", [128, SEQ], F32)
            tqs = sbt(tmp, "tqs", [128, SEQ], F32)
            rc = sbt(tmp, "rc", [128, SEQ], F32)
            rs_ = sbt(tmp, "rs_", [128, SEQ], F32)
            MAGIC = 12582912.0
            C1 = 6.28125
            C2 = 2.0 * math.pi - C1
            Bp, Bw, Bw2 = B("posi"), B("ropew"), B("ropew2")
            kb.dma(posi[:], pos_in[0:1, :].partition_broadcast(128), (), [Bp])
            CP("dve", ang[:], posi[:], [Bp], [Bw])
            TS("dve", ang[:], ang[:], inv[:, 0:1], None, ALU.mult, None, [Bw, Bc], [Bw])
            TS("dve", a2[:], ang[:], math.pi / 2.0, None, ALU.add, None, [Bw], [B("ra2")])
            TS("dve", tqc[:], a2[:], 1.0 / (2.0 * math.pi), MAGIC, ALU.mult, ALU.add, [B("ra2")], [B("rtqc")])
            TS("dve", tqc[:], tqc[:], -MAGIC, None, ALU.add, None, [B("rtqc")], [B("rtqc")])
            STT(rc[:], tqc[:], -C1, a2[:], ALU.mult, ALU.add, [B("rtqc"), B("ra2")], [B("rrc")])
            STT(rc[:], tqc[:], -C2, rc[:], ALU.mult, ALU.add, [B("rtqc"), B("rrc")], [B("rrc")])
            TS("dve", rc[:], rc[:], -3.1415925, 3.1415925, ALU.max, ALU.min, [B("rrc")], [B("rrc")])
            ACTF(rc[:], rc[:], AF.Sin, [B("rrc")], [B("rrc")])
            TS("dve", tqs[:], ang[:], 1.0 / (2.0 * math.pi), MAGIC, ALU.mult, ALU.add, [Bw], [B("rtqs")])
            TS("dve", tqs[:], tqs[:], -MAGIC, None, ALU.add, None, [B("rtqs")], [B("rtqs")])
            STT(rs_[:], tqs[:], -C1, ang[:], ALU.mult, ALU.add, [B("rtqs"), Bw], [B("rrs")])
            STT(rs_[:], tqs[:], -C2, rs_[:], ALU.mult, ALU.add, [B("rtqs"), B("rrs")], [B("rrs")])
            TS("dve", rs_[:], rs_[:], -3.1415925, 3.1415925, ALU.max, ALU.min, [B("rrs")], [B("rrs")])
            ACTF(rs_[:], rs_[:], AF.Sin, [B("rrs")], [B("rrs")])
            TS("dve", rs_[:], rs_[:], sgn[:, 0:1], None, ALU.mult, None, [B("rrs"), Bc], [B("rrs")])
            csv = cs_d.rearrange("i p c -> p i c")
            kb.dma(csv[:, :, 0:256], rc[:].rearrange("p (i q) -> p i q", q=256), [B("rrc")], [B("csd0")], sembuf=B("rrc"))
            kb.dma(csv[:, :, 256:512], rs_[:].rearrange("p (i q) -> p i q", q=256), [B("rrs")], [B("csd1")], sembuf=B("rrs"))
            for i in range(2, NT):
                B("csd%d" % i)
            kb.wait_all("sp", [B("csd%d" % i) for i in range(NT)])
            for e in ("pe", "dve", "act", "pool", "sp"):
                kb.wait_all(e, [Bt, Bw, Bp, Bc, B("ra2"), B("rtqc"), B("rrc"), B("rtqs"), B("rrs"), B("csd0"), B("csd1")])

        scale1 = sbt(cst, "scale1", [128, 8], F32)
        shiftc = sbt(cst, "shiftc", [128, 8], F32)

        COLS = dict(a_q=0, a_k=256, a_v=512, a_g=768, b_x=1024, b_g=1280, c_q=1536, c_k=1792, c_v=2048,
                    c_g=2304, d_q=2560, d_k=2688, d_v=2816, d_g=3072, d_r=3328)

        def ln_stats(xrow, tagR, st, mv, sd, rstd, nmr, Bst):
            kb.op("dve", lambda g: g.bn_stats(out=st[:, 0:6], in_=xrow[:, 0:512]), tagR, [Bst])
            kb.op("dve", lambda g: g.bn_stats(out=st[:, 6:12], in_=xrow[:, 512:1024]), tagR, [Bst])
            kb.op("dve", lambda g: g.bn_aggr(out=mv[:, 0:2], in_=st[:, 0:12]), [Bst], [Bst])
            TS("dve", sd[:, 0:1], mv[:, 1:2], LN_EPS, None, ALU.add, None, [Bst], [Bst])
            ACTF(sd[:, 0:1], sd[:, 0:1], AF.Ln, [Bst], [Bst])
            ACTF(rstd[:, 0:1], sd[:, 0:1], AF.Exp, [Bst], [Bst], scale=-0.5)
            TS("dve", nmr[:, 0:1], mv[:, 0:1], rstd[:, 0:1], -1.0, ALU.mult, ALU.mult, [Bst], [Bst])

        ckpt("c")
        for l in range(DEPTH):
            xsrc = x_in if l == 0 else x1_d
            xdst = out_d if l == DEPTH - 1 else x1_d
            src_is_scr = l > 0
            L = "L%d_" % l
            with ExitStack() as lay:
                mixAC = sbt(lay, L + "mixAC", [128, 4, SEQ], BF16)

                with ExitStack() as p1:
                    W1 = sbt(p1, L + "W1", [128, 8, 2048], BF16)
                    BW1 = B(L + "W1")
                    with ExitStack() as ms:
                        wm = [sbt(ms, L + "wm%d" % k, [128, 8, 512], F32) for k in range(2)]
                        modB = sbt(ms, L + "modB", [128, 3072], F32)
                        bmodB = sbt(ms, L + "bmodB", [128, 3072], F32)
                        dtmp = sbt(ms, L + "dtmp", [128, 8, 128], F32)
                        stage = [sbt(ms, L + "stg%d" % k, [128, 2048], F32) for k in range(4)]
                        Bm = B("modB")

                        def mod_chain():
                            kb.dma(bmodB[:], b_mod[l:l + 1, :].partition_broadcast(128), (), [B("bmodB")])
                            wmv = w_mod[l].rearrange("(kc p) n -> p kc n", p=128)
                            for g in range(6):
                                slot = g % 2
                                Bw_ = B("wm%d" % slot)
                                kb.dma(wm[slot][:], wmv[:, :, g * 512:(g + 1) * 512], (), [Bw_])
                                yield
                                bank, Bb = gp("p1f")
                                for kc in range(8):
                                    MM(bank[:, 0:512], cB[:, kc, :], wm[slot][:, kc, :], kc == 0, kc == 7, [Bc, Bw_], [Bb])
                                    yield
                                TT("dve", modB[:, g * 512:(g + 1) * 512], bank[:, 0:512], bmodB[:, g * 512:(g + 1) * 512],
                                   ALU.add, [Bb, B("bmodB")], [Bm])
                                yield
                            for (dst, off, add1) in ((shiftc, 0, 0.0), (scale1, 1024, 1.0)):
                                TT("dve", dtmp[:], modB[:, off:off + 1024].rearrange("p (k n) -> p k n", k=8),
                                   ident_f[:].unsqueeze(1).broadcast_to([128, 8, 128]), ALU.mult, [Bm, Bc], [B(L + "dtmp")])
                                kb.op("dve", lambda g: g.tensor_reduce(out=dst[:, 0:8], in_=dtmp[:], axis=AX.X, op=ALU.add),
                                      [B(L + "dtmp")], [B("modcols")])
                                if add1:
                                    TS("dve", dst[:, 0:8], dst[:, 0:8], 1.0, None, ALU.add, None, [B("modcols")], [B("modcols")])
                                yield
                            kb.dma(mod_d[l:l + 1, :], modB[0:1, 2048:3072], [Bm], [B("modd")], sembuf=Bm)
                            yield

                        def w1_chain():
                            srcblk = [0, 1, 4, 5, 3, 7, 2, 6]
                            for kc in range(8):
                                slot = kc % 4
                                Bs = B("stg%d" % slot)
                                kb.dma(stage[slot][:, 0:1024], w_in[l, kc * 128:(kc + 1) * 128, 0:1024], (), [Bs])
                                kb.dma(stage[slot][:, 1024:2048], w_in[l, kc * 128:(kc + 1) * 128, 1536:2560], (), [Bs])
                                yield
                                for j in range(8):
                                    eng = ("pool", "act", "dve")[j % 3]
                                    sb_ = srcblk[j]
                                    CP(eng, W1[:, kc, j * 256:(j + 1) * 256], stage[slot][:, sb_ * 256:(sb_ + 1) * 256], [Bs], [BW1])
                                    yield

                        chains_ = [w1_chain(), mod_chain()]
                        while chains_:
                            for g_ in list(chains_):
                                try:
                                    next(g_)
                                except StopIteration:
                                    chains_.remove(g_)
                        for e in ("pe", "dve", "pool", "act", "sp"):
                            kb.wait_all(e, [Bm, B("bmodB"), B("wm0"), B("wm1"), B(L + "dtmp"), B("modd"), B("stg0"), B("stg1"), B("stg2"), B("stg3")])

                    ckpt("w")
                    KT = [sbt(p1, L + "KT%d" % m, [128, 2, SEQ], BF16) for m in range(2)]
                    VC = [sbt(p1, L + "VC%d" % m, [128, NS, 2, 129], BF16) for m in range(2)]
                    for m in range(2):
                        MEMSET("pool", VC[m][:, :, :, 64:65], 1.0, [B(L + "Vones%d" % m)])
                    xs = [sbt(p1, L + "xs%d" % k, [128, 1024], F32) for k in range(2)]
                    st_ = sbt(p1, L + "st", [128, 12], F32)
                    mv_ = sbt(p1, L + "mv", [128, 2], F32)
                    sd_ = sbt(p1, L + "sd", [128, 1], F32)
                    rstd_ = sbt(p1, L + "rstd", [128, 1], F32)
                    nmr_ = sbt(p1, L + "nmr", [128, 1], F32)
                    uT = [sbt(p1, L + "uT%d" % k, [128, 8, 256], BF16) for k in range(2)]
                    cs = [sbt(p1, L + "cs%d" % k, [128, 512], F32) for k in range(2)]
                    qb = [sbt(p1, L + "qb%d" % k, [128, 256], BF16) for k in range(2)]
                    t1 = [sbt(p1, L + "t1%d" % k, [128, 256], F32) for k in range(2)]
                    t2 = [sbt(p1, L + "t2%d" % k, [128, 256], F32) for k in range(2)]
                    QT = [[sbt(p1, L + "QT%d_%d" % (m, k), [128, 4, 256], BF16) for k in range(2)] for m in range(2)]
                    for m in range(2):
                        for k in range(2):
                            MEMSET("pool", QT[m][k][:], 0.0, [B(L + "QT%d_%d" % (m, k))])
                    sg = [sbt(p1, L + "sg%d" % k, [128, 4, 256], F32) for k in range(2)]
                    ksb = sbt(p1, L + "ksb", [128, 2, 16], BF16)
                    ksf = sbt(p1, L + "ksf", [128, 2, 16], F32)
                    gm = sbt(p1, L + "gm", [128, 8, 16], F32)
                    t8 = sbt(p1, L + "t8", [128, 8, 8], F32)
                    mbf = sbt(p1, L + "mbf", [128, 8, 16], F32)
                    mbias = sbt(p1, L + "mbias", [128, 8, 16], BF16)
                    MBT = [sbt(p1, L + "MBT%d" % k, [128, 4, 256], BF16) for k in range(2)]
                    for k in range(2):
                        MEMSET("pool", MBT[k][:], 0.0, [B(L + "MBT%d" % k)])
                    PT = [sbt(p1, L + "PT%d" % k, [128, 512], BF16) for k in range(4)]
                    nzs = [sbt(p1, L + "nzs%d" % k, [128, 256], F32) for k in range(2)]
                    for k in range(2):
                        MEMSET("pool", nzs[k][:], 0.0, [B(L + "nzs%d" % k)])
                    rz = [sbt(p1, L + "rz%d" % k, [128, 256], F32) for k in range(2)]
                    MEMSET("pool", ksb[:], 0.0, [B(L + "ksb")])
                    pt_state = [0]
                    st_state = [0]
                    hp_state = [0]
                    rope_state = [0]
                    Bst = B(L + "lnst")

                    def load_ln_transpose(i, s, uTt, BuT):
                        gsub = 2 * i + s
                        xt = xs[gsub % 2]
                        Bx = B("xs%d" % (gsub % 2))
                        r0 = gsub * 128
                        rd = [B("x1t%d" % (gsub // 2))] if src_is_scr else []
                        kb.dma(xt[:], xsrc[r0:r0 + 128, :], rd, [Bx])
                        ln_stats(xt, [Bx], st_, mv_, sd_, rstd_, nmr_, Bst)
                        ACTF(xt[:], xt[:], AF.Identity, [Bx, Bst], [Bx], bias=nmr_[:, 0:1], scale=rstd_[:, 0:1])
                        for g in range(2):
                            bank, Bb = gp("p1f")
                            for k4 in range(4):
                                kc = 4 * g + k4
                                TR(bank[:, k4 * 128:(k4 + 1) * 128], xt[:, kc * 128:(kc + 1) * 128], ident_f[:], [Bx, Bc], [Bb])
                            for k4 in range(4):
                                kc = 4 * g + k4
                                if k4 % 2 == 0:
                                    TS("dve", uTt[:, kc, s * 128:(s + 1) * 128], bank[:, k4 * 128:(k4 + 1) * 128],
                                       scale1[:, kc:kc + 1], shiftc[:, kc:kc + 1], ALU.mult, ALU.add,
                                       [Bb, B("modcols")], [BuT])
                                else:
                                    ACTF(uTt[:, kc, s * 128:(s + 1) * 128], bank[:, k4 * 128:(k4 + 1) * 128], AF.Identity,
                                         [Bb, B("modcols")], [BuT], bias=shiftc[:, kc:kc + 1], scale=scale1[:, kc:kc + 1])

                    def p1_front(i):
                        t0 = 256 * i
                        sl = i % 2
                        uTt, BuT = uT[sl], B(L + "uT%d" % sl)
                        cst_, Bcs = cs[sl], B("cs%d" % sl)
                        kb.dma(cst_[:], cs_d[i], [B("csd0"), B("csd1")], [Bcs])
                        yield
                        for s in range(2):
                            load_ln_transpose(i, s, uTt, BuT)
                            yield
                        kb.dma(ut_d[i], uTt[:].rearrange("p k t -> p (k t)"), [BuT], [B("utd%d" % i)], sembuf=BuT)
                        yield
                        ckpt("ln")
                        sgt, Bsg = sg[sl], B(L + "sg%d" % sl)
                        for gi in range(12):
                            bank, Bb = gp("p1f")
                            for kc in range(8):
                                MM(bank[:, 0:256], W1[:, kc, gi * 128:(gi + 1) * 128], uTt[:, kc, :], kc == 0, kc == 7,
                                   [BW1, BuT], [Bb])
                                yield
                            if gi >= 8:
                                if "silu" not in _KSKIP:
                                    SIGM(sgt[:, gi - 8, :], bank[:, 0:256], [Bb], Bsg)
                                    TT("dve", sgt[:, gi - 8, :], bank[:, 0:256], sgt[:, gi - 8, :], ALU.mult, [Bb, Bsg], [Bsg])
                                    yield
                                continue
                            if "rope" in _KSKIP:
                                continue
                            m = gi // 4
                            isk = (gi // 2) % 2
                            ct = gi % 2
                            if isk:
                                dest = KT[m][:, ct, t0:t0 + 256]
                                Bd = B(L + "KT%d_%d_%d" % (m, ct, i))
                            else:
                                dest = None
                                Bd = B(L + "QT%d_%d" % (m, sl))
                            r_ = rope_state[0] % 2
                            rope_state[0] += 1
                            Bq, Bt1, Bt2 = B(L + "qb%d" % r_), B(L + "t1%d" % r_), B(L + "t2%d" % r_)
                            CP("act", qb[r_][:], bank[:, 0:256], [Bb], [Bq])
                            yield
                            bank2, Bb2 = gp("p1f")
                            MM(bank2[:, 0:256], rperm[:], qb[r_][:], True, True, [Bc, Bq], [Bb2])
                            yield
                            if "ropett" in _KSKIP:
                                continue
                            if "nocs" in _KSKIP:
                                TT("dve", t1[r_][:], bank[:, 0:256], sgt[:, 0, :], ALU.mult, [Bb], [Bt1])
                                yield
                                TT("dve", t2[r_][:], bank2[:, 0:256], sgt[:, 1, :], ALU.mult, [Bb2], [Bt2])
                                yield
                            else:
                                TT("dve", t1[r_][:], bank[:, 0:256], cst_[:, 0:256], ALU.mult, [Bb, Bcs], [Bt1])
                                yield
                                TT("dve", t2[r_][:], bank2[:, 0:256], cst_[:, 256:512], ALU.mult, [Bb2, Bcs], [Bt2])
                                yield
                            if "ropepool" in _KSKIP:
                                continue
                            if dest is not None:
                                TT("pool", dest, t1[r_][:], t2[r_][:], ALU.add, [Bt1, Bt2], [Bd])
                                yield
                            else:
                                for hb in range(2):
                                    rs = slice(64 * hb, 64 * hb + 64)
                                    TT("pool", QT[m][sl][rs, 2 * ct + hb, :], t1[r_][rs, :], t2[r_][rs, :], ALU.add,
                                       [Bt1, Bt2], [Bd])
                                    yield
                        ckpt("qk")
                        for s in range(2):
                            gsub = 2 * i + s
                            bank, Bb = gp("p1f")
                            for kc in range(8):
                                MM(bank[:, 0:512], uTt[:, kc, s * 128:(s + 1) * 128], W1[:, kc, 1536:2048], kc == 0, kc == 7,
                                   [BW1, BuT], [Bb])
                                yield
                            for m in range(2):
                                Bv = B(L + "V%d_%d" % (m, i))
                                src = bank[:, m * 256:(m + 1) * 256].rearrange("p (a h d) -> p a h d", a=2, h=2)
                                CP("act" if m == 0 else "dve", VC[m][:, gsub, :, 0:64], src[:, :, 0, :],
                                   [Bb, B(L + "Vones%d" % m)], [Bv])
                                yield
                                CP("dve" if m == 0 else "act", VC[m][:, gsub, :, 65:129], src[:, :, 1, :],
                                   [Bb, B(L + "Vones%d" % m)], [Bv])
                                yield
                        ckpt("v")
                        for ct in range(2):
                            kb.op("dve", lambda g: g.tensor_reduce(out=ksf[:, ct, i:i + 1], in_=KT[0][:, ct, t0:t0 + 256],
                                                                    axis=AX.X, op=ALU.add),
                                  [B(L + "KT0_%d_%d" % (ct, i))], [B(L + "ksf")])
                            yield
                        CP("dve", ksb[:, :, i:i + 1], ksf[:, :, i:i + 1], [B(L + "ksf")], [B(L + "ksb")])
                        yield
                        ckpt("ks%d" % i)
                        MBTt, BMB = MBT[sl], B(L + "MBT%d" % sl)
                        BQA = B(L + "QT0_%d" % sl)
                        if i >= 1:
                            bank, Bb = gp("p1f")
                            for h in range(4):
                                ct = h // 2
                                for s in range(2):
                                    g8 = h * 2 + s
                                    MM(bank[:, g8 * 16:(g8 + 1) * 16], QT[0][sl][:, h, s * 128:(s + 1) * 128],
                                       ksb[:, ct, 0:16], True, True, [BQA, B(L + "ksb")], [Bb])
                                    yield
                            ckpt("gmm%d" % i)
                            Bg = B(L + "gate")
                            TT("dve", gm[:], bank[:, 0:128].rearrange("p (g n) -> p g n", g=8),
                               bmall[:, i, :].unsqueeze(1).broadcast_to([128, 8, 16]), ALU.add, [Bb, Bc], [Bg])
                            yield
                            for g8 in range(8):
                                kb.op("dve", lambda g: g.max(out=t8[:, g8, :], in_=gm[:, g8, :]), [Bg], [Bg])
                                yield
                            TT("dve", mbf[:], gm[:], t8[:, :, 2:3].broadcast_to([128, 8, 16]), ALU.is_ge, [Bg], [Bg])
                            yield
                            TS("dve", mbf[:], mbf[:], -1.0, -NEGM, ALU.add, ALU.mult, [Bg], [Bg])
                            yield
                            TT("dve", mbias[:], mbf[:], bmall[:, i, :].unsqueeze(1).broadcast_to([128, 8, 16]), ALU.add,
                               [Bg, Bc], [Bg])
                            yield
                            ckpt("gtop%d" % i)
                            pb, Bpb = PB
                            for g8 in range(8):
                                TR(pb[0:16, g8 * 128:(g8 + 1) * 128], mbias[:, g8, :], ident_b[:], [Bg, Bc], [Bpb])
                                yield
                            CP("act", MBTt[0:16].rearrange("p h q -> p (h q)"), pb[0:16, 0:1024], [Bpb], [BMB])
                            yield

                        if i == 1:
                            ckpt("g")

                    def p1_att(i):
                        t0 = 256 * i
                        sl = i % 2
                        sgt, Bsg = sg[sl], B(L + "sg%d" % sl)
                        MBTt, BMB = MBT[sl], B(L + "MBT%d" % sl)
                        items = [(m, h) for m in range(2) for h in range(4)]
                        work = []
                        for k_, (m, h) in enumerate(items):
                            if m == 0:
                                kts = list(range(0, 2 * i + 2))
                            else:
                                kts = list(range(max(0, 2 * i - 16), 2 * i + 2))
                            ng = (len(kts) + 1) // 2
                            for gi_ in range(ng):
                                work.append(dict(k=k_, m=m, h=h, grp=kts[2 * gi_:2 * gi_ + 2], base=2 * gi_, nk=len(kts),
                                                 last=(gi_ == ng - 1), idx=len(work)))

                        def emitS(w):
                            m, h = w["m"], w["h"]
                            hr = 64 * (h % 2)
                            ct = h // 2
                            rows = slice(hr, hr + 64)
                            BQ = B(L + "QT%d_%d" % (m, sl))
                            st, Bst_ = ST[w["idx"] % 3]
                            kt0 = w["grp"][0]
                            ti = kt0 // 2
                            if m == 0:
                                w["ord"] = list(w["grp"])
                                if ti < i:
                                    MM(st[:, 0:512], ind[:, ti, :], MBTt[:, h, :].unsqueeze(1).broadcast_to([128, 2, 256]), True, False,
                                       [Bc, BMB], [Bst_], LR=[Bc])
                                else:
                                    MM(st[:, 0:512], ident_b[:], cm[:].rearrange("p c q -> p (c q)"), True, False, [Bc], [Bst_], LR=[Bc])
                            else:
                                w["ord"] = [kt0 + 1, kt0]
                                d1_ = 2 * i - (kt0 + 1)
                                off = 128 * (d1_ + 1)
                                rhs_ap = bass.AP(tb, off, [[2432, 128], [128, 2], [1, 256]])
                                MM(st[:, 0:512], ident_b[:], rhs_ap, True, False, [Bc], [Bst_], LR=[Bc])
                            for c, kt in enumerate(w["ord"]):
                                ti = kt // 2
                                MM(st[:, c * 256:(c + 1) * 256], KT[m][:, ct, kt * 128:(kt + 1) * 128],
                                   QT[m][sl][:, h, :], False, c == 1, [B(L + "KT%d_%d_%d" % (m, ct, ti)), BQ], [Bst_],
                                   LR=[B(L + "KT%d_%d_%d" % (m, ct, ti))])

                        def emitE(w):
                            st, Bst_ = ST[w["idx"] % 3]
                            pslot = w["idx"] % 4
                            wdt = 256 * len(w["grp"])
                            ACTF(PT[pslot][:, 0:wdt], st[:, 0:wdt], AF.Exp, [Bst_], [B(L + "PT%d" % pslot)], scale=0.125)

                        def emitPV(w):
                            m, h = w["m"], w["h"]
                            odd = h % 2
                            M = 128 if odd else 65
                            win = slice(1, 129) if odd else slice(0, 65)
                            pslot = w["idx"] % 4
                            nz, Bnz = NZb[w["k"] % 2]
                            for c, kt in enumerate(w["ord"]):
                                idx_ = w["base"] + c
                                MM(nz[0:M, 0:256], VC[m][:, kt, h // 2, win], PT[pslot][:, c * 256:(c + 1) * 256],
                                   idx_ == 0, idx_ == w["nk"] - 1, [B(L + "V%d_%d" % (m, kt // 2)), B(L + "PT%d" % pslot)], [Bnz],
                                   LR=[B(L + "V%d_%d" % (m, kt // 2))], inc=(c == len(w["ord"]) - 1))

                        def fin1(w):
                            odd = w["h"] % 2
                            M = 128 if odd else 65
                            nz, Bnz = NZb[w["k"] % 2]
                            hp = w["k"] % 2
                            CP("act", nzs[hp][0:M, :], nz[0:M, 0:256], [Bnz], [B(L + "nzs%d" % hp)])

                        def fin2(w):
                            m, h = w["m"], w["h"]
                            odd = h % 2
                            hr = 64 * odd
                            ct = h // 2
                            rows = slice(hr, hr + 64)
                            hp = w["k"] % 2
                            Bn, Br = B(L + "nzs%d" % hp), B(L + "rz%d" % hp)
                            zb, Bzb = NZb[w["k"] % 2]
                            if odd:
                                MM(zb[:, 256:512], selB[:, :], nzs[hp][:, :], True, True, [Bc, Bn], [Bzb])
                            else:
                                MM(zb[0:64, 256:512], selA[:, 0:64], nzs[hp][:, :], True, True, [Bc, Bn], [Bzb])
                            kb.op("dve", lambda g: g.reciprocal(out=rz[hp][rows, :], in_=zb[rows, 256:512]), [Bzb], [Br])
                            gidx = 2 * m + ct
                            TT("pool", nzs[hp][rows, :], nzs[hp][rows, :], sgt[rows, gidx, :], ALU.mult, [Bn, Bsg], [Bn])
                            TT("dve", mixAC[rows, gidx, t0:t0 + 256], nzs[hp][rows, :], rz[hp][rows, :], ALU.mult,
                               [Bn, Br], [B(L + "mix%d_%d" % (gidx, i))])

                        deferred = []
                        emitS(work[0])
                        if len(work) > 1:
                            emitS(work[1])
                        for idx, w in enumerate(work):
                            emitE(w)
                            if idx + 2 < len(work):
                                emitS(work[idx + 2])
                            emitPV(w)
                            if w["last"]:
                                fin1(w)
                                deferred.append((idx + 1, w))
                            while deferred and deferred[0][0] <= idx:
                                fin2(deferred.pop(0)[1])
                            yield
                        while deferred:
                            fin2(deferred.pop(0)[1])

                    P1LEN = {}

                    def run_chains1(gens, names=None):
                        gens = list(gens)
                        names = list(names) if names else [None] * len(gens)
                        cnt = [0] * len(gens)
                        alive = [True] * len(gens)
                        tot = [float(P1LEN.get(n, 100)) for n in names]
                        while any(alive):
                            j = min((k for k in range(len(gens)) if alive[k]), key=lambda k: cnt[k] / tot[k])
                            try:
                                next(gens[j])
                                cnt[j] += 1
                            except StopIteration:
                                alive[j] = False
                                if names[j] is not None:
                                    P1LEN[names[j]] = max(cnt[j], 1)

                    run_chains1([p1_front(0)], ["front"])
                    for i in range(NT):
                        gens = [p1_att(i)]
                        nms = [None]
                        P1LEN["att"] = 4 * (i + 1) + 4 * min(i + 1, 9) + 1
                        nms = ["att"]
                        if i + 1 < NT:
                            gens.append(p1_front(i + 1))
                            nms.append("front")
                        run_chains1(gens, nms)
                    for e in ("pe", "dve", "act", "pool", "sp"):
                        kb.wait_all(e, list(kb.bufs.values()))

                ckpt("p1")
                with ExitStack() as p2:
                    W2 = sbt(p2, L + "W2", [128, 8, 1296], BF16)
                    Wo = sbt(p2, L + "Wo", [128, 8, 1024], BF16)
                    lnG = sbt(p2, L + "lnG", [128, 1024], F32)
                    lnBt = sbt(p2, L + "lnB", [128, 1024], F32)
                    cw = sbt(p2, L + "cw", [128, 8], F32)
                    cbv = sbt(p2, L + "cbv", [128, 2], F32)
                    bav = sbt(p2, L + "bav", [128, 2], F32)
                    bxv = sbt(p2, L + "bxv", [128, 2], F32)
                    lamv = sbt(p2, L + "lamv", [128, 2], F32)
                    n8sp = sbt(p2, L + "n8sp", [128, 2], F32)
                    nbav = sbt(p2, L + "nbav", [128, 2], F32)
                    nbxv = sbt(p2, L + "nbxv", [128, 2], F32)
                    WaBD = sbt(p2, L + "WaBD", [128, 2, 128], BF16)
                    WxBD = sbt(p2, L + "WxBD", [128, 2, 128], BF16)
                    wr_f = sbt(p2, L + "wr_f", [128, 128], F32)
                    br_f = sbt(p2, L + "br_f", [1, 128], F32)
                    gnB = sbt(p2, L + "gnB", [128, 64], F32)
                    BW2, BWo, Bpr = B(L + "W2"), B(L + "Wo"), B(L + "p2par")
                    w2map = [(COLS["b_x"], 0, 512), (COLS["d_q"], 512, 768), (COLS["d_r"], 1280, 16)]
                    with ExitStack() as stg:
                        stage = [sbt(stg, L + "stgb%d" % k, [128, 1296], F32) for k in range(4)]
                        g1B = sbt(stg, L + "g1B", [128, 1024], F32)
                        bd = sbt(stg, L + "bd", [128, 2, 2, 128], F32)
                        for kc in range(8):
                            slot = kc % 4
                            Bs = B("stgb%d" % slot)
                            kb.dma(stage[slot][:, 0:512], w_in[l, kc * 128:(kc + 1) * 128, 1024:1536], (), [Bs])
                            kb.dma(stage[slot][:, 512:1296], w_in[l, kc * 128:(kc + 1) * 128, 2560:3344], (), [Bs])
                            CP("dve", W2[:, kc, 0:512], stage[slot][:, 0:512], [Bs], [BW2])
                            CP("pool", W2[:, kc, 512:896], stage[slot][:, 512:896], [Bs], [BW2])
                            CP("act", W2[:, kc, 896:1296], stage[slot][:, 896:1296], [Bs], [BW2])
                        kb.dma(g1B[:], mod_d[l:l + 1, :].partition_broadcast(128), [B("modd")], [B("g1B")])
                        TS("dve", g1B[:], g1B[:], 1.0, None, ALU.add, None, [B("g1B")], [B("g1B")])
                        for kc in range(8):
                            slot = kc % 4
                            Bs = B("stgb%d" % slot)
                            kb.dma(stage[slot][:, 0:1024], w_out[l, kc * 128:(kc + 1) * 128, :], (), [Bs])
                            TT("dve" if kc % 2 == 0 else "pool", Wo[:, kc, :], stage[slot][:, 0:1024], g1B[:], ALU.mult,
                               [Bs, B("g1B")], [BWo])
                        kb.dma(lnG[:], lng_in[l:l + 1, :].partition_broadcast(128), (), [B("lnG")])
                        kb.dma(lnBt[:], lnb_in[l:l + 1, :].partition_broadcast(128), (), [B("lnB")])
                        kb.dma(cw[:], convw_in[l], (), [B("cw")])
                        kb.dma(cbv[:], convb_in[l], (), [B("cbv")])
                        kb.dma(bav[:], lruba_in[l], (), [B("bav")])
                        kb.dma(bxv[:], lrubx_in[l], (), [B("bxv")])
                        kb.dma(lamv[:], lrulam_in[l], (), [B("lamv")])
                        MEMSET("pool", wr_f[:], 0.0, [B("wr_f")])
                        kb.dma(wr_f[0:16, :], glawr_in[l], (), [B("wr_f")])
                        kb.dma(wr_f[32:33, :], glabr_in[l:l + 1, :], (), [B("wr_f")])
                        kb.dma(br_f[:], glabr_in[l:l + 1, :], (), [B("br_f")])
                        kb.dma(gnB[:], glagn_in[l:l + 1, :].partition_broadcast(128), (), [B("gnB")])
                        ACTF(n8sp[:], lamv[:], AF.Exp, [B("lamv")], [Bpr], scale=-1.0)
                        ACTF(n8sp[:], n8sp[:], AF.Ln, [Bpr], [Bpr], bias=1.0)
                        TS("dve", n8sp[:], n8sp[:], -8.0, None, ALU.mult, None, [Bpr], [Bpr])
                        TS("dve", nbav[:], bav[:], -1.0, None, ALU.mult, None, [B("bav")], [Bpr])
                        TS("dve", nbxv[:], bxv[:], -1.0, None, ALU.mult, None, [B("bxv")], [Bpr])
                        MEMSET("pool", bd[:], 0.0, [B("bd")])
                        for wi, src in enumerate((lruwa_in, lruwx_in)):
                            for ct in range(2):
                                for hb in range(2):
                                    kb.dma(bd[64 * hb:64 * hb + 64, wi, ct, 64 * hb:64 * hb + 64], src[l, 2 * ct + hb],
                                           (), [B("bd")])
                        CP("dve", WaBD[:], bd[:, 0, :, :], [B("bd")], [Bpr])
                        CP("dve", WxBD[:], bd[:, 1, :, :], [B("bd")], [Bpr])
                        for e in ("dve", "pool", "act", "sp"):
                            kb.wait_all(e, [B("stgb0"), B("stgb1"), B("stgb2"), B("stgb3"), B("g1B"), B("bd")])
                    Bpar = [Bpr, B("cw"), B("cbv"), B("bav"), B("bxv"), B("wr_f"), B("br_f"),
                            B("gnB")]

                    ckpt("s2")
                    xt2 = [sbt(p2, L + "xt%d" % k, [128, 2, 1024], F32) for k in range(3)]
                    LNS = []
                    for k_ in range(2):
                        LNS.append((sbt(p2, L + "lst%d" % k_, [128, 12], F32), sbt(p2, L + "lmv%d" % k_, [128, 2], F32),
                                    sbt(p2, L + "lsd%d" % k_, [128, 1], F32), sbt(p2, L + "lrs%d" % k_, [128, 1], F32),
                                    sbt(p2, L + "lnm%d" % k_, [128, 1], F32), B(L + "lnst2_%d" % k_)))
                    xn = [sbt(p2, L + "xn%d" % k, [128, 1024], F32) for k in range(2)]
                    st_ = sbt(p2, L + "st2", [128, 12], F32)
                    mv_ = sbt(p2, L + "mv2", [128, 2], F32)
                    sd_ = sbt(p2, L + "sd2", [128, 1], F32)
                    rstd_ = sbt(p2, L + "rstd2", [128, 1], F32)
                    nmr_ = sbt(p2, L + "nmr2", [128, 1], F32)
                    uT = [sbt(p2, L + "uTb%d" % k, [128, 8, 256], BF16) for k in range(2)]
                    mix2 = [sbt(p2, L + "mix2_%d" % k, [128, 4, 256], BF16) for k in range(2)]
                    XH = sbt(p2, L + "XH", [128, 2, 259], F32)
                    xc = sbt(p2, L + "xc", [128, 2, 256], F32)
                    xcb = sbt(p2, L + "xcb", [128, 2, 256], BF16)
                    rg = sbt(p2, L + "rg", [128, 2, 256], F32)
                    ig = sbt(p2, L + "ig", [128, 2, 256], F32)
                    av = sbt(p2, L + "av", [128, 2, 256], F32)
                    uv = sbt(p2, L + "uv", [128, 2, 256], F32)
                    hv = sbt(p2, L + "hv", [128, 2, 256], F32)
                    hcar = sbt(p2, L + "hcar", [128, 2], F32)
                    sgb = sbt(p2, L + "sgb", [128, 2, 256], F32)
                    drf = sbt(p2, L + "drf", [128, 256], F32)
                    MEMSET("pool", drf[:], 0.0, [B(L + "drf")])
                    MEMSET("pool", drf[32:33, :], 1.0, [B(L + "drf")])
                    e1 = [sbt(p2, L + "e1%d" % k_, [128, 128], F32) for k_ in range(2)]
                    lsp = [sbt(p2, L + "lsp%d" % k_, [128, 128], F32) for k_ in range(2)]
                    eq = [sbt(p2, L + "eq%d" % k_, [128, 128], F32) for k_ in range(2)]
                    ek = [sbt(p2, L + "ek%d" % k_, [128, 128], F32) for k_ in range(2)]
                    eb = [sbt(p2, L + "eb%d" % k_, [128, 128], F32) for k_ in range(2)]
                    ekl = [sbt(p2, L + "ekl%d" % k_, [128, 128], F32) for k_ in range(2)]
                    dcol = [sbt(p2, L + "dcol%d" % k_, [128, 1], F32) for k_ in range(2)]
                    qs = [sbt(p2, L + "qs%d" % k_, [128, 128], BF16) for k_ in range(2)]
                    ks = [sbt(p2, L + "ks%d" % k_, [128, 128], BF16) for k_ in range(2)]
                    kh = [sbt(p2, L + "kh%d" % k_, [128, 128], BF16) for k_ in range(2)]
                    vb = [sbt(p2, L + "vb%d" % k_, [128, 256], BF16) for k_ in range(2)]
                    QB = [sbt(p2, L + "QB%d" % k_, [128, 4, 128], BF16) for k_ in range(2)]
                    kst = [sbt(p2, L + "kst%d" % k_, [128, 128], BF16) for k_ in range(2)]
                    ATm = [sbt(p2, L + "ATm%d" % k_, [128, 4, 128], BF16) for k_ in range(2)]
                    Sw = sbt(p2, L + "Sw", [128, 256], F32)
                    Swb = sbt(p2, L + "Swb", [128, 256], BF16)
                    osb = [sbt(p2, L + "osb%d" % k_, [128, 4, 64], F32) for k_ in range(2)]
                    osq = [sbt(p2, L + "osq%d" % k_, [128, 4, 64], F32) for k_ in range(2)]
                    ss = [sbt(p2, L + "ss%d" % k_, [128, 4], F32) for k_ in range(2)]
                    rr = [sbt(p2, L + "rr%d" % k_, [128, 4], F32) for k_ in range(2)]
                    sgd = [sbt(p2, L + "sgd%d" % k_, [128, 4, 64], F32) for k_ in range(2)]
                    mixD = [sbt(p2, L + "mixD%d" % k_, [128, 256], BF16) for k_ in range(2)]
                    Bst = B(L + "lnst2")
                    MEMSET("pool", XH[:], 0.0, [B(L + "XH")])
                    MEMSET("pool", Sw[:], 0.0, [B(L + "Sw")])
                    MEMSET("pool", Swb[:], 0.0, [B(L + "Swb")])
                    MEMSET("pool", hcar[:], 0.0, [B(L + "hcar")])

                    def ch_front(i):
                        t0 = 256 * i
                        sl = i % 2
                        xtt, Bx = xt2[i % 3], B("xt%d" % (i % 3))
                        uTt, BuT = uT[sl], B(L + "uTb%d" % sl)
                        m2, Bm2 = mix2[sl], B(L + "mix2_%d" % sl)
                        st_, mv_, sd_, rstd_, nmr_, Bst = LNS[0]
                        rd = [B("x1t%d" % i)] if src_is_scr else []
                        for s in range(2):
                            r0 = t0 + 128 * s
                            kb.dma(xtt[:, s, :], xsrc[r0:r0 + 128, :], rd, [Bx])
                            yield
                        kb.dma(uTt[:].rearrange("p k t -> p (k t)"), ut_d[i], [B("utd%d" % i)], [BuT])
                        yield

                    def ch_lru(i):
                        t0 = 256 * i
                        sl = i % 2
                        xtt, Bx = xt2[i % 3], B("xt%d" % (i % 3))
                        uTt, BuT = uT[sl], B(L + "uTb%d" % sl)
                        m2, Bm2 = mix2[sl], B(L + "mix2_%d" % sl)
                        BXH, Bxc, Bg_ = B(L + "XH"), B(L + "xc"), B(L + "lrug")
                        for gi in range(4):
                            bank, Bb = gp("lru")
                            for kc in range(8):
                                MM(bank[:, 0:256], W2[:, kc, gi * 128:(gi + 1) * 128], uTt[:, kc, :], kc == 0, kc == 7,
                                   [BW2, BuT], [Bb])
                                yield
                            if gi < 2:
                                CP("act", XH[:, gi, 3:259], bank[:, 0:256], [Bb], [BXH])
                                yield
                            else:
                                SIGM(sgb[:, gi - 2, :], bank[:, 0:256], [Bb], B(L + "sgb"))
                                TT("dve", sgb[:, gi - 2, :], bank[:, 0:256], sgb[:, gi - 2, :], ALU.mult, [Bb, B(L + "sgb")], [B(L + "sgb")])
                                yield
                        for ct in range(2):
                            TS("dve", xc[:, ct, :], XH[:, ct, 0:256], cw[:, 4 * ct:4 * ct + 1], cbv[:, ct:ct + 1],
                               ALU.mult, ALU.add, [BXH] + Bpar, [Bxc])
                            yield
                            for k in range(1, 4):
                                STT(xc[:, ct, :], XH[:, ct, k:k + 256], cw[:, 4 * ct + k:4 * ct + k + 1], xc[:, ct, :],
                                    ALU.mult, ALU.add, [BXH, Bxc] + Bpar, [Bxc])
                                yield
                        CP("dve", XH[:, :, 0:3], XH[:, :, 256:259], [BXH, Bxc], [BXH])
                        yield
                        CP("act", xcb[:], xc[:], [Bxc], [B(L + "xcb")])
                        yield
                        for ct in range(2):
                            for wi, (Wt, bvec, dst) in enumerate(((WaBD, nbav, rg), (WxBD, nbxv, ig))):
                                bank, Bb = gp("lru")
                                MM(bank[:, 0:256], Wt[:, ct, :], xcb[:, ct, :], True, True, [Bpr, B(L + "xcb")], [Bb])
                                yield
                                SIGM(dst[:, ct, :], bank[:, 0:256], [Bb] + Bpar, Bg_, nbias=bvec[:, ct:ct + 1])
                                yield
                        for ct in range(2):
                            ACTF(av[:, ct, :], rg[:, ct, :], AF.Exp, [Bg_, Bpr], [B(L + "av")], scale=n8sp[:, ct:ct + 1])
                            yield
                        TT("dve", uv[:], av[:], av[:], ALU.mult, [B(L + "av")], [B(L + "uv")])
                        yield
                        TS("dve", uv[:], uv[:], -1.0, 1.0, ALU.mult, ALU.add, [B(L + "uv")], [B(L + "uv")])
                        yield
                        TS("dve", uv[:], uv[:], 1e-30, None, ALU.max, None, [B(L + "uv")], [B(L + "uv")])
                        yield
                        ACTF(uv[:], uv[:], AF.Ln, [B(L + "uv")], [B(L + "uv")])
                        ACTF(uv[:], uv[:], AF.Exp, [B(L + "uv")], [B(L + "uv")], scale=0.5)
                        yield
                        TT("dve", ig[:], ig[:], xc[:], ALU.mult, [Bg_, Bxc], [Bg_])
                        yield
                        TT("dve", uv[:], uv[:], ig[:], ALU.mult, [B(L + "uv"), Bg_], [B(L + "uv")])
                        yield
                        for ct in range(2):
                            kb.op("dve", lambda g: g.tensor_tensor_scan(out=hv[:, ct, :], data0=av[:, ct, :], data1=uv[:, ct, :],
                                                                        initial=hcar[:, ct:ct + 1], op0=ALU.mult, op1=ALU.add),
                                  [B(L + "av"), B(L + "uv"), B(L + "hcar")], [B(L + "hv")])
                            yield
                        CP("dve", hcar[:], hv[:, :, 255], [B(L + "hv")], [B(L + "hcar")])
                        yield
                        TT("dve", m2[:, 0:2, :], hv[:], sgb[:], ALU.mult, [B(L + "hv"), B(L + "sgb")], [Bm2])
                        yield

                    def ch_gla(i, s):
                        t0 = 256 * i
                        sl = i % 2
                        xtt, Bx = xt2[i % 3], B("xt%d" % (i % 3))
                        uTt, BuT = uT[sl], B(L + "uTb%d" % sl)
                        m2, Bm2 = mix2[sl], B(L + "mix2_%d" % sl)
                        if s == 0:
                            bank, Bb = gp("gla%d" % s)
                            for kc in range(8):
                                MM(bank[0:16, 0:256], W2[:, kc, 1280:1296], uTt[:, kc, :], kc == 0, kc == 7, [BW2, BuT], [Bb])
                                yield
                            CP("act", drf[0:16, :], bank[0:16, 0:256], [Bb], [B(L + "drf")])
                            yield
                            gflags[("drf", i)] = True
                        else:
                            while not gflags.get(("drf", i)):
                                yield
                        Bgl = B(L + "gla")
                        lin, Bl = gp("gla%d" % s)
                        MM(lin[:, 0:128], drf[:, s * 128:(s + 1) * 128], wr_f[:, :], True, True,
                           [B(L + "drf")] + Bpar, [Bl])
                        yield
                        ACTF(e1[s][:], lin[:, 0:128], AF.Exp, [Bl], [B(L + "e1%d" % s)], scale=-1.0)
                        yield
                        ACTF(lsp[s][:], e1[s][:], AF.Ln, [B(L + "e1%d" % s)], [B(L + "lsp%d" % s)], bias=1.0)
                        yield
                        bc_, Bbc = gp("gla%d" % s)
                        MM(bc_[:, 0:128], tri16[:], lsp[s][:], True, True, [Bc, B(L + "lsp%d" % s)], [Bbc])
                        yield
                        MM(bc_[:, 128:256], ones16[:], lsp[s][:], True, True, [Bc, B(L + "lsp%d" % s)], [Bbc])
                        yield
                        MM(bc_[:, 256:257], lsp[s][:], ones16[:, 0:1], True, True, [Bc, B(L + "lsp%d" % s)], [Bbc])
                        yield
                        ACTF(eq[s][:], bc_[:, 0:128], AF.Exp, [Bbc], [B(L + "eq%d" % s)])
                        yield
                        ACTF(ek[s][:], bc_[:, 0:128], AF.Exp, [Bbc], [B(L + "ek%d" % s)], scale=-1.0)
                        yield
                        ACTF(eb[s][:], bc_[:, 128:256], AF.Exp, [Bbc], [B(L + "eb%d" % s)])
                        yield
                        ACTF(dcol[s][:], bc_[:, 256:257], AF.Exp, [Bbc], [B(L + "dcol%d" % s)])
                        yield
                        TT("dve", ekl[s][:], ek[s][:], eb[s][:], ALU.mult, [B(L + "ek%d" % s), B(L + "eb%d" % s)], [B(L + "ekl%d" % s)])
                        yield
                        d1, Bd1 = gp("gla%d" % s)
                        for kc in range(8):
                            MM(d1[:, 0:512], uTt[:, kc, s * 128:(s + 1) * 128], W2[:, kc, 512:1024], kc == 0, kc == 7,
                               [BW2, BuT], [Bd1])
                            yield
                        STT(qs[s][:], d1[:, 0:128], 32.0 ** -0.5, eq[s][:], ALU.mult, ALU.mult, [Bd1, B(L + "eq%d" % s)], [B(L + "qs%d" % s)])
                        yield
                        TT("dve", ks[s][:], d1[:, 128:256], ek[s][:], ALU.mult, [Bd1, B(L + "ek%d" % s)], [B(L + "ks%d" % s)])
                        yield
                        TT("dve", kh[s][:], d1[:, 128:256], ekl[s][:], ALU.mult, [Bd1, B(L + "ekl%d" % s)], [B(L + "kh%d" % s)])
                        yield
                        CP("act", vb[s][:], d1[:, 256:512], [Bd1], [B(L + "vb%d" % s)])
                        yield
                        pb, Bpb = PB
                        po = 512 * s
                        TR(pb[:, po:po + 128], qs[s][:], ident_b[:], [B(L + "qs%d" % s), Bc], [Bpb])
                        yield
                        TR(pb[:, po + 128:po + 256], ks[s][:], ident_b[:], [B(L + "ks%d" % s), Bc], [Bpb])
                        yield
                        TT("dve", QB[s][:], pb[:, po:po + 128].unsqueeze(1).broadcast_to([128, 4, 128]), hm[:], ALU.mult,
                           [Bpb, Bc], [B(L + "QB%d" % s)])
                        yield
                        CP("act", kst[s][:], pb[:, po + 128:po + 256], [Bpb], [B(L + "kst%d" % s)])
                        yield
                        at, Bat = gp("gla%d" % s)
                        MM(at[:, 0:512], kst[s][:], QB[s][:].rearrange("p h i -> p (h i)"), True, True,
                           [B(L + "kst%d" % s), B(L + "QB%d" % s)], [Bat])
                        yield
                        TT("dve", ATm[s][:], at[:, 0:512].rearrange("p (h i) -> p h i", h=4),
                           tri[:].unsqueeze(1).broadcast_to([128, 4, 128]), ALU.mult, [Bat, Bc], [B(L + "ATm%d" % s)])
                        yield
                        if s == 1:
                            while not gflags.get(("swb", i, 0)):
                                yield
                        ob, Bob = gp("gla%d" % s)
                        for h in range(4):
                            MM(ob[:, 64 * h:64 * h + 64], ATm[s][:, h, :], vb[s][:, 64 * h:64 * h + 64], True, False,
                               [B(L + "ATm%d" % s), B(L + "vb%d" % s)], [Bob])
                            yield
                            MM(ob[:, 64 * h:64 * h + 64], QB[s][:, h, :], Swb[:, 64 * h:64 * h + 64], False, True,
                               [B(L + "QB%d" % s), B(L + "Swb")], [Bob])
                            yield
                        CP("act", osb[s][:].rearrange("p h d -> p (h d)"), ob[:, 0:256], [Bob], [B(L + "osb%d" % s)])
                        yield
                        spb, Bsp = gp("gla%d" % s)
                        MM(spb[:, 0:256], kh[s][:], vb[s][:], True, True, [B(L + "kh%d" % s), B(L + "vb%d" % s)], [Bsp])
                        yield
                        STT(Sw[:], Sw[:], dcol[s][:, 0:1], spb[:, 0:256], ALU.mult, ALU.add, [B(L + "Sw"), B(L + "dcol%d" % s), Bsp],
                            [B(L + "Sw")])
                        yield
                        CP("act", Swb[:], Sw[:], [B(L + "Sw")], [B(L + "Swb")])
                        gflags[("swb", i, s)] = True
                        yield
                        d2, Bd2 = gp("gla%d" % s)
                        for kc in range(8):
                            MM(d2[:, 0:256], uTt[:, kc, s * 128:(s + 1) * 128], W2[:, kc, 1024:1280], kc == 0, kc == 7,
                               [BW2, BuT], [Bd2])
                            yield
                        TT("dve", osq[s][:], osb[s][:], osb[s][:], ALU.mult, [B(L + "osb%d" % s)], [B(L + "osq%d" % s)])
                        yield
                        kb.op("dve", lambda g: g.tensor_reduce(out=ss[s][:, 0:4], in_=osq[s][:], axis=AX.X, op=ALU.add),
                              [B(L + "osq%d" % s)], [B(L + "ss%d" % s)])
                        yield
                        TS("dve", ss[s][:], ss[s][:], 1.0 / 64.0, LN_EPS, ALU.mult, ALU.add, [B(L + "ss%d" % s)], [B(L + "ss%d" % s)])
                        yield
                        ACTF(ss[s][:], ss[s][:], AF.Ln, [B(L + "ss%d" % s)], [B(L + "ss%d" % s)])
                        yield
                        ACTF(rr[s][:], ss[s][:], AF.Exp, [B(L + "ss%d" % s)], [B(L + "rr%d" % s)], scale=-0.5)
                        yield
                        SIGM(sgd[s][:].rearrange("p h d -> p (h d)"), d2[:, 0:256], [Bd2], B(L + "sgd%d" % s))
                        TT("dve", sgd[s][:].rearrange("p h d -> p (h d)"), d2[:, 0:256], sgd[s][:].rearrange("p h d -> p (h d)"), ALU.mult,
                           [Bd2, B(L + "sgd%d" % s)], [B(L + "sgd%d" % s)])
                        yield
                        TT("dve", sgd[s][:], sgd[s][:], gnB[:].unsqueeze(1).broadcast_to([128, 4, 64]), ALU.mult,
                           [B(L + "sgd%d" % s)] + Bpar, [B(L + "sgd%d" % s)])
                        yield
                        TT("dve", osb[s][:], osb[s][:], rr[s][:].unsqueeze(2).broadcast_to([128, 4, 64]), ALU.mult,
                           [B(L + "osb%d" % s), B(L + "rr%d" % s)], [B(L + "osb%d" % s)])
                        yield
                        TT("dve", mixD[s][:].rearrange("p (h d) -> p h d", h=4), osb[s][:], sgd[s][:], ALU.mult,
                           [B(L + "osb%d" % s), B(L + "sgd%d" % s)], [B(L + "mixD%d" % s)])
                        yield
                        TR(pb[:, po + 256:po + 384], mixD[s][:, 0:128], ident_b[:], [B(L + "mixD%d" % s), Bc], [Bpb])
                        yield
                        TR(pb[:, po + 384:po + 512], mixD[s][:, 128:256], ident_b[:], [B(L + "mixD%d" % s), Bc], [Bpb])
                        yield
                        CP("act", m2[:, 2:4, s * 128:(s + 1) * 128], pb[:, po + 256:po + 512].rearrange("p (k t) -> p k t", k=2),
                           [Bpb], [Bm2])
                        yield

                    def ch_out(i):
                        t0 = 256 * i
                        sl = i % 2
                        xtt, Bx = xt2[i % 3], B("xt%d" % (i % 3))
                        uTt, BuT = uT[sl], B(L + "uTb%d" % sl)
                        m2, Bm2 = mix2[sl], B(L + "mix2_%d" % sl)
                        st_, mv_, sd_, rstd_, nmr_, Bst = LNS[1]
                        for s in range(2):
                            for half in range(2):
                                yb, Byb = gp("out")
                                for f in range(8):
                                    grp = f // 2
                                    if grp in (0, 2):
                                        slot4 = (0 if grp == 0 else 2) + (f % 2)
                                        lhs = mixAC[:, slot4, t0 + s * 128:t0 + (s + 1) * 128]
                                        Rd = [B(L + "mix%d_%d" % (slot4, i))]
                                    else:
                                        slot4 = (0 if grp == 1 else 2) + (f % 2)
                                        lhs = m2[:, slot4, s * 128:(s + 1) * 128]
                                        Rd = [Bm2]
                                    MM(yb[:, 0:512], lhs, Wo[:, f, half * 512:(half + 1) * 512], f == 0, f == 7, Rd + [BWo], [Byb])
                                    yield
                                STT(xtt[:, s, half * 512:(half + 1) * 512], xtt[:, s, half * 512:(half + 1) * 512], ALPHA,
                                    yb[:, 0:512], ALU.mult, ALU.add, [Bx, Byb], [Bx])
                                yield
                            ln_stats(xtt[:, s, :], [Bx], st_, mv_, sd_, rstd_, nmr_, Bst)
                            yield
                            ACTF(xtt[:, s, :], xtt[:, s, :], AF.Identity, [Bx, Bst], [Bx], bias=nmr_[:, 0:1], scale=rstd_[:, 0:1])
                            yield
                            TT("dve", xtt[:, s, :], xtt[:, s, :], lnG[:], ALU.mult, [Bx, B("lnG")], [Bx])
                            yield
                            TT("dve", xtt[:, s, :], xtt[:, s, :], lnBt[:], ALU.add, [Bx, B("lnB")], [Bx])
                            yield
                        Bdst = B("x1t%d" % i) if l < DEPTH - 1 else B("outt%d" % i)
                        for s in range(2):
                            r0 = t0 + 128 * s
                            kb.dma(xdst[r0:r0 + 128, :], xtt[:, s, :], [Bx], [Bdst], sembuf=Bx)
                            yield


                    def run_chains(gens):
                        gens = list(gens)
                        while gens:
                            for g_ in list(gens):
                                try:
                                    next(g_)
                                except StopIteration:
                                    gens.remove(g_)

                    gflags = {}
                    run_chains([ch_front(0)])
                    for i in range(NT):
                        gens = [ch_gla(i, 0), ch_gla(i, 1), ch_lru(i)]
                        if i + 1 < NT:
                            gens.append(ch_front(i + 1))
                        if i >= 1:
                            gens.append(ch_out(i - 1))
                        run_chains(gens)
                    run_chains([ch_out(NT - 1)])
                    for e in ("pe", "dve", "act", "pool", "sp"):
                        kb.wait_all(e, list(kb.bufs.values()))
        for e in ("sp", "act"):
            kb.wait_all(e, [b for n_, b in kb.bufs.items() if n_.startswith("outt")])
        build.stats = dict(nins=kb.nins, nwait=kb.nwait, nsem=kb.nsem)
    return nc


_NC_CACHE = {}


def _layout_inputs(inp, b, SEQ):
    f = lambda a: np.ascontiguousarray(a, dtype=np.float32)
    dep = inp["w_mod"].shape[0]
    half = 32
    inv = (np.float32(10000.0) ** (-(np.arange(128) % half).astype(np.float32) / np.float32(half))).astype(np.float32)
    d = {
        "x": f(inp["x"][b]),
        "cT": f(inp["c"][b].reshape(8, 128).T),
        "pos": np.ascontiguousarray(inp["positions"][b][None, :].astype(np.int32)),
        "rope_inv": f(inv[:, None]),
        "w_mod": f(inp["w_mod"]), "b_mod": f(inp["b_mod"]), "w_in": f(inp["w_in"]),
        "conv_w": f(inp["conv_w"].reshape(dep, 4, 2, 128).transpose(0, 3, 2, 1).reshape(dep, 128, 8)),
        "conv_b": f(inp["conv_b"].reshape(dep, 2, 128).transpose(0, 2, 1)),
        "lru_wa": f(inp["lru_wa"]), "lru_ba": f(inp["lru_ba"].reshape(dep, 2, 128).transpose(0, 2, 1)),
        "lru_wx": f(inp["lru_wx"]), "lru_bx": f(inp["lru_bx"].reshape(dep, 2, 128).transpose(0, 2, 1)),
        "lru_lam": f(inp["lru_lam"].reshape(dep, 2, 128).transpose(0, 2, 1)),
        "gla_wr": f(inp["gla_wr"]), "gla_br": f(inp["gla_br"]), "gla_gn": f(inp["gla_gn"]),
        "w_out": f(inp["w_out"]), "ln_g": f(inp["ln_g"]), "ln_b": f(inp["ln_b"]),
    }
    return d


def kernel(**inputs):
    inp = {k: np.asarray(v) for k, v in inputs.items()}
    Bn, SEQ, _ = inp["x"].shape
    DEPTH = inp["w_mod"].shape[0]
    key = (SEQ, DEPTH)
    if key not in _NC_CACHE:
        _NC_CACHE[key] = build(SEQ, DEPTH)
    nc = _NC_CACHE[key]
    in_maps = [_layout_inputs(inp, b, SEQ) for b in range(Bn)]
    res = run_bass_kernel_spmd(nc, in_maps, core_ids=list(range(Bn)))
    return np.stack([np.asarray(r["out"], dtype=np.float32) for r in res.results], axis=0)
```

```python
import math
from contextlib import ExitStack

import numpy as np
import concourse.bass as bass
import concourse.mybir as mybir
from concourse.bass_utils import run_bass_kernel_spmd

F32 = mybir.dt.float32
BF16 = mybir.dt.bfloat16
I32 = mybir.dt.int32
AF = mybir.ActivationFunctionType
ALU = mybir.AluOpType
AX = mybir.AxisListType

D = 1024
NEGM = -240000.0
LN_EPS = 1e-5
ALPHA = 4.0 ** 0.25
SEM_LIMIT = 30000
import os
_KSTOP = os.environ.get("KSTOP", "")
_KSKIP = set(os.environ.get("KSKIP", "").split(","))


_STOP = [False]


def ckpt(name):
    if _KSTOP and _KSTOP == name:
        _STOP[0] = True


class Buf:
    __slots__ = ("name", "w", "r", "sem", "cnt")

    def __init__(self, name):
        self.name = name
        self.w = None
        self.r = {}
        self.sem = None
        self.cnt = 0


class KB:
    def __init__(self, nc, es):
        self.nc = nc
        self.es = es
        self.eng = {"pe": nc.tensor, "dve": nc.vector, "act": nc.scalar, "pool": nc.gpsimd, "sp": nc.sync}
        self.sem = {}
        self.cnt = {}
        self.nsem = 0
        for k in self.eng:
            self._newsem(k)
        self.waited = {k: {} for k in self.eng}
        self.nwait = 0
        self.nins = 0
        self.bufs = {}
        self.streams = {}
        self.semtot = {}
        self.store_names = set()

    def B(self, name):
        b = self.bufs.get(name)
        if b is None:
            b = self.bufs[name] = Buf(name)
        return b

    def _newsem(self, k):
        self.sem[k] = self.es.enter_context(self.nc.semaphore("s_%s_%d" % (k, self.nsem)))
        self.cnt[k] = 0
        self.nsem += 1

    def _need(self, e, ev, need):
        if ev is None:
            return
        sem, val, src = ev
        if src == "pe" and e == "pe":
            return
        if self.waited[e].get(id(sem), 0) >= val:
            return
        cur = need.get(id(sem))
        if cur is None or cur[1] < val:
            need[id(sem)] = (sem, val)

    def _deps(self, e, reads, writes):
        need = {}
        for b in reads:
            self._need(e, b.w, need)
        for b in writes:
            self._need(e, b.w, need)
            for ev in b.r.values():
                self._need(e, ev, need)
        return list(need.values())

    def _emit_waits(self, e, waits):
        for sem, val in waits:
            self.eng[e].wait_ge(sem, val)
            self.waited[e][id(sem)] = val
            self.nwait += 1

    def _commit(self, ev, reads, writes):
        for b in reads:
            b.r[id(ev[0])] = ev
        for b in writes:
            b.w = ev
            b.r = {}

    def op(self, e, fn, reads=(), writes=(), attach=None, lhs_reads=None, inc=True):
        if _STOP[0]:
            return None
        if attach is None:
            attach = (e != "pe")
        if e == "pe" and lhs_reads is not None:
            pre = self._deps(e, lhs_reads, ())
            self._emit_waits(e, pre)
            attach = True
        ex = [b for b in reads if b.name.startswith("ps")]
        if ex:
            writes = list(writes) + [b for b in ex if b not in writes]
        waits = self._deps(e, reads, writes)
        last = None
        if attach and waits:
            last = waits.pop()
        self._emit_waits(e, waits)
        ins = fn(self.eng[e])
        if last is not None:
            ins._wait_ge(last[0], last[1])
            self.waited[e][id(last[0])] = last[1]
        self.nins += 1
        if e == "pe" and not inc:
            ev = (self.sem[e], self.cnt[e] + 1, e)
            self._commit(ev, reads, writes)
            return ev
        if e != "pe" and self.cnt[e] >= SEM_LIMIT:
            self._newsem(e)
        self.cnt[e] += 1
        ev = (self.sem[e], self.cnt[e], e)
        ins.then_inc(self.sem[e], 1)
        self._commit(ev, reads, writes)
        return ev

    def dma(self, out, in_, reads=(), writes=(), e="sp", sembuf=None):
        if _STOP[0]:
            return None
        waits = self._deps(e, reads, writes)
        self._emit_waits(e, waits)
        b = sembuf if sembuf is not None else writes[0]
        if b.sem is None or b.cnt >= SEM_LIMIT:
            b.sem = self.es.enter_context(self.nc.semaphore("d_%d" % self.nsem))
            b.cnt = 0
            self.nsem += 1
        ins = self.eng[e].dma_start(out=out, in_=in_)
        b.cnt += 16
        ins.then_inc(b.sem, 16)
        self.nins += 1
        ev = (b.sem, b.cnt, "dma")
        self._commit(ev, reads, writes)
        return ev

    def pe_fence(self):
        if _STOP[0] or self.cnt["pe"] == 0:
            return
        self.eng["pe"].wait_ge(self.sem["pe"], self.cnt["pe"])
        self.nwait += 1

    def wait_all(self, e, bufs):
        if _STOP[0]:
            return
        need = {}
        for b in bufs:
            self._need(e, b.w, need)
            for ev in b.r.values():
                self._need(e, ev, need)
        self._emit_waits(e, list(need.values()))


def build(SEQ=4096, DEPTH=2):
    _STOP[0] = False
    NT = SEQ // 256
    NS = SEQ // 128
    nc = bass.Bass("TRN2", target_bir_lowering=False)

    def din(name, shape, dtype=F32):
        return nc.dram_tensor(name, list(shape), dtype, kind="ExternalInput").ap()

    x_in = din("x", [SEQ, D])
    cT_in = din("cT", [128, 8])
    pos_in = din("pos", [1, SEQ], I32)
    inv_in = din("rope_inv", [128, 1])
    w_mod = din("w_mod", [DEPTH, D, 3 * D])
    b_mod = din("b_mod", [DEPTH, 3 * D])
    w_in = din("w_in", [DEPTH, D, 3344])
    convw_in = din("conv_w", [DEPTH, 128, 8])
    convb_in = din("conv_b", [DEPTH, 128, 2])
    lruwa_in = din("lru_wa", [DEPTH, 4, 64, 64])
    lruba_in = din("lru_ba", [DEPTH, 128, 2])
    lruwx_in = din("lru_wx", [DEPTH, 4, 64, 64])
    lrubx_in = din("lru_bx", [DEPTH, 128, 2])
    lrulam_in = din("lru_lam", [DEPTH, 128, 2])
    glawr_in = din("gla_wr", [DEPTH, 16, 128])
    glabr_in = din("gla_br", [DEPTH, 128])
    glagn_in = din("gla_gn", [DEPTH, 64])
    w_out = din("w_out", [DEPTH, D, D])
    lng_in = din("ln_g", [DEPTH, D])
    lnb_in = din("ln_b", [DEPTH, D])
    out_d = nc.dram_tensor("out", [SEQ, D], F32, kind="ExternalOutput").ap()
    x1_d = nc.dram_tensor("x1_scr", [SEQ, D], F32).ap()
    cs_d = nc.dram_tensor("cs_scr", [NT, 128, 512], F32).ap()
    mod_d = nc.dram_tensor("mod_scr", [DEPTH, 1024], F32).ap()
    ut_d = nc.dram_tensor("ut_scr", [NT, 128, 2048], BF16).ap()

    es = ExitStack()
    with es:
        kb = KB(nc, es)
        kb.store_names = {"out", "x1_scr", "cs_scr", "mod_scr", "ut_scr"}
        B = kb.B

        def sbt(st, name, shape, dtype):
            return st.enter_context(nc.sbuf_tensor(name, list(shape), dtype))

        PS = [es.enter_context(nc.psum_tensor("ps%d" % i, [128, 512], F32)) for i in range(7)]
        PSB = es.enter_context(nc.psum_tensor("psb", [128, 1024], BF16))
        gp_state = [0]
        NGEN = 3

        POOLS = {"front": [0], "lru": [1], "gla0": [3, 4], "gla1": [5, 2], "out": [6], "p1f": [0, 1], "p1z": [2]}
        pool_state = {k: 0 for k in POOLS}

        def gp(pool=None):
            if pool is None:
                i = gp_state[0] % NGEN
                gp_state[0] += 1
            else:
                lst = POOLS[pool]
                i = lst[pool_state[pool] % len(lst)]
                pool_state[pool] += 1
            return PS[i], B("ps%d" % i)

        ST = [(PS[3], B("ps3")), (PS[4], B("ps4")), (PS[2], B("ps2"))]
        NZb = [(PS[5], B("ps5")), (PS[6], B("ps6"))]
        PB = (PSB, B("psb"))

        block = es.enter_context(nc.Block())

        def TT(e, out, in0, in1, op, R, W):
            return kb.op(e, lambda g: g.tensor_tensor(out=out, in0=in0, in1=in1, op=op), R, W)

        def TS(e, out, in0, s1, s2, op0, op1, R, W):
            if op1 is None:
                return kb.op(e, lambda g: g.tensor_scalar(out=out, in0=in0, scalar1=s1, scalar2=None, op0=op0), R, W)
            return kb.op(e, lambda g: g.tensor_scalar(out=out, in0=in0, scalar1=s1, scalar2=s2, op0=op0, op1=op1), R, W)

        def STT(out, in0, scalar, in1, op0, op1, R, W):
            return kb.op("dve", lambda g: g.scalar_tensor_tensor(out=out, in0=in0, scalar=scalar, in1=in1, op0=op0, op1=op1), R, W)

        def ACTF(out, in_, func, R, W, bias=None, scale=None):
            kw = {}
            if bias is not None:
                kw["bias"] = bias
            if scale is not None:
                kw["scale"] = scale
            return kb.op("act", lambda g: g.activation(out=out, in_=in_, func=func, **kw), R, W)

        def SIGM(dst, src, R, Wb, nbias=None):
            ACTF(dst, src, AF.Exp, R, [Wb], scale=-1.0, bias=nbias)
            ACTF(dst, dst, AF.Ln, [Wb], [Wb], bias=1.0)
            ACTF(dst, dst, AF.Exp, [Wb], [Wb], scale=-1.0)

        def CP(e, out, in_, R, W):
            if e == "act":
                return kb.op("act", lambda g: g.activation(out=out, in_=in_, func=AF.Copy), R, W)
            return kb.op(e, lambda g: g.tensor_copy(out=out, in_=in_), R, W)

        def MM(out, lhsT, rhs, start, stop, R, W, LR=None, inc=None):
            if inc is None:
                inc = bool(stop)
            return kb.op("pe", lambda g: g.matmul(out, lhsT=lhsT, rhs=rhs, start=start, stop=stop), R, W, lhs_reads=LR, inc=inc)

        def TR(out, in_, ident, R, W):
            return kb.op("pe", lambda g: g.transpose(out=out, in_=in_, identity=ident), R, W)

        def MEMSET(e, ap, val, W):
            return kb.op(e, lambda g: g.memset(ap, val), (), W)

        def ASEL(out, in_, pattern, cmp, fill, base, cm, R, W):
            return kb.op("pool", lambda g: g.affine_select(out=out, in_=in_, pattern=pattern, compare_op=cmp,
                                                            fill=fill, base=base, channel_multiplier=cm), R, W)

        cst = es
        ident_f = sbt(cst, "ident_f", [128, 128], F32)
        ident_b = sbt(cst, "ident_b", [128, 128], BF16)
        rperm = sbt(cst, "rperm", [128, 128], BF16)
        tri = sbt(cst, "tri", [128, 128], F32)
        tri16 = sbt(cst, "tri16", [128, 128], F32)
        ones16 = sbt(cst, "ones16", [128, 128], F32)
        onesrow = sbt(cst, "onesrow", [1, 128], F32)
        cm = sbt(cst, "cm", [128, 2, 256], BF16)
        tb = sbt(cst, "tb", [128, 2432], BF16)
        ind = sbt(cst, "ind", [128, 16, 128], BF16)
        selA = sbt(cst, "selA", [128, 128], F32)
        selB = sbt(cst, "selB", [128, 128], F32)
        hm = sbt(cst, "hm", [128, 4, 128], BF16)
        bmall = sbt(cst, "bmall", [128, 16, 16], F32)
        inv = sbt(cst, "inv", [128, 1], F32)
        sgn = sbt(cst, "sgn", [128, 1], F32)
        cB = sbt(cst, "cB", [128, 8, 128], F32)
        Bc = B("consts")

        MEMSET("pool", ident_f[:], 1.0, [Bc])
        ASEL(ident_f[:], ident_f[:], [[-1, 128]], ALU.is_equal, 0.0, 0, 1, [Bc], [Bc])
        CP("dve", ident_b[:], ident_f[:], [Bc], [Bc])
        for blk, src in ((0, 1), (1, 0), (2, 3), (3, 2)):
            CP("dve", rperm[:, blk * 32:(blk + 1) * 32], ident_b[:, src * 32:(src + 1) * 32], [Bc], [Bc])
        MEMSET("pool", tri[:], 1.0, [Bc])
        ASEL(tri[:], tri[:], [[1, 128]], ALU.is_ge, 0.0, 0, -1, [Bc], [Bc])
        TS("dve", tri16[:], tri[:], -1.0 / 16.0, None, ALU.mult, None, [Bc], [Bc])
        MEMSET("pool", ones16[:], -1.0 / 16.0, [Bc])
        MEMSET("pool", onesrow[:], 1.0, [Bc])
        MEMSET("pool", selA[:], 0.0, [Bc])
        MEMSET("pool", selA[64:65, 0:64], 1.0, [Bc])
        MEMSET("pool", selB[:], 1.0, [Bc])
        ASEL(selB[:], selB[:], [[0, 128]], ALU.is_equal, 0.0, -63, 1, [Bc], [Bc])
        MEMSET("pool", selB[:, 0:64], 0.0, [Bc])
        MEMSET("pool", ind[:], 1.0, [Bc])
        ASEL(ind[:], ind[:], [[-1, 16], [0, 128]], ALU.is_equal, 0.0, 0, 1, [Bc], [Bc])
        MEMSET("pool", hm[:], 1.0, [Bc])
        ASEL(hm[:], hm[:], [[-32, 4], [0, 128]], ALU.is_ge, 0.0, 0, 1, [Bc], [Bc])
        ASEL(hm[:], hm[:], [[32, 4], [0, 128]], ALU.is_ge, 0.0, 31, -1, [Bc], [Bc])
        MEMSET("pool", bmall[:], 0.0, [Bc])
        ASEL(bmall[:], bmall[:], [[1, 16], [-1, 16]], ALU.is_ge, NEGM, -1, 0, [Bc], [Bc])
        with ExitStack() as tmp:
            zer = sbt(tmp, "zer", [128, 256], F32)
            pidx = sbt(tmp, "pidx", [128, 1], I32)
            pf = sbt(tmp, "pf", [128, 1], F32)
            cT = sbt(tmp, "cT_sb", [128, 8], F32)
            Bt = B("ctmp")
            MEMSET("pool", zer[:], 0.0, [Bt])
            for c in range(2):
                ASEL(cm[:, c, :], zer[:], [[1, 256]], ALU.is_ge, NEGM, -128 * c, -1, [Bt], [Bc])
            TBW = 2432
            with ExitStack() as t2s:
                di = sbt(t2s, "di2", [128, TBW], I32)
                dfl = sbt(t2s, "dfl2", [128, TBW], F32)
                ge0 = sbt(t2s, "ge02", [128, TBW], F32)
                ca = sbt(t2s, "ca2", [128, TBW], F32)
                cb_ = sbt(t2s, "cb2", [128, TBW], F32)
                mi = sbt(t2s, "mi2", [128, TBW], I32)
                msum = sbt(t2s, "msum2", [128, TBW], F32)
                kb.op("pool", lambda g: g.iota(di[:], [[1, TBW]], base=-128, channel_multiplier=-1), (), [Bt])
                CP("dve", dfl[:], di[:], [Bt], [Bt])
                TS("dve", ge0[:], dfl[:], 0.0, None, ALU.is_ge, None, [Bt], [Bt])
                TS("dve", ca[:], dfl[:], 128.0, None, ALU.is_le, None, [Bt], [Bt])
                TT("dve", msum[:], ca[:], ge0[:], ALU.mult, [Bt], [Bt])
                for msk, lim in ((3, 512.0), (15, 2048.0)):
                    TS("dve", mi[:], di[:], msk, None, ALU.bitwise_and, None, [Bt], [Bt])
                    CP("dve", ca[:], mi[:], [Bt], [Bt])
                    TS("dve", ca[:], ca[:], 0.0, None, ALU.is_equal, None, [Bt], [Bt])
                    TS("dve", cb_[:], dfl[:], lim, None, ALU.is_le, None, [Bt], [Bt])
                    TT("dve", cb_[:], cb_[:], ge0[:], ALU.mult, [Bt], [Bt])
                    TT("dve", ca[:], ca[:], cb_[:], ALU.mult, [Bt], [Bt])
                    TT("dve", msum[:], msum[:], ca[:], ALU.add, [Bt], [Bt])
                TS("dve", ca[:], msum[:], 0.0, NEGM, ALU.is_equal, ALU.mult, [Bt], [Bt])
                TS("dve", cb_[:], msum[:], 1.0, None, ALU.max, None, [Bt], [Bt])
                ACTF(cb_[:], cb_[:], AF.Ln, [Bt], [Bt])
                STT(tb[:], cb_[:], 8.0, ca[:], ALU.mult, ALU.add, [Bt], [Bc])
                for e in ("dve", "act", "pool"):
                    kb.wait_all(e, [Bt, Bc])
            kb.dma(inv[:], inv_in[:, :], (), [Bc])
            kb.op("pool", lambda g: g.iota(pidx[:], [[0, 1]], base=0, channel_multiplier=1), (), [Bt])
            TS("dve", pidx[:], pidx[:], 63, None, ALU.bitwise_and, None, [Bt], [Bt])
            CP("dve", pf[:], pidx[:], [Bt], [Bt])
            TS("dve", pf[:], pf[:], 32.0, None, ALU.is_lt, None, [Bt], [Bt])
            TS("dve", sgn[:], pf[:], -2.0, 1.0, ALU.mult, ALU.add, [Bt], [Bc])
            kb.dma(cT[:], cT_in[:, :], (), [Bt])
            for kc in range(8):
                CP("dve", cB[:, kc, :], cT[:, kc:kc + 1].broadcast_to([128, 128]), [Bt], [Bc])
            posi = sbt(tmp, "posi", [128, SEQ], I32)
            ang = sbt(tmp, "ang", [128, SEQ], F32)
            a2 = sbt(tmp, "a2", [128, SEQ], F32)
            tqc = sbt(tmp, "tqc", [128, SEQ], F32)
            tqs = sbt(tmp, "tqs", [128, SEQ], F32)
            rc = sbt(tmp, "rc", [128, SEQ], F32)
            rs_ = sbt(tmp, "rs_", [128, SEQ], F32)
            MAGIC = 12582912.0
            C1 = 6.28125
            C2 = 2.0 * math.pi - C1
            Bp, Bw, Bw2 = B("posi"), B("ropew"), B("ropew2")
            kb.dma(posi[:], pos_in[0:1, :].partition_broadcast(128), (), [Bp])
            CP("dve", ang[:], posi[:], [Bp], [Bw])
            TS("dve", ang[:], ang[:], inv[:, 0:1], None, ALU.mult, None, [Bw, Bc], [Bw])
            TS("dve", a2[:], ang[:], math.pi / 2.0, None, ALU.add, None, [Bw], [B("ra2")])
            TS("dve", tqc[:], a2[:], 1.0 / (2.0 * math.pi), MAGIC, ALU.mult, ALU.add, [B("ra2")], [B("rtqc")])
            TS("dve", tqc[:], tqc[:], -MAGIC, None, ALU.add, None, [B("rtqc")], [B("rtqc")])
            STT(rc[:], tqc[:], -C1, a2[:], ALU.mult, ALU.add, [B("rtqc"), B("ra2")], [B("rrc")])
            STT(rc[:], tqc[:], -C2, rc[:], ALU.mult, ALU.add, [B("rtqc"), B("rrc")], [B("rrc")])
            TS("dve", rc[:], rc[:], -3.1415925, 3.1415925, ALU.max, ALU.min, [B("rrc")], [B("rrc")])
            ACTF(rc[:], rc[:], AF.Sin, [B("rrc")], [B("rrc")])
            TS("dve", tqs[:], ang[:], 1.0 / (2.0 * math.pi), MAGIC, ALU.mult, ALU.add, [Bw], [B("rtqs")])
            TS("dve", tqs[:], tqs[:], -MAGIC, None, ALU.add, None, [B("rtqs")], [B("rtqs")])
            STT(rs_[:], tqs[:], -C1, ang[:], ALU.mult, ALU.add, [B("rtqs"), Bw], [B("rrs")])
            STT(rs_[:], tqs[:], -C2, rs_[:], ALU.mult, ALU.add, [B("rtqs"), B("rrs")], [B("rrs")])
            TS("dve", rs_[:], rs_[:], -3.1415925, 3.1415925, ALU.max, ALU.min, [B("rrs")], [B("rrs")])
            ACTF(rs_[:], rs_[:], AF.Sin, [B("rrs")], [B("rrs")])
            TS("dve", rs_[:], rs_[:], sgn[:, 0:1], None, ALU.mult, None, [B("rrs"), Bc], [B("rrs")])
            csv = cs_d.rearrange("i p c -> p i c")
            kb.dma(csv[:, :, 0:256], rc[:].rearrange("p (i q) -> p i q", q=256), [B("rrc")], [B("csd0")], sembuf=B("rrc"))
            kb.dma(csv[:, :, 256:512], rs_[:].rearrange("p (i q) -> p i q", q=256), [B("rrs")], [B("csd1")], sembuf=B("rrs"))
            for i in range(2, NT):
                B("csd%d" % i)
            kb.wait_all("sp", [B("csd%d" % i) for i in range(NT)])
            for e in ("pe", "dve", "act", "pool", "sp"):
                kb.wait_all(e, [Bt, Bw, Bp, Bc, B("ra2"), B("rtqc"), B("rrc"), B("rtqs"), B("rrs"), B("csd0"), B("csd1")])

        scale1 = sbt(cst, "scale1", [128, 8], F32)
        shiftc = sbt(cst, "shiftc", [128, 8], F32)

        COLS = dict(a_q=0, a_k=256, a_v=512, a_g=768, b_x=1024, b_g=1280, c_q=1536, c_k=1792, c_v=2048,
                    c_g=2304, d_q=2560, d_k=2688, d_v=2816, d_g=3072, d_r=3328)

        def ln_stats(xrow, tagR, st, mv, sd, rstd, nmr, Bst):
            kb.op("dve", lambda g: g.bn_stats(out=st[:, 0:6], in_=xrow[:, 0:512]), tagR, [Bst])
            kb.op("dve", lambda g: g.bn_stats(out=st[:, 6:12], in_=xrow[:, 512:1024]), tagR, [Bst])
            kb.op("dve", lambda g: g.bn_aggr(out=mv[:, 0:2], in_=st[:, 0:12]), [Bst], [Bst])
            TS("dve", sd[:, 0:1], mv[:, 1:2], LN_EPS, None, ALU.add, None, [Bst], [Bst])
            ACTF(sd[:, 0:1], sd[:, 0:1], AF.Ln, [Bst], [Bst])
            ACTF(rstd[:, 0:1], sd[:, 0:1], AF.Exp, [Bst], [Bst], scale=-0.5)
            TS("dve", nmr[:, 0:1], mv[:, 0:1], rstd[:, 0:1], -1.0, ALU.mult, ALU.mult, [Bst], [Bst])

        ckpt("c")
        for l in range(DEPTH):
            xsrc = x_in if l == 0 else x1_d
            xdst = out_d if l == DEPTH - 1 else x1_d
            src_is_scr = l > 0
            L = "L%d_" % l
            with ExitStack() as lay:
                mixAC = sbt(lay, L + "mixAC", [128, 4, SEQ], BF16)

                with ExitStack() as p1:
                    W1 = sbt(p1, L + "W1", [128, 8, 2048], BF16)
                    BW1 = B(L + "W1")
                    with ExitStack() as ms:
                        wm = [sbt(ms, L + "wm%d" % k, [128, 8, 512], F32) for k in range(2)]
                        modB = sbt(ms, L + "modB", [128, 3072], F32)
                        bmodB = sbt(ms, L + "bmodB", [128, 3072], F32)
                        dtmp = sbt(ms, L + "dtmp", [128, 8, 128], F32)
                        stage = [sbt(ms, L + "stg%d" % k, [128, 2048], F32) for k in range(4)]
                        Bm = B("modB")

                        def mod_chain():
                            kb.dma(bmodB[:], b_mod[l:l + 1, :].partition_broadcast(128), (), [B("bmodB")])
                            wmv = w_mod[l].rearrange("(kc p) n -> p kc n", p=128)
                            for g in range(6):
                                slot = g % 2
                                Bw_ = B("wm%d" % slot)
                                kb.dma(wm[slot][:], wmv[:, :, g * 512:(g + 1) * 512], (), [Bw_])
                                yield
                                bank, Bb = gp("p1f")
                                for kc in range(8):
                                    MM(bank[:, 0:512], cB[:, kc, :], wm[slot][:, kc, :], kc == 0, kc == 7, [Bc, Bw_], [Bb])
                                    yield
                                TT("dve", modB[:, g * 512:(g + 1) * 512], bank[:, 0:512], bmodB[:, g * 512:(g + 1) * 512],
                                   ALU.add, [Bb, B("bmodB")], [Bm])
                                yield
                            for (dst, off, add1) in ((shiftc, 0, 0.0), (scale1, 1024, 1.0)):
                                TT("dve", dtmp[:], modB[:, off:off + 1024].rearrange("p (k n) -> p k n", k=8),
                                   ident_f[:].unsqueeze(1).broadcast_to([128, 8, 128]), ALU.mult, [Bm, Bc], [B(L + "dtmp")])
                                kb.op("dve", lambda g: g.tensor_reduce(out=dst[:, 0:8], in_=dtmp[:], axis=AX.X, op=ALU.add),
                                      [B(L + "dtmp")], [B("modcols")])
                                if add1:
                                    TS("dve", dst[:, 0:8], dst[:, 0:8], 1.0, None, ALU.add, None, [B("modcols")], [B("modcols")])
                                yield
                            kb.dma(mod_d[l:l + 1, :], modB[0:1, 2048:3072], [Bm], [B("modd")], sembuf=Bm)
                            yield

                        def w1_chain():
                            srcblk = [0, 1, 4, 5, 3, 7, 2, 6]
                            for kc in range(8):
                                slot = kc % 4
                                Bs = B("stg%d" % slot)
                                kb.dma(stage[slot][:, 0:1024], w_in[l, kc * 128:(kc + 1) * 128, 0:1024], (), [Bs])
                                kb.dma(stage[slot][:, 1024:2048], w_in[l, kc * 128:(kc + 1) * 128, 1536:2560], (), [Bs])
                                yield
                                for j in range(8):
                                    eng = ("pool", "act", "dve")[j % 3]
                                    sb_ = srcblk[j]
                                    CP(eng, W1[:, kc, j * 256:(j + 1) * 256], stage[slot][:, sb_ * 256:(sb_ + 1) * 256], [Bs], [BW1])
                                    yield

                        chains_ = [w1_chain(), mod_chain()]
                        while chains_:
                            for g_ in list(chains_):
                                try:
                                    next(g_)
                                except StopIteration:
                                    chains_.remove(g_)
                        for e in ("pe", "dve", "pool", "act", "sp"):
                            kb.wait_all(e, [Bm, B("bmodB"), B("wm0"), B("wm1"), B(L + "dtmp"), B("modd"), B("stg0"), B("stg1"), B("stg2"), B("stg3")])

                    ckpt("w")
                    KT = [sbt(p1, L + "KT%d" % m, [128, 2, SEQ], BF16) for m in range(2)]
                    VC = [sbt(p1, L + "VC%d" % m, [128, NS, 2, 129], BF16) for m in range(2)]
                    for m in range(2):
                        MEMSET("pool", VC[m][:, :, :, 64:65], 1.0, [B(L + "Vones%d" % m)])
                    xs = [sbt(p1, L + "xs%d" % k, [128, 1024], F32) for k in range(2)]
                    st_ = sbt(p1, L + "st", [128, 12], F32)
                    mv_ = sbt(p1, L + "mv", [128, 2], F32)
                    sd_ = sbt(p1, L + "sd", [128, 1], F32)
                    rstd_ = sbt(p1, L + "rstd", [128, 1], F32)
                    nmr_ = sbt(p1, L + "nmr", [128, 1], F32)
                    uT = [sbt(p1, L + "uT%d" % k, [128, 8, 256], BF16) for k in range(2)]
                    cs = [sbt(p1, L + "cs%d" % k, [128, 512], F32) for k in range(2)]
                    qb = [sbt(p1, L + "qb%d" % k, [128, 256], BF16) for k in range(2)]
                    t1 = [sbt(p1, L + "t1%d" % k, [128, 256], F32) for k in range(2)]
                    t2 = [sbt(p1, L + "t2%d" % k, [128, 256], F32) for k in range(2)]
                    QT = [[sbt(p1, L + "QT%d_%d" % (m, k), [128, 4, 256], BF16) for k in range(2)] for m in range(2)]
                    for m in range(2):
                        for k in range(2):
                            MEMSET("pool", QT[m][k][:], 0.0, [B(L + "QT%d_%d" % (m, k))])
                    sg = [sbt(p1, L + "sg%d" % k, [128, 4, 256], F32) for k in range(2)]
                    ksb = sbt(p1, L + "ksb", [128, 2, 16], BF16)
                    ksf = sbt(p1, L + "ksf", [128, 2, 16], F32)
                    gm = sbt(p1, L + "gm", [128, 8, 16], F32)
                    t8 = sbt(p1, L + "t8", [128, 8, 8], F32)
                    mbf = sbt(p1, L + "mbf", [128, 8, 16], F32)
                    mbias = sbt(p1, L + "mbias", [128, 8, 16], BF16)
                    MBT = [sbt(p1, L + "MBT%d" % k, [128, 4, 256], BF16) for k in range(2)]
                    for k in range(2):
                        MEMSET("pool", MBT[k][:], 0.0, [B(L + "MBT%d" % k)])
                    PT = [sbt(p1, L + "PT%d" % k, [128, 512], BF16) for k in range(4)]
                    nzs = [sbt(p1, L + "nzs%d" % k, [128, 256], F32) for k in range(2)]
                    for k in range(2):
                        MEMSET("pool", nzs[k][:], 0.0, [B(L + "nzs%d" % k)])
                    rz = [sbt(p1, L + "rz%d" % k, [128, 256], F32) for k in range(2)]
                    MEMSET("pool", ksb[:], 0.0, [B(L + "ksb")])
                    pt_state = [0]
                    st_state = [0]
                    hp_state = [0]
                    rope_state = [0]
                    Bst = B(L + "lnst")

                    def load_ln_transpose(i, s, uTt, BuT):
                        gsub = 2 * i + s
                        xt = xs[gsub % 2]
                        Bx = B("xs%d" % (gsub % 2))
                        r0 = gsub * 128
                        rd = [B("x1t%d" % (gsub // 2))] if src_is_scr else []
                        kb.dma(xt[:], xsrc[r0:r0 + 128, :], rd, [Bx])
                        ln_stats(xt, [Bx], st_, mv_, sd_, rstd_, nmr_, Bst)
                        ACTF(xt[:], xt[:], AF.Identity, [Bx, Bst], [Bx], bias=nmr_[:, 0:1], scale=rstd_[:, 0:1])
                        for g in range(2):
                            bank, Bb = gp("p1f")
                            for k4 in range(4):
                                kc = 4 * g + k4
                                TR(bank[:, k4 * 128:(k4 + 1) * 128], xt[:, kc * 128:(kc + 1) * 128], ident_f[:], [Bx, Bc], [Bb])
                            for k4 in range(4):
                                kc = 4 * g + k4
                                if k4 % 2 == 0:
                                    TS("dve", uTt[:, kc, s * 128:(s + 1) * 128], bank[:, k4 * 128:(k4 + 1) * 128],
                                       scale1[:, kc:kc + 1], shiftc[:, kc:kc + 1], ALU.mult, ALU.add,
                                       [Bb, B("modcols")], [BuT])
                                else:
                                    ACTF(uTt[:, kc, s * 128:(s + 1) * 128], bank[:, k4 * 128:(k4 + 1) * 128], AF.Identity,
                                         [Bb, B("modcols")], [BuT], bias=shiftc[:, kc:kc + 1], scale=scale1[:, kc:kc + 1])

                    def p1_front(i):
                        t0 = 256 * i
                        sl = i % 2
                        uTt, BuT = uT[sl], B(L + "uT%d" % sl)
                        cst_, Bcs = cs[sl], B("cs%d" % sl)
                        kb.dma(cst_[:], cs_d[i], [B("csd0"), B("csd1")], [Bcs])
                        yield
                        for s in range(2):
                            load_ln_transpose(i, s, uTt, BuT)
                            yield
                        kb.dma(ut_d[i], uTt[:].rearrange("p k t -> p (k t)"), [BuT], [B("utd%d" % i)], sembuf=BuT)
                        yield
                        ckpt("ln")
                        sgt, Bsg = sg[sl], B(L + "sg%d" % sl)
                        for gi in range(12):
                            bank, Bb = gp("p1f")
                            for kc in range(8):
                                MM(bank[:, 0:256], W1[:, kc, gi * 128:(gi + 1) * 128], uTt[:, kc, :], kc == 0, kc == 7,
                                   [BW1, BuT], [Bb])
                                yield
                            if gi >= 8:
                                if "silu" not in _KSKIP:
                                    SIGM(sgt[:, gi - 8, :], bank[:, 0:256], [Bb], Bsg)
                                    TT("dve", sgt[:, gi - 8, :], bank[:, 0:256], sgt[:, gi - 8, :], ALU.mult, [Bb, Bsg], [Bsg])
                                    yield
                                continue
                            if "rope" in _KSKIP:
                                continue
                            m = gi // 4
                            isk = (gi // 2) % 2
                            ct = gi % 2
                            if isk:
                                dest = KT[m][:, ct, t0:t0 + 256]
                                Bd = B(L + "KT%d_%d_%d" % (m, ct, i))
                            else:
                                dest = None
                                Bd = B(L + "QT%d_%d" % (m, sl))
                            r_ = rope_state[0] % 2
                            rope_state[0] += 1
                            Bq, Bt1, Bt2 = B(L + "qb%d" % r_), B(L + "t1%d" % r_), B(L + "t2%d" % r_)
                            CP("act", qb[r_][:], bank[:, 0:256], [Bb], [Bq])
                            yield
                            bank2, Bb2 = gp("p1f")
                            MM(bank2[:, 0:256], rperm[:], qb[r_][:], True, True, [Bc, Bq], [Bb2])
                            yield
                            if "ropett" in _KSKIP:
                                continue
                            if "nocs" in _KSKIP:
                                TT("dve", t1[r_][:], bank[:, 0:256], sgt[:, 0, :], ALU.mult, [Bb], [Bt1])
                                yield
                                TT("dve", t2[r_][:], bank2[:, 0:256], sgt[:, 1, :], ALU.mult, [Bb2], [Bt2])
                                yield
                            else:
                                TT("dve", t1[r_][:], bank[:, 0:256], cst_[:, 0:256], ALU.mult, [Bb, Bcs], [Bt1])
                                yield
                                TT("dve", t2[r_][:], bank2[:, 0:256], cst_[:, 256:512], ALU.mult, [Bb2, Bcs], [Bt2])
                                yield
                            if "ropepool" in _KSKIP:
                                continue
                            if dest is not None:
                                TT("pool", dest, t1[r_][:], t2[r_][:], ALU.add, [Bt1, Bt2], [Bd])
                                yield
                            else:
                                for hb in range(2):
                                    rs = slice(64 * hb, 64 * hb + 64)
                                    TT("pool", QT[m][sl][rs, 2 * ct + hb, :], t1[r_][rs, :], t2[r_][rs, :], ALU.add,
                                       [Bt1, Bt2], [Bd])
                                    yield
                        ckpt("qk")
                        for s in range(2):
                            gsub = 2 * i + s
                            bank, Bb = gp("p1f")
                            for kc in range(8):
                                MM(bank[:, 0:512], uTt[:, kc, s * 128:(s + 1) * 128], W1[:, kc, 1536:2048], kc == 0, kc == 7,
                                   [BW1, BuT], [Bb])
                                yield
                            for m in range(2):
                                Bv = B(L + "V%d_%d" % (m, i))
                                src = bank[:, m * 256:(m + 1) * 256].rearrange("p (a h d) -> p a h d", a=2, h=2)
                                CP("act" if m == 0 else "dve", VC[m][:, gsub, :, 0:64], src[:, :, 0, :],
                                   [Bb, B(L + "Vones%d" % m)], [Bv])
                                yield
                                CP("dve" if m == 0 else "act", VC[m][:, gsub, :, 65:129], src[:, :, 1, :],
                                   [Bb, B(L + "Vones%d" % m)], [Bv])
                                yield
                        ckpt("v")
                        for ct in range(2):
                            kb.op("dve", lambda g: g.tensor_reduce(out=ksf[:, ct, i:i + 1], in_=KT[0][:, ct, t0:t0 + 256],
                                                                    axis=AX.X, op=ALU.add),
                                  [B(L + "KT0_%d_%d" % (ct, i))], [B(L + "ksf")])
                            yield
                        CP("dve", ksb[:, :, i:i + 1], ksf[:, :, i:i + 1], [B(L + "ksf")], [B(L + "ksb")])
                        yield
                        ckpt("ks%d" % i)
                        MBTt, BMB = MBT[sl], B(L + "MBT%d" % sl)
                        BQA = B(L + "QT0_%d" % sl)
                        if i >= 1:
                            bank, Bb = gp("p1f")
                            for h in range(4):
                                ct = h // 2
                                for s in range(2):
                                    g8 = h * 2 + s
                                    MM(bank[:, g8 * 16:(g8 + 1) * 16], QT[0][sl][:, h, s * 128:(s + 1) * 128],
                                       ksb[:, ct, 0:16], True, True, [BQA, B(L + "ksb")], [Bb])
                                    yield
                            ckpt("gmm%d" % i)
                            Bg = B(L + "gate")
                            TT("dve", gm[:], bank[:, 0:128].rearrange("p (g n) -> p g n", g=8),
                               bmall[:, i, :].unsqueeze(1).broadcast_to([128, 8, 16]), ALU.add, [Bb, Bc], [Bg])
                            yield
                            for g8 in range(8):
                                kb.op("dve", lambda g: g.max(out=t8[:, g8, :], in_=gm[:, g8, :]), [Bg], [Bg])
                                yield
                            TT("dve", mbf[:], gm[:], t8[:, :, 2:3].broadcast_to([128, 8, 16]), ALU.is_ge, [Bg], [Bg])
                            yield
                            TS("dve", mbf[:], mbf[:], -1.0, -NEGM, ALU.add, ALU.mult, [Bg], [Bg])
                            yield
                            TT("dve", mbias[:], mbf[:], bmall[:, i, :].unsqueeze(1).broadcast_to([128, 8, 16]), ALU.add,
                               [Bg, Bc], [Bg])
                            yield
                            ckpt("gtop%d" % i)
                            pb, Bpb = PB
                            for g8 in range(8):
                                TR(pb[0:16, g8 * 128:(g8 + 1) * 128], mbias[:, g8, :], ident_b[:], [Bg, Bc], [Bpb])
                                yield
                            CP("act", MBTt[0:16].rearrange("p h q -> p (h q)"), pb[0:16, 0:1024], [Bpb], [BMB])
                            yield

                        if i == 1:
                            ckpt("g")

                    def p1_att(i):
                        t0 = 256 * i
                        sl = i % 2
                        sgt, Bsg = sg[sl], B(L + "sg%d" % sl)
                        MBTt, BMB = MBT[sl], B(L + "MBT%d" % sl)
                        items = [(m, h) for m in range(2) for h in range(4)]
                        work = []
                        for k_, (m, h) in enumerate(items):
                            if m == 0:
                                kts = list(range(0, 2 * i + 2))
                            else:
                                kts = list(range(max(0, 2 * i - 16), 2 * i + 2))
                            ng = (len(kts) + 1) // 2
                            for gi_ in range(ng):
                                work.append(dict(k=k_, m=m, h=h, grp=kts[2 * gi_:2 * gi_ + 2], base=2 * gi_, nk=len(kts),
                                                 last=(gi_ == ng - 1), idx=len(work)))

                        def emitS(w):
                            m, h = w["m"], w["h"]
                            hr = 64 * (h % 2)
                            ct = h // 2
                            rows = slice(hr, hr + 64)
                            BQ = B(L + "QT%d_%d" % (m, sl))
                            st, Bst_ = ST[w["idx"] % 3]
                            kt0 = w["grp"][0]
                            ti = kt0 // 2
                            if m == 0:
                                w["ord"] = list(w["grp"])
                                if ti < i:
                                    MM(st[:, 0:512], ind[:, ti, :], MBTt[:, h, :].unsqueeze(1).broadcast_to([128, 2, 256]), True, False,
                                       [Bc, BMB], [Bst_], LR=[Bc])
                                else:
                                    MM(st[:, 0:512], ident_b[:], cm[:].rearrange("p c q -> p (c q)"), True, False, [Bc], [Bst_], LR=[Bc])
                            else:
                                w["ord"] = [kt0 + 1, kt0]
                                d1_ = 2 * i - (kt0 + 1)
                                off = 128 * (d1_ + 1)
                                rhs_ap = bass.AP(tb, off, [[2432, 128], [128, 2], [1, 256]])
                                MM(st[:, 0:512], ident_b[:], rhs_ap, True, False, [Bc], [Bst_], LR=[Bc])
                            for c, kt in enumerate(w["ord"]):
                                ti = kt // 2
                                MM(st[:, c * 256:(c + 1) * 256], KT[m][:, ct, kt * 128:(kt + 1) * 128],
                                   QT[m][sl][:, h, :], False, c == 1, [B(L + "KT%d_%d_%d" % (m, ct, ti)), BQ], [Bst_],
                                   LR=[B(L + "KT%d_%d_%d" % (m, ct, ti))])

                        def emitE(w):
                            st, Bst_ = ST[w["idx"] % 3]
                            pslot = w["idx"] % 4
                            wdt = 256 * len(w["grp"])
                            ACTF(PT[pslot][:, 0:wdt], st[:, 0:wdt], AF.Exp, [Bst_], [B(L + "PT%d" % pslot)], scale=0.125)

                        def emitPV(w):
                            m, h = w["m"], w["h"]
                            odd = h % 2
                            M = 128 if odd else 65
                            win = slice(1, 129) if odd else slice(0, 65)
                            pslot = w["idx"] % 4
                            nz, Bnz = NZb[w["k"] % 2]
                            for c, kt in enumerate(w["ord"]):
                                idx_ = w["base"] + c
                                MM(nz[0:M, 0:256], VC[m][:, kt, h // 2, win], PT[pslot][:, c * 256:(c + 1) * 256],
                                   idx_ == 0, idx_ == w["nk"] - 1, [B(L + "V%d_%d" % (m, kt // 2)), B(L + "PT%d" % pslot)], [Bnz],
                                   LR=[B(L + "V%d_%d" % (m, kt // 2))], inc=(c == len(w["ord"]) - 1))

                        def fin1(w):
                            odd = w["h"] % 2
                            M = 128 if odd else 65
                            nz, Bnz = NZb[w["k"] % 2]
                            hp = w["k"] % 2
                            CP("act", nzs[hp][0:M, :], nz[0:M, 0:256], [Bnz], [B(L + "nzs%d" % hp)])

                        def fin2(w):
                            m, h = w["m"], w["h"]
                            odd = h % 2
                            hr = 64 * odd
                            ct = h // 2
                            rows = slice(hr, hr + 64)
                            hp = w["k"] % 2
                            Bn, Br = B(L + "nzs%d" % hp), B(L + "rz%d" % hp)
                            zb, Bzb = NZb[w["k"] % 2]
                            if odd:
                                MM(zb[:, 256:512], selB[:, :], nzs[hp][:, :], True, True, [Bc, Bn], [Bzb])
                            else:
                                MM(zb[0:64, 256:512], selA[:, 0:64], nzs[hp][:, :], True, True, [Bc, Bn], [Bzb])
                            kb.op("dve", lambda g: g.reciprocal(out=rz[hp][rows, :], in_=zb[rows, 256:512]), [Bzb], [Br])
                            gidx = 2 * m + ct
                            TT("pool", nzs[hp][rows, :], nzs[hp][rows, :], sgt[rows, gidx, :], ALU.mult, [Bn, Bsg], [Bn])
                            TT("dve", mixAC[rows, gidx, t0:t0 + 256], nzs[hp][rows, :], rz[hp][rows, :], ALU.mult,
                               [Bn, Br], [B(L + "mix%d_%d" % (gidx, i))])

                        deferred = []
                        emitS(work[0])
                        if len(work) > 1:
                            emitS(work[1])
                        for idx, w in enumerate(work):
                            emitE(w)
                            if idx + 2 < len(work):
                                emitS(work[idx + 2])
                            emitPV(w)
                            if w["last"]:
                                fin1(w)
                                deferred.append((idx + 1, w))
                            while deferred and deferred[0][0] <= idx:
                                fin2(deferred.pop(0)[1])
                            yield
                        while deferred:
                            fin2(deferred.pop(0)[1])

                    P1LEN = {}

                    def run_chains1(gens, names=None):
                        gens = list(gens)
                        names = list(names) if names else [None] * len(gens)
                        cnt = [0] * len(gens)
                        alive = [True] * len(gens)
                        tot = [float(P1LEN.get(n, 100)) for n in names]
                        while any(alive):
                            j = min((k for k in range(len(gens)) if alive[k]), key=lambda k: cnt[k] / tot[k])
                            try:
                                next(gens[j])
                                cnt[j] += 1
                            except StopIteration:
                                alive[j] = False
                                if names[j] is not None:
                                    P1LEN[names[j]] = max(cnt[j], 1)

                    run_chains1([p1_front(0)], ["front"])
                    for i in range(NT):
                        gens = [p1_att(i)]
                        nms = [None]
                        P1LEN["att"] = 4 * (i + 1) + 4 * min(i + 1, 9) + 1
                        nms = ["att"]
                        if i + 1 < NT:
                            gens.append(p1_front(i + 1))
                            nms.append("front")
                        run_chains1(gens, nms)
                    for e in ("pe", "dve", "act", "pool", "sp"):
                        kb.wait_all(e, list(kb.bufs.values()))

                ckpt("p1")
                with ExitStack() as p2:
                    W2 = sbt(p2, L + "W2", [128, 8, 1408], BF16)
                    MEMSET("pool", W2[:, :, 1296:1408], 0.0, [B(L + "W2")])
                    Wo = sbt(p2, L + "Wo", [128, 8, 1024], BF16)
                    lnG = sbt(p2, L + "lnG", [128, 1024], F32)
                    lnBt = sbt(p2, L + "lnB", [128, 1024], F32)
                    cw = sbt(p2, L + "cw", [128, 8], F32)
                    cbv = sbt(p2, L + "cbv", [128, 2], F32)
                    bav = sbt(p2, L + "bav", [128, 2], F32)
                    bxv = sbt(p2, L + "bxv", [128, 2], F32)
                    lamv = sbt(p2, L + "lamv", [128, 2], F32)
                    n8sp = sbt(p2, L + "n8sp", [128, 2], F32)
                    nbav = sbt(p2, L + "nbav", [128, 2], F32)
                    nbxv = sbt(p2, L + "nbxv", [128, 2], F32)
                    WaBD = sbt(p2, L + "WaBD", [128, 2, 128], BF16)
                    WxBD = sbt(p2, L + "WxBD", [128, 2, 128], BF16)
                    wr_f = sbt(p2, L + "wr_f", [128, 128], F32)
                    br_f = sbt(p2, L + "br_f", [1, 128], F32)
                    gnB = sbt(p2, L + "gnB", [128, 64], F32)
                    BW2, BWo, Bpr = B(L + "W2"), B(L + "Wo"), B(L + "p2par")
                    w2map = [(COLS["b_x"], 0, 512), (COLS["d_q"], 512, 768), (COLS["d_r"], 1280, 16)]
                    with ExitStack() as stg:
                        stage = [sbt(stg, L + "stgb%d" % k, [128, 1296], F32) for k in range(4)]
                        g1B = sbt(stg, L + "g1B", [128, 1024], F32)
                        bd = sbt(stg, L + "bd", [128, 2, 2, 128], F32)
                        for kc in range(8):
                            slot = kc % 4
                            Bs = B("stgb%d" % slot)
                            kb.dma(stage[slot][:, 0:512], w_in[l, kc * 128:(kc + 1) * 128, 1024:1536], (), [Bs])
                            kb.dma(stage[slot][:, 512:1296], w_in[l, kc * 128:(kc + 1) * 128, 2560:3344], (), [Bs])
                            CP("dve", W2[:, kc, 0:512], stage[slot][:, 0:512], [Bs], [BW2])
                            CP("pool", W2[:, kc, 512:896], stage[slot][:, 512:896], [Bs], [BW2])
                            CP("act", W2[:, kc, 896:1296], stage[slot][:, 896:1296], [Bs], [BW2])
                        kb.dma(g1B[:], mod_d[l:l + 1, :].partition_broadcast(128), [B("modd")], [B("g1B")])
                        TS("dve", g1B[:], g1B[:], 1.0, None, ALU.add, None, [B("g1B")], [B("g1B")])
                        for kc in range(8):
                            slot = kc % 4
                            Bs = B("stgb%d" % slot)
                            kb.dma(stage[slot][:, 0:1024], w_out[l, kc * 128:(kc + 1) * 128, :], (), [Bs])
                            TT("dve" if kc % 2 == 0 else "pool", Wo[:, kc, :], stage[slot][:, 0:1024], g1B[:], ALU.mult,
                               [Bs, B("g1B")], [BWo])
                        kb.dma(lnG[:], lng_in[l:l + 1, :].partition_broadcast(128), (), [B("lnG")])
                        kb.dma(lnBt[:], lnb_in[l:l + 1, :].partition_broadcast(128), (), [B("lnB")])
                        kb.dma(cw[:], convw_in[l], (), [B("cw")])
                        kb.dma(cbv[:], convb_in[l], (), [B("cbv")])
                        kb.dma(bav[:], lruba_in[l], (), [B("bav")])
                        kb.dma(bxv[:], lrubx_in[l], (), [B("bxv")])
                        kb.dma(lamv[:], lrulam_in[l], (), [B("lamv")])
                        MEMSET("pool", wr_f[:], 0.0, [B("wr_f")])
                        kb.dma(wr_f[0:16, :], glawr_in[l], (), [B("wr_f")])
                        kb.dma(wr_f[32:33, :], glabr_in[l:l + 1, :], (), [B("wr_f")])
                        kb.dma(br_f[:], glabr_in[l:l + 1, :], (), [B("br_f")])
                        kb.dma(gnB[:], glagn_in[l:l + 1, :].partition_broadcast(128), (), [B("gnB")])
                        ACTF(n8sp[:], lamv[:], AF.Exp, [B("lamv")], [Bpr], scale=-1.0)
                        ACTF(n8sp[:], n8sp[:], AF.Ln, [Bpr], [Bpr], bias=1.0)
                        TS("dve", n8sp[:], n8sp[:], -8.0, None, ALU.mult, None, [Bpr], [Bpr])
                        TS("dve", nbav[:], bav[:], -1.0, None, ALU.mult, None, [B("bav")], [Bpr])
                        TS("dve", nbxv[:], bxv[:], -1.0, None, ALU.mult, None, [B("bxv")], [Bpr])
                        MEMSET("pool", bd[:], 0.0, [B("bd")])
                        for wi, src in enumerate((lruwa_in, lruwx_in)):
                            for ct in range(2):
                                for hb in range(2):
                                    kb.dma(bd[64 * hb:64 * hb + 64, wi, ct, 64 * hb:64 * hb + 64], src[l, 2 * ct + hb],
                                           (), [B("bd")])
                        CP("dve", WaBD[:], bd[:, 0, :, :], [B("bd")], [Bpr])
                        CP("dve", WxBD[:], bd[:, 1, :, :], [B("bd")], [Bpr])
                        for e in ("dve", "pool", "act", "sp"):
                            kb.wait_all(e, [B("stgb0"), B("stgb1"), B("stgb2"), B("stgb3"), B("g1B"), B("bd")])
                    Bpar = [Bpr, B("cw"), B("cbv"), B("bav"), B("bxv"), B("wr_f"), B("br_f"),
                            B("gnB")]

                    ckpt("s2")
                    xt2 = [sbt(p2, L + "xt%d" % k, [128, 2, 1024], F32) for k in range(3)]
                    LNS = []
                    for k_ in range(2):
                        LNS.append((sbt(p2, L + "lst%d" % k_, [128, 12], F32), sbt(p2, L + "lmv%d" % k_, [128, 2], F32),
                                    sbt(p2, L + "lsd%d" % k_, [128, 1], F32), sbt(p2, L + "lrs%d" % k_, [128, 1], F32),
                                    sbt(p2, L + "lnm%d" % k_, [128, 1], F32), B(L + "lnst2_%d" % k_)))
                    xn = [sbt(p2, L + "xn%d" % k, [128, 1024], F32) for k in range(2)]
                    st_ = sbt(p2, L + "st2", [128, 12], F32)
                    mv_ = sbt(p2, L + "mv2", [128, 2], F32)
                    sd_ = sbt(p2, L + "sd2", [128, 1], F32)
                    rstd_ = sbt(p2, L + "rstd2", [128, 1], F32)
                    nmr_ = sbt(p2, L + "nmr2", [128, 1], F32)
                    uT = [sbt(p2, L + "uTb%d" % k, [128, 8, 256], BF16) for k in range(2)]
                    mix2 = [sbt(p2, L + "mix2_%d" % k, [128, 4, 256], BF16) for k in range(2)]
                    XH = sbt(p2, L + "XH", [128, 2, 259], F32)
                    xc = sbt(p2, L + "xc", [128, 2, 256], F32)
                    xcb = sbt(p2, L + "xcb", [128, 2, 256], BF16)
                    rg = sbt(p2, L + "rg", [128, 2, 256], F32)
                    ig = sbt(p2, L + "ig", [128, 2, 256], F32)
                    av = sbt(p2, L + "av", [128, 2, 256], F32)
                    uv = sbt(p2, L + "uv", [128, 2, 256], F32)
                    hv = sbt(p2, L + "hv", [128, 2, 256], F32)
                    hcar = sbt(p2, L + "hcar", [128, 2], F32)
                    sgb = sbt(p2, L + "sgb", [128, 2, 256], F32)
                    drf = sbt(p2, L + "drf", [128, 256], F32)
                    MEMSET("pool", drf[:], 0.0, [B(L + "drf")])
                    MEMSET("pool", drf[32:33, :], 1.0, [B(L + "drf")])
                    e1 = [sbt(p2, L + "e1%d" % k_, [128, 128], F32) for k_ in range(2)]
                    lsp = [sbt(p2, L + "lsp%d" % k_, [128, 128], F32) for k_ in range(2)]
                    eq = [sbt(p2, L + "eq%d" % k_, [128, 128], F32) for k_ in range(2)]
                    ek = [sbt(p2, L + "ek%d" % k_, [128, 128], F32) for k_ in range(2)]
                    eb = [sbt(p2, L + "eb%d" % k_, [128, 128], F32) for k_ in range(2)]
                    ekl = [sbt(p2, L + "ekl%d" % k_, [128, 128], F32) for k_ in range(2)]
                    dcol = [sbt(p2, L + "dcol%d" % k_, [128, 1], F32) for k_ in range(2)]
                    qs = [sbt(p2, L + "qs%d" % k_, [128, 128], BF16) for k_ in range(2)]
                    ks = [sbt(p2, L + "ks%d" % k_, [128, 128], BF16) for k_ in range(2)]
                    kh = [sbt(p2, L + "kh%d" % k_, [128, 128], BF16) for k_ in range(2)]
                    vb = [sbt(p2, L + "vb%d" % k_, [128, 256], BF16) for k_ in range(2)]
                    QB = [sbt(p2, L + "QB%d" % k_, [128, 4, 128], BF16) for k_ in range(2)]
                    kst = [sbt(p2, L + "kst%d" % k_, [128, 128], BF16) for k_ in range(2)]
                    ATm = [sbt(p2, L + "ATm%d" % k_, [128, 4, 128], BF16) for k_ in range(2)]
                    Sw = sbt(p2, L + "Sw", [128, 256], F32)
                    Swb = sbt(p2, L + "Swb", [128, 256], BF16)
                    osb = [sbt(p2, L + "osb%d" % k_, [128, 4, 64], F32) for k_ in range(2)]
                    osq = [sbt(p2, L + "osq%d" % k_, [128, 4, 64], F32) for k_ in range(2)]
                    ss = [sbt(p2, L + "ss%d" % k_, [128, 4], F32) for k_ in range(2)]
                    rr = [sbt(p2, L + "rr%d" % k_, [128, 4], F32) for k_ in range(2)]
                    sgd = [sbt(p2, L + "sgd%d" % k_, [128, 4, 64], F32) for k_ in range(2)]
                    mixD = [sbt(p2, L + "mixD%d" % k_, [128, 256], BF16) for k_ in range(2)]
                    Bst = B(L + "lnst2")
                    MEMSET("pool", XH[:], 0.0, [B(L + "XH")])
                    MEMSET("pool", Sw[:], 0.0, [B(L + "Sw")])
                    MEMSET("pool", Swb[:], 0.0, [B(L + "Swb")])
                    MEMSET("pool", hcar[:], 0.0, [B(L + "hcar")])

                    def ch_front(i):
                        t0 = 256 * i
                        sl = i % 2
                        xtt, Bx = xt2[i % 3], B("xt%d" % (i % 3))
                        uTt, BuT = uT[sl], B(L + "uTb%d" % sl)
                        m2, Bm2 = mix2[sl], B(L + "mix2_%d" % sl)
                        st_, mv_, sd_, rstd_, nmr_, Bst = LNS[0]
                        rd = [B("x1t%d" % i)] if src_is_scr else []
                        for s in range(2):
                            r0 = t0 + 128 * s
                            kb.dma(xtt[:, s, :], xsrc[r0:r0 + 128, :], rd, [Bx])
                            yield
                        kb.dma(uTt[:].rearrange("p k t -> p (k t)"), ut_d[i], [B("utd%d" % i)], [BuT])
                        yield

                    def ch_lru(i):
                        t0 = 256 * i
                        sl = i % 2
                        xtt, Bx = xt2[i % 3], B("xt%d" % (i % 3))
                        uTt, BuT = uT[sl], B(L + "uTb%d" % sl)
                        m2, Bm2 = mix2[sl], B(L + "mix2_%d" % sl)
                        BXH, Bxc, Bg_ = B(L + "XH"), B(L + "xc"), B(L + "lrug")
                        for gi in range(4):
                            bank, Bb = gp("lru")
                            for kc in range(8):
                                MM(bank[:, 0:256], W2[:, kc, gi * 128:(gi + 1) * 128], uTt[:, kc, :], kc == 0, kc == 7,
                                   [BW2, BuT], [Bb])
                                yield
                            if gi < 2:
                                CP("act", XH[:, gi, 3:259], bank[:, 0:256], [Bb], [BXH])
                                yield
                            else:
                                SIGM(sgb[:, gi - 2, :], bank[:, 0:256], [Bb], B(L + "sgb"))
                                TT("dve", sgb[:, gi - 2, :], bank[:, 0:256], sgb[:, gi - 2, :], ALU.mult, [Bb, B(L + "sgb")], [B(L + "sgb")])
                                yield
                        for ct in range(2):
                            TS("dve", xc[:, ct, :], XH[:, ct, 0:256], cw[:, 4 * ct:4 * ct + 1], cbv[:, ct:ct + 1],
                               ALU.mult, ALU.add, [BXH] + Bpar, [Bxc])
                            yield
                            for k in range(1, 4):
                                STT(xc[:, ct, :], XH[:, ct, k:k + 256], cw[:, 4 * ct + k:4 * ct + k + 1], xc[:, ct, :],
                                    ALU.mult, ALU.add, [BXH, Bxc] + Bpar, [Bxc])
                                yield
                        CP("dve", XH[:, :, 0:3], XH[:, :, 256:259], [BXH, Bxc], [BXH])
                        yield
                        CP("act", xcb[:], xc[:], [Bxc], [B(L + "xcb")])
                        yield
                        for ct in range(2):
                            for wi, (Wt, bvec, dst) in enumerate(((WaBD, nbav, rg), (WxBD, nbxv, ig))):
                                bank, Bb = gp("lru")
                                MM(bank[:, 0:256], Wt[:, ct, :], xcb[:, ct, :], True, True, [Bpr, B(L + "xcb")], [Bb])
                                yield
                                SIGM(dst[:, ct, :], bank[:, 0:256], [Bb] + Bpar, Bg_, nbias=bvec[:, ct:ct + 1])
                                yield
                        for ct in range(2):
                            ACTF(av[:, ct, :], rg[:, ct, :], AF.Exp, [Bg_, Bpr], [B(L + "av")], scale=n8sp[:, ct:ct + 1])
                            yield
                        TT("dve", uv[:], av[:], av[:], ALU.mult, [B(L + "av")], [B(L + "uv")])
                        yield
                        TS("dve", uv[:], uv[:], -1.0, 1.0, ALU.mult, ALU.add, [B(L + "uv")], [B(L + "uv")])
                        yield
                        TS("dve", uv[:], uv[:], 1e-30, None, ALU.max, None, [B(L + "uv")], [B(L + "uv")])
                        yield
                        ACTF(uv[:], uv[:], AF.Ln, [B(L + "uv")], [B(L + "uv")])
                        ACTF(uv[:], uv[:], AF.Exp, [B(L + "uv")], [B(L + "uv")], scale=0.5)
                        yield
                        TT("dve", ig[:], ig[:], xc[:], ALU.mult, [Bg_, Bxc], [Bg_])
                        yield
                        TT("dve", uv[:], uv[:], ig[:], ALU.mult, [B(L + "uv"), Bg_], [B(L + "uv")])
                        yield
                        for ct in range(2):
                            kb.op("dve", lambda g: g.tensor_tensor_scan(out=hv[:, ct, :], data0=av[:, ct, :], data1=uv[:, ct, :],
                                                                        initial=hcar[:, ct:ct + 1], op0=ALU.mult, op1=ALU.add),
                                  [B(L + "av"), B(L + "uv"), B(L + "hcar")], [B(L + "hv")])
                            yield
                        CP("dve", hcar[:], hv[:, :, 255], [B(L + "hv")], [B(L + "hcar")])
                        yield
                        TT("dve", m2[:, 0:2, :], hv[:], sgb[:], ALU.mult, [B(L + "hv"), B(L + "sgb")], [Bm2])
                        yield

                    def ch_gla(i, s):
                        t0 = 256 * i
                        sl = i % 2
                        xtt, Bx = xt2[i % 3], B("xt%d" % (i % 3))
                        uTt, BuT = uT[sl], B(L + "uTb%d" % sl)
                        m2, Bm2 = mix2[sl], B(L + "mix2_%d" % sl)
                        if s == 0:
                            bank, Bb = gp("gla%d" % s)
                            for kc in range(8):
                                MM(bank[:, 0:256], W2[:, kc, 1280:1408], uTt[:, kc, :], kc == 0, kc == 7, [BW2, BuT], [Bb])
                                yield
                            CP("act", drf[0:16, :], bank[0:16, 0:256], [Bb], [B(L + "drf")])
                            yield
                            gflags[("drf", i)] = True
                        else:
                            while not gflags.get(("drf", i)):
                                yield
                        Bgl = B(L + "gla")
                        lin, Bl = gp("gla%d" % s)
                        MM(lin[:, 0:128], drf[:, s * 128:(s + 1) * 128], wr_f[:, :], True, True,
                           [B(L + "drf")] + Bpar, [Bl])
                        yield
                        ACTF(e1[s][:], lin[:, 0:128], AF.Exp, [Bl], [B(L + "e1%d" % s)], scale=-1.0)
                        yield
                        ACTF(lsp[s][:], e1[s][:], AF.Ln, [B(L + "e1%d" % s)], [B(L + "lsp%d" % s)], bias=1.0)
                        yield
                        bc_, Bbc = gp("gla%d" % s)
                        MM(bc_[:, 0:128], tri16[:], lsp[s][:], True, True, [Bc, B(L + "lsp%d" % s)], [Bbc])
                        yield
                        MM(bc_[:, 128:256], ones16[:], lsp[s][:], True, True, [Bc, B(L + "lsp%d" % s)], [Bbc])
                        yield
                        MM(bc_[:, 256:257], lsp[s][:], ones16[:, 0:1], True, True, [Bc, B(L + "lsp%d" % s)], [Bbc])
                        yield
                        ACTF(eq[s][:], bc_[:, 0:128], AF.Exp, [Bbc], [B(L + "eq%d" % s)])
                        yield
                        ACTF(ek[s][:], bc_[:, 0:128], AF.Exp, [Bbc], [B(L + "ek%d" % s)], scale=-1.0)
                        yield
                        ACTF(eb[s][:], bc_[:, 128:256], AF.Exp, [Bbc], [B(L + "eb%d" % s)])
                        yield
                        ACTF(dcol[s][:], bc_[:, 256:257], AF.Exp, [Bbc], [B(L + "dcol%d" % s)])
                        yield
                        TT("dve", ekl[s][:], ek[s][:], eb[s][:], ALU.mult, [B(L + "ek%d" % s), B(L + "eb%d" % s)], [B(L + "ekl%d" % s)])
                        yield
                        d1, Bd1 = gp("gla%d" % s)
                        for kc in range(8):
                            MM(d1[:, 0:512], uTt[:, kc, s * 128:(s + 1) * 128], W2[:, kc, 512:1024], kc == 0, kc == 7,
                               [BW2, BuT], [Bd1])
                            yield
                        STT(qs[s][:], d1[:, 0:128], 32.0 ** -0.5, eq[s][:], ALU.mult, ALU.mult, [Bd1, B(L + "eq%d" % s)], [B(L + "qs%d" % s)])
                        yield
                        TT("dve", ks[s][:], d1[:, 128:256], ek[s][:], ALU.mult, [Bd1, B(L + "ek%d" % s)], [B(L + "ks%d" % s)])
                        yield
                        TT("dve", kh[s][:], d1[:, 128:256], ekl[s][:], ALU.mult, [Bd1, B(L + "ekl%d" % s)], [B(L + "kh%d" % s)])
                        yield
                        CP("act", vb[s][:], d1[:, 256:512], [Bd1], [B(L + "vb%d" % s)])
                        yield
                        pb, Bpb = PB
                        po = 512 * s
                        TR(pb[:, po:po + 128], qs[s][:], ident_b[:], [B(L + "qs%d" % s), Bc], [Bpb])
                        yield
                        TR(pb[:, po + 128:po + 256], ks[s][:], ident_b[:], [B(L + "ks%d" % s), Bc], [Bpb])
                        yield
                        TT("dve", QB[s][:], pb[:, po:po + 128].unsqueeze(1).broadcast_to([128, 4, 128]), hm[:], ALU.mult,
                           [Bpb, Bc], [B(L + "QB%d" % s)])
                        yield
                        CP("act", kst[s][:], pb[:, po + 128:po + 256], [Bpb], [B(L + "kst%d" % s)])
                        yield
                        at, Bat = gp("gla%d" % s)
                        MM(at[:, 0:512], kst[s][:], QB[s][:].rearrange("p h i -> p (h i)"), True, True,
                           [B(L + "kst%d" % s), B(L + "QB%d" % s)], [Bat])
                        yield
                        TT("dve", ATm[s][:], at[:, 0:512].rearrange("p (h i) -> p h i", h=4),
                           tri[:].unsqueeze(1).broadcast_to([128, 4, 128]), ALU.mult, [Bat, Bc], [B(L + "ATm%d" % s)])
                        yield
                        if s == 1:
                            while not gflags.get(("swb", i, 0)):
                                yield
                        ob, Bob = gp("gla%d" % s)
                        for h in range(4):
                            MM(ob[:, 64 * h:64 * h + 64], ATm[s][:, h, :], vb[s][:, 64 * h:64 * h + 64], True, False,
                               [B(L + "ATm%d" % s), B(L + "vb%d" % s)], [Bob])
                            yield
                            MM(ob[:, 64 * h:64 * h + 64], QB[s][:, h, :], Swb[:, 64 * h:64 * h + 64], False, True,
                               [B(L + "QB%d" % s), B(L + "Swb")], [Bob])
                            yield
                        CP("act", osb[s][:].rearrange("p h d -> p (h d)"), ob[:, 0:256], [Bob], [B(L + "osb%d" % s)])
                        yield
                        spb, Bsp = gp("gla%d" % s)
                        MM(spb[:, 0:256], kh[s][:], vb[s][:], True, True, [B(L + "kh%d" % s), B(L + "vb%d" % s)], [Bsp])
                        yield
                        STT(Sw[:], Sw[:], dcol[s][:, 0:1], spb[:, 0:256], ALU.mult, ALU.add, [B(L + "Sw"), B(L + "dcol%d" % s), Bsp],
                            [B(L + "Sw")])
                        yield
                        CP("act", Swb[:], Sw[:], [B(L + "Sw")], [B(L + "Swb")])
                        gflags[("swb", i, s)] = True
                        yield
                        d2, Bd2 = gp("gla%d" % s)
                        for kc in range(8):
                            MM(d2[:, 0:256], uTt[:, kc, s * 128:(s + 1) * 128], W2[:, kc, 1024:1280], kc == 0, kc == 7,
                               [BW2, BuT], [Bd2])
                            yield
                        TT("dve", osq[s][:], osb[s][:], osb[s][:], ALU.mult, [B(L + "osb%d" % s)], [B(L + "osq%d" % s)])
                        yield
                        kb.op("dve", lambda g: g.tensor_reduce(out=ss[s][:, 0:4], in_=osq[s][:], axis=AX.X, op=ALU.add),
                              [B(L + "osq%d" % s)], [B(L + "ss%d" % s)])
                        yield
                        TS("dve", ss[s][:], ss[s][:], 1.0 / 64.0, LN_EPS, ALU.mult, ALU.add, [B(L + "ss%d" % s)], [B(L + "ss%d" % s)])
                        yield
                        ACTF(ss[s][:], ss[s][:], AF.Ln, [B(L + "ss%d" % s)], [B(L + "ss%d" % s)])
                        yield
                        ACTF(rr[s][:], ss[s][:], AF.Exp, [B(L + "ss%d" % s)], [B(L + "rr%d" % s)], scale=-0.5)
                        yield
                        SIGM(sgd[s][:].rearrange("p h d -> p (h d)"), d2[:, 0:256], [Bd2], B(L + "sgd%d" % s))
                        TT("dve", sgd[s][:].rearrange("p h d -> p (h d)"), d2[:, 0:256], sgd[s][:].rearrange("p h d -> p (h d)"), ALU.mult,
                           [Bd2, B(L + "sgd%d" % s)], [B(L + "sgd%d" % s)])
                        yield
                        TT("dve", sgd[s][:], sgd[s][:], gnB[:].unsqueeze(1).broadcast_to([128, 4, 64]), ALU.mult,
                           [B(L + "sgd%d" % s)] + Bpar, [B(L + "sgd%d" % s)])
                        yield
                        TT("dve", osb[s][:], osb[s][:], rr[s][:].unsqueeze(2).broadcast_to([128, 4, 64]), ALU.mult,
                           [B(L + "osb%d" % s), B(L + "rr%d" % s)], [B(L + "osb%d" % s)])
                        yield
                        TT("dve", mixD[s][:].rearrange("p (h d) -> p h d", h=4), osb[s][:], sgd[s][:], ALU.mult,
                           [B(L + "osb%d" % s), B(L + "sgd%d" % s)], [B(L + "mixD%d" % s)])
                        yield
                        TR(pb[:, po + 256:po + 384], mixD[s][:, 0:128], ident_b[:], [B(L + "mixD%d" % s), Bc], [Bpb])
                        yield
                        TR(pb[:, po + 384:po + 512], mixD[s][:, 128:256], ident_b[:], [B(L + "mixD%d" % s), Bc], [Bpb])
                        yield
                        CP("act", m2[:, 2:4, s * 128:(s + 1) * 128], pb[:, po + 256:po + 512].rearrange("p (k t) -> p k t", k=2),
                           [Bpb], [Bm2])
                        yield

                    def ch_out(i):
                        t0 = 256 * i
                        sl = i % 2
                        xtt, Bx = xt2[i % 3], B("xt%d" % (i % 3))
                        uTt, BuT = uT[sl], B(L + "uTb%d" % sl)
                        m2, Bm2 = mix2[sl], B(L + "mix2_%d" % sl)
                        st_, mv_, sd_, rstd_, nmr_, Bst = LNS[1]
                        for s in range(2):
                            for half in range(2):
                                yb, Byb = gp("out")
                                for f in range(8):
                                    grp = f // 2
                                    if grp in (0, 2):
                                        slot4 = (0 if grp == 0 else 2) + (f % 2)
                                        lhs = mixAC[:, slot4, t0 + s * 128:t0 + (s + 1) * 128]
                                        Rd = [B(L + "mix%d_%d" % (slot4, i))]
                                    else:
                                        slot4 = (0 if grp == 1 else 2) + (f % 2)
                                        lhs = m2[:, slot4, s * 128:(s + 1) * 128]
                                        Rd = [Bm2]
                                    MM(yb[:, 0:512], lhs, Wo[:, f, half * 512:(half + 1) * 512], f == 0, f == 7, Rd + [BWo], [Byb])
                                    yield
                                STT(xtt[:, s, half * 512:(half + 1) * 512], xtt[:, s, half * 512:(half + 1) * 512], ALPHA,
                                    yb[:, 0:512], ALU.mult, ALU.add, [Bx, Byb], [Bx])
                                yield
                            ln_stats(xtt[:, s, :], [Bx], st_, mv_, sd_, rstd_, nmr_, Bst)
                            yield
                            ACTF(xtt[:, s, :], xtt[:, s, :], AF.Identity, [Bx, Bst], [Bx], bias=nmr_[:, 0:1], scale=rstd_[:, 0:1])
                            yield
                            TT("dve", xtt[:, s, :], xtt[:, s, :], lnG[:], ALU.mult, [Bx, B("lnG")], [Bx])
                            yield
                            TT("dve", xtt[:, s, :], xtt[:, s, :], lnBt[:], ALU.add, [Bx, B("lnB")], [Bx])
                            yield
                        Bdst = B("x1t%d" % i) if l < DEPTH - 1 else B("outt%d" % i)
                        for s in range(2):
                            r0 = t0 + 128 * s
                            kb.dma(xdst[r0:r0 + 128, :], xtt[:, s, :], [Bx], [Bdst], sembuf=Bx)
                            yield


                    def run_chains(gens):
                        gens = list(gens)
                        while gens:
                            for g_ in list(gens):
                                try:
                                    next(g_)
                                except StopIteration:
                                    gens.remove(g_)

                    gflags = {}
                    run_chains([ch_front(0)])
                    for i in range(NT):
                        gens = [ch_gla(i, 0), ch_gla(i, 1), ch_lru(i)]
                        if i + 1 < NT:
                            gens.append(ch_front(i + 1))
                        if i >= 1:
                            gens.append(ch_out(i - 1))
                        run_chains(gens)
                    run_chains([ch_out(NT - 1)])
                    for e in ("pe", "dve", "act", "pool", "sp"):
                        kb.wait_all(e, list(kb.bufs.values()))
        for e in ("sp", "act"):
            kb.wait_all(e, [b for n_, b in kb.bufs.items() if n_.startswith("outt")])
        build.stats = dict(nins=kb.nins, nwait=kb.nwait, nsem=kb.nsem)
    return nc


_NC_CACHE = {}


def _layout_inputs(inp, b, SEQ):
    f = lambda a: np.ascontiguousarray(a, dtype=np.float32)
    dep = inp["w_mod"].shape[0]
    half = 32
    inv = (np.float32(10000.0) ** (-(np.arange(128) % half).astype(np.float32) / np.float32(half))).astype(np.float32)
    d = {
        "x": f(inp["x"][b]),
        "cT": f(inp["c"][b].reshape(8, 128).T),
        "pos": np.ascontiguousarray(inp["positions"][b][None, :].astype(np.int32)),
        "rope_inv": f(inv[:, None]),
        "w_mod": f(inp["w_mod"]), "b_mod": f(inp["b_mod"]), "w_in": f(inp["w_in"]),
        "conv_w": f(inp["conv_w"].reshape(dep, 4, 2, 128).transpose(0, 3, 2, 1).reshape(dep, 128, 8)),
        "conv_b": f(inp["conv_b"].reshape(dep, 2, 128).transpose(0, 2, 1)),
        "lru_wa": f(inp["lru_wa"]), "lru_ba": f(inp["lru_ba"].reshape(dep, 2, 128).transpose(0, 2, 1)),
        "lru_wx": f(inp["lru_wx"]), "lru_bx": f(inp["lru_bx"].reshape(dep, 2, 128).transpose(0, 2, 1)),
        "lru_lam": f(inp["lru_lam"].reshape(dep, 2, 128).transpose(0, 2, 1)),
        "gla_wr": f(inp["gla_wr"]), "gla_br": f(inp["gla_br"]), "gla_gn": f(inp["gla_gn"]),
        "w_out": f(inp["w_out"]), "ln_g": f(inp["ln_g"]), "ln_b": f(inp["ln_b"]),
    }
    return d


def kernel(**inputs):
    inp = {k: np.asarray(v) for k, v in inputs.items()}
    Bn, SEQ, _ = inp["x"].shape
    DEPTH = inp["w_mod"].shape[0]
    key = (SEQ, DEPTH)
    if key not in _NC_CACHE:
        _NC_CACHE[key] = build(SEQ, DEPTH)
    nc = _NC_CACHE[key]
    in_maps = [_layout_inputs(inp, b, SEQ) for b in range(Bn)]
    res = run_bass_kernel_spmd(nc, in_maps, core_ids=list(range(Bn)))
    return np.stack([np.asarray(r["out"], dtype=np.float32) for r in res.results], axis=0)
```
